# Optimizing a Trainium2 kernel written in Bass

```python
import math
import jax, jax.numpy as jnp
from jax import lax
import numpy as np

D_MODEL = 2048
BATCH = 4
SEQ = 4096
DEPTH = 1

HEAD_DIM = 128
ATTN_HEADS = 8
ATTN_KV_HEADS = 2
ATTN_GROUP = ATTN_HEADS // ATTN_KV_HEADS
WINDOW = 128
ATTN_BLOCK = 128
N_BUCKETS = 32
MAX_DISTANCE = 128
DN_HEADS = 8
DN_KEY_DIM = 128
DN_VAL_DIM = 128
CONV_WIDTH = 5
DN_CHUNK = 64
MIX_WIDTH = ATTN_HEADS * HEAD_DIM + DN_HEADS * DN_VAL_DIM
PEER_HEADS = 8
PEER_NKEYS = 128
PEER_EXPERTS = PEER_NKEYS * PEER_NKEYS
PEER_QDIM = 256
PEER_HALF = PEER_QDIM // 2
PEER_TOPK = 16
PEER_BLOCK = 128
DEEPNORM_ALPHA = (2.0 * DEPTH) ** 0.25
DEEPNORM_BETA = (8.0 * DEPTH) ** -0.25
LN_EPS = 1e-5
RMS_EPS = 1e-6

SPLIT_SIZES = (ATTN_HEADS * HEAD_DIM, ATTN_KV_HEADS * HEAD_DIM, ATTN_KV_HEADS * HEAD_DIM,
               DN_HEADS * DN_KEY_DIM, DN_HEADS * DN_KEY_DIM, DN_HEADS * DN_VAL_DIM, DN_HEADS * DN_VAL_DIM,
               2 * DN_HEADS, 2 * DN_HEADS)
IN_WIDTH = int(sum(SPLIT_SIZES))
SPLIT_POINTS = tuple(int(c) for c in np.cumsum(SPLIT_SIZES)[:-1])
CONV_CH = 2 * DN_HEADS * DN_KEY_DIM + DN_HEADS * DN_VAL_DIM

kernel_name = "hymba_swa_gdn_peer_deepnorm"


def _layer_norm(x, g, b):
    xf = x.astype(jnp.float32)
    mu = jnp.mean(xf, axis=-1, keepdims=True)
    var = jnp.mean(jnp.square(xf - mu), axis=-1, keepdims=True)
    return ((xf - mu) * lax.rsqrt(var + LN_EPS) * g.astype(jnp.float32) + b.astype(jnp.float32)).astype(x.dtype)


def _l2norm(t):
    return t * lax.rsqrt(jnp.sum(t * t, axis=-1, keepdims=True) + RMS_EPS)


def _t5_bucket(rel):
    nb = N_BUCKETS // 2
    max_exact = nb // 2
    ret = jnp.where(rel > 0, nb, 0)
    n = jnp.abs(rel)
    large = max_exact + (jnp.log(jnp.maximum(n, 1).astype(jnp.float32) / max_exact)
                         / math.log(MAX_DISTANCE / max_exact) * (nb - max_exact)).astype(jnp.int32)
    large = jnp.minimum(large, nb - 1)
    return ret + jnp.where(n < max_exact, n, large)


def _window_attention(q, k, v, sink, rel_bias):
    B, S = q.shape[0], q.shape[1]
    nb = S // ATTN_BLOCK
    qb = q.reshape(B, nb, ATTN_BLOCK, ATTN_KV_HEADS, ATTN_GROUP, HEAD_DIM)

    def band(t):
        tp = jnp.pad(t, ((0, 0), (ATTN_BLOCK, ATTN_BLOCK), (0, 0), (0, 0)))
        parts = [tp[:, o * ATTN_BLOCK: o * ATTN_BLOCK + S].reshape(B, nb, ATTN_BLOCK, ATTN_KV_HEADS, HEAD_DIM)
                 for o in range(3)]
        return jnp.concatenate(parts, axis=2)

    kb, vb = band(k), band(v)
    rel = jnp.arange(3 * ATTN_BLOCK)[None, :] - jnp.arange(ATTN_BLOCK)[:, None] - ATTN_BLOCK
    bias = rel_bias[_t5_bucket(rel)]
    bias = jnp.transpose(bias, (2, 0, 1)).reshape(ATTN_KV_HEADS, ATTN_GROUP, ATTN_BLOCK, 3 * ATTN_BLOCK)
    kpos = (jnp.arange(nb)[:, None] - 1) * ATTN_BLOCK + jnp.arange(3 * ATTN_BLOCK)[None, :]
    mask = (jnp.abs(rel) <= WINDOW)[None] & ((kpos >= 0) & (kpos < S))[:, None, :]
    s = (jnp.einsum('bnqhgd,bnkhd->bnhgqk', qb, kb).astype(jnp.float32) * (HEAD_DIM ** -0.5)
         + bias.astype(jnp.float32))
    s = jnp.where(mask[None, :, None, None], s, -jnp.inf)
    sk = sink.astype(jnp.float32).reshape(1, 1, ATTN_KV_HEADS, ATTN_GROUP, 1, 1)
    m = jnp.maximum(jnp.max(s, axis=-1, keepdims=True), sk)
    p = jnp.exp(s - m)
    den = jnp.sum(p, axis=-1, keepdims=True) + jnp.exp(sk - m)
    o = jnp.einsum('bnhgqk,bnkhd->bnqhgd', (p / den).astype(v.dtype), vb)
    return o.reshape(B, S, ATTN_HEADS * HEAD_DIM)


def _short_conv(x, w):
    pad = CONV_WIDTH // 2
    y = lax.conv_general_dilated(x, w[:, None, :].astype(x.dtype), window_strides=(1,),
                                 padding=((pad, pad),), dimension_numbers=('NWC', 'WIO', 'NWC'),
                                 feature_group_count=x.shape[-1])
    return jax.nn.silu(y)


def _chunk_gated_delta(q, k, v, g, beta):
    B, H, S, Dk = q.shape
    Dv = v.shape[-1]
    C = DN_CHUNK
    N = S // C
    chunks = lambda t: t.reshape((B, H, N, C) + t.shape[3:])
    q = chunks(q * (Dk ** -0.5))
    k = chunks(k)
    v = chunks(v)
    beta = chunks(beta)
    g = jnp.cumsum(chunks(g), axis=-1)
    incl = jnp.tril(jnp.ones((C, C), bool))
    strict = jnp.tril(jnp.ones((C, C), bool), -1)
    diff = g[..., :, None] - g[..., None, :]
    decay = jnp.where(incl, jnp.exp(jnp.where(incl, diff, 0.0)), 0.0)
    kb = k * beta[..., None]
    L = jnp.where(strict, jnp.einsum('bhnid,bhnjd->bhnij', kb, k) * decay, 0.0)
    eye = jnp.eye(C, dtype=L.dtype)
    T = lax.linalg.triangular_solve(L + eye, jnp.broadcast_to(eye, L.shape), left_side=True,
                                    lower=True, unit_diagonal=True)
    u = T @ (v * beta[..., None])
    w = T @ (kb * jnp.exp(g)[..., None])
    A = jnp.einsum('bhnid,bhnjd->bhnij', q, k) * decay
    qg = q * jnp.exp(g)[..., None]
    kd = k * jnp.exp(g[..., -1:] - g)[..., None]
    gl = jnp.exp(g[..., -1])

    def step(state, inp):
        qg_i, kd_i, u_i, w_i, A_i, gl_i = inp
        v_new = u_i - w_i @ state
        o = qg_i @ state + A_i @ v_new
        state = state * gl_i[..., None, None] + jnp.einsum('bhck,bhcv->bhkv', kd_i, v_new)
        return state, o

    xs = tuple(jnp.moveaxis(t, 2, 0) for t in (qg, kd, u, w, A, gl))
    _, o = lax.scan(step, jnp.zeros((B, H, Dk, Dv), q.dtype), xs)
    return jnp.moveaxis(o, 0, 2).reshape(B, H, S, Dv)


def _deltanet(dq, dk, dv, dz, dbeta, da, conv_w, a_log, dt_bias, norm_w):
    B, S = dq.shape[0], dq.shape[1]
    f32 = jnp.float32
    qkv = _short_conv(jnp.concatenate([dq, dk, dv], axis=-1), conv_w)
    q, k, v = jnp.split(qkv, [DN_HEADS * DN_KEY_DIM, 2 * DN_HEADS * DN_KEY_DIM], axis=-1)
    heads = lambda t, d: t.reshape(B, S, DN_HEADS, d).transpose(0, 2, 1, 3).astype(f32)
    q = _l2norm(heads(q, DN_KEY_DIM))
    k = _l2norm(heads(k, DN_KEY_DIM))
    v = heads(v, DN_VAL_DIM)
    beta = jax.nn.sigmoid(dbeta.astype(f32)).reshape(B, S, 2, DN_HEADS).transpose(2, 0, 3, 1)
    a = da.astype(f32).reshape(B, S, 2, DN_HEADS).transpose(2, 0, 3, 1)
    g = -jnp.exp(a_log.astype(f32))[:, None, :, None] * jax.nn.softplus(a + dt_bias.astype(f32)[:, None, :, None])
    o_fwd = _chunk_gated_delta(q, k, v, g[0], beta[0])
    flip = lambda t: jnp.flip(t, axis=2)
    o_bwd = flip(_chunk_gated_delta(flip(q), flip(k), flip(v), flip(g[1]), flip(beta[1])))
    o = (o_fwd + o_bwd).transpose(0, 2, 1, 3)
    o = o * lax.rsqrt(jnp.mean(o * o, axis=-1, keepdims=True) + RMS_EPS) * norm_w.astype(f32)
    z = dz.reshape(B, S, DN_HEADS, DN_VAL_DIM).astype(f32)
    return (o * jax.nn.silu(z)).reshape(B, S, DN_HEADS * DN_VAL_DIM).astype(dq.dtype)


def _peer(h, wq, keys, u, v):
    B, S, D = h.shape
    T = B * S
    t = h.reshape(T, D)
    q = (t @ wq).reshape(T, PEER_HEADS, 2, PEER_HALF)
    s = jnp.einsum('thpd,pkd->thpk', q, keys).astype(jnp.float32)
    sv, si = lax.top_k(s, PEER_TOPK)
    cand = (sv[:, :, 0, :, None] + sv[:, :, 1, None, :]).reshape(T, PEER_HEADS, PEER_TOPK * PEER_TOPK)
    cidx = (si[:, :, 0, :, None] * PEER_NKEYS + si[:, :, 1, None, :]).reshape(T, PEER_HEADS, PEER_TOPK * PEER_TOPK)
    top_s, top_c = lax.top_k(cand, PEER_TOPK)
    eidx = jnp.take_along_axis(cidx, top_c, axis=-1).reshape(T, PEER_HEADS * PEER_TOPK)
    gate = jax.nn.softmax(top_s, axis=-1).reshape(T, PEER_HEADS * PEER_TOPK).astype(h.dtype)
    nblk = T // PEER_BLOCK

    def block(args):
        tb, ib, gb = args
        act = jax.nn.gelu(jnp.einsum('tpd,td->tp', u[ib], tb), approximate=False)
        return jnp.einsum('tp,tpd->td', act * gb, v[ib])

    out = lax.map(block, (t.reshape(nblk, PEER_BLOCK, D),
                          eidx.reshape(nblk, PEER_BLOCK, PEER_HEADS * PEER_TOPK),
                          gate.reshape(nblk, PEER_BLOCK, PEER_HEADS * PEER_TOPK)))
    return out.reshape(B, S, D)


def _layer(x, w_in, conv_w, a_log, dt_bias, dn_norm_w, attn_sink, rel_bias, w_out,
           ln1_g, ln1_b, peer_wq, peer_keys, peer_u, peer_v, ln2_g, ln2_b):
    B, S = x.shape[0], x.shape[1]
    proj = x @ w_in
    aq, ak, av, dq, dk, dv, dz, dbeta, da = jnp.split(proj, SPLIT_POINTS, axis=-1)
    attn = _window_attention(aq.reshape(B, S, ATTN_HEADS, HEAD_DIM),
                             ak.reshape(B, S, ATTN_KV_HEADS, HEAD_DIM),
                             av.reshape(B, S, ATTN_KV_HEADS, HEAD_DIM), attn_sink, rel_bias)
    dn = _deltanet(dq, dk, dv, dz, dbeta, da, conv_w, a_log, dt_bias, dn_norm_w)
    mix = jnp.concatenate([attn, dn], axis=-1) @ w_out
    h = _layer_norm(DEEPNORM_ALPHA * x + mix, ln1_g, ln1_b)
    return _layer_norm(DEEPNORM_ALPHA * h + _peer(h, peer_wq, peer_keys, peer_u, peer_v), ln2_g, ln2_b)


def setup_inputs(seed: int = 0) -> dict:
    key = jax.random.key(seed)
    ks = jax.random.split(key, 17)
    f32 = jnp.float32
    nrm = lambda k, shape, scale: jax.random.normal(k, shape, f32) * scale
    x = nrm(ks[0], (BATCH, SEQ, D_MODEL), 1.0)
    w_in = nrm(ks[1], (DEPTH, D_MODEL, IN_WIDTH), D_MODEL ** -0.5)
    conv_w = nrm(ks[2], (DEPTH, CONV_WIDTH, CONV_CH), CONV_WIDTH ** -0.5)
    a_log = jnp.log(jax.random.uniform(ks[3], (DEPTH, 2, DN_HEADS), f32, 1.0, 16.0))
    dt = jnp.exp(jax.random.uniform(ks[4], (DEPTH, 2, DN_HEADS), f32, math.log(1e-3), math.log(1e-1)))
    dt_bias = dt + jnp.log(-jnp.expm1(-dt))
    dn_norm_w = 1.0 + nrm(ks[5], (DEPTH, DN_VAL_DIM), 0.02)
    attn_sink = nrm(ks[6], (DEPTH, ATTN_HEADS), 0.5)
    rel_bias = nrm(ks[7], (N_BUCKETS, ATTN_HEADS), 0.5)
    w_out = nrm(ks[8], (DEPTH, MIX_WIDTH, D_MODEL), MIX_WIDTH ** -0.5 * DEEPNORM_BETA)
    ln1_g = 1.0 + nrm(ks[9], (DEPTH, D_MODEL), 0.02)
    ln1_b = nrm(ks[10], (DEPTH, D_MODEL), 0.02)
    peer_wq = nrm(ks[11], (DEPTH, D_MODEL, PEER_HEADS * PEER_QDIM), D_MODEL ** -0.5)
    peer_keys = nrm(ks[12], (DEPTH, 2, PEER_NKEYS, PEER_HALF), PEER_HALF ** -0.5)
    peer_u = nrm(ks[13], (DEPTH, PEER_EXPERTS, D_MODEL), D_MODEL ** -0.5)
    peer_v = nrm(ks[14], (DEPTH, PEER_EXPERTS, D_MODEL), DEEPNORM_BETA * PEER_HEADS ** -0.5)
    ln2_g = 1.0 + nrm(ks[15], (DEPTH, D_MODEL), 0.02)
    ln2_b = nrm(ks[16], (DEPTH, D_MODEL), 0.02)
    return {"x": x, "w_in": w_in, "conv_w": conv_w, "a_log": a_log, "dt_bias": dt_bias,
            "dn_norm_w": dn_norm_w, "attn_sink": attn_sink, "rel_bias": rel_bias, "w_out": w_out,
            "ln1_g": ln1_g, "ln1_b": ln1_b, "peer_wq": peer_wq, "peer_keys": peer_keys,
            "peer_u": peer_u, "peer_v": peer_v, "ln2_g": ln2_g, "ln2_b": ln2_b}


def reference(x, w_in, conv_w, a_log, dt_bias, dn_norm_w, attn_sink, rel_bias, w_out,
              ln1_g, ln1_b, peer_wq, peer_keys, peer_u, peer_v, ln2_g, ln2_b):
    for l in range(DEPTH):
        x = _layer(x, w_in[l], conv_w[l], a_log[l], dt_bias[l], dn_norm_w[l], attn_sink[l], rel_bias,
                   w_out[l], ln1_g[l], ln1_b[l], peer_wq[l], peer_keys[l], peer_u[l], peer_v[l],
                   ln2_g[l], ln2_b[l])
    return x
```

```python
import numpy as np
from contextlib import ExitStack
import concourse.bass as bass
import concourse.mybir as mybir
from concourse.bass_utils import run_bass_kernel_spmd

F32 = mybir.dt.float32
BF16 = mybir.dt.bfloat16
U32 = mybir.dt.uint32
I32 = mybir.dt.int32
AF = mybir.ActivationFunctionType
ALU = mybir.AluOpType
AX = mybir.AxisListType

D = 2048
SEQ = 4096
OWN = 2048
INW = 5664
NEG = -30000.0
ALPHA = 2.0 ** 0.25
LN_EPS = 1e-5
RMS_EPS = 1e-6


class Buf:
    __slots__ = ("name", "w", "rs", "dsem", "dcnt")

    def __init__(self, name):
        self.name = name
        self.w = None
        self.rs = []
        self.dsem = None
        self.dcnt = 0


class FW:
    def __init__(self, nc, es):
        self.nc = nc
        self.es = es
        self.eng = {"pe": nc.tensor, "act": nc.scalar, "dve": nc.vector, "pool": nc.gpsimd, "sp": nc.sync}
        self.sem = {k: es.enter_context(nc.semaphore("s_" + k)) for k in self.eng}
        self.cnt = {k: 0 for k in self.eng}
        self.waited = {k: {} for k in self.eng}
        self.dbufs = []
        self.nbuf = 0

    def buf(self, name=None):
        self.nbuf += 1
        return Buf(name or ("b%d" % self.nbuf))

    def _resolve(self, ev):
        if ev[0] == "e":
            return ("e_" + ev[1], self.sem[ev[1]], ev[2])
        b = ev[1]
        return ("d_" + b.name, b.dsem, b.dcnt)

    def _waits(self, e, reads, writes, skip_dma_owner=None):
        evs = []
        for b in reads:
            if b.w is not None:
                evs.append(b.w)
        for b in writes:
            if b.w is not None:
                evs.append(b.w)
            evs.extend(b.rs)
        need = {}
        for ev in evs:
            if ev[0] == "e" and ev[1] == "pe" and e == "pe":
                continue
            if ev[0] == "d" and skip_dma_owner is not None and ev[1] is skip_dma_owner:
                continue
            key, sem, val = self._resolve(ev)
            if need.get(key, (None, 0))[1] < val:
                need[key] = (sem, val)
        for key, (sem, val) in need.items():
            if self.waited[e].get(key, 0) >= val:
                continue
            self.eng[e].wait_ge(sem, val)
            self.waited[e][key] = val

    def op(self, e, fn, reads=(), writes=()):
        self._waits(e, reads, writes)
        ins = fn(self.eng[e])
        self.cnt[e] += 1
        ins.then_inc(self.sem[e], 1)
        ev = ("e", e, self.cnt[e])
        for b in reads:
            b.rs.append(ev)
        for b in writes:
            b.w = ev
            b.rs = []
        return ins

    def _dsem(self, b):
        if b.dsem is None:
            b.dsem = self.es.enter_context(self.nc.semaphore("d_" + b.name))
            self.dbufs.append(b)
        return b.dsem

    def dma(self, q, owner, reads=(), writes=(), fn=None, out=None, in_=None):
        self._dsem(owner)
        self._waits(q, reads, writes, skip_dma_owner=owner)
        if fn is None:
            ins = self.eng[q].dma_start(out=out, in_=in_)
        else:
            ins = fn(self.eng[q])
        owner.dcnt += 16
        ins.then_inc(owner.dsem, 16)
        ev = ("d", owner)
        for b in reads:
            b.rs.append(ev)
        for b in writes:
            b.w = ev
            b.rs = []
        return ins

    def barrier(self):
        for e in self.eng:
            for k in self.eng:
                if k == e or self.cnt[k] == 0:
                    continue
                key = "e_" + k
                if self.waited[e].get(key, 0) < self.cnt[k]:
                    self.eng[e].wait_ge(self.sem[k], self.cnt[k])
                    self.waited[e][key] = self.cnt[k]
            for b in self.dbufs:
                key = "d_" + b.name
                if b.dcnt and self.waited[e].get(key, 0) < b.dcnt:
                    self.eng[e].wait_ge(b.dsem, b.dcnt)
                    self.waited[e][key] = b.dcnt


def rsqrt_inplace(fw, ap, b):
    fw.op("act", lambda g: g.activation(out=ap, in_=ap, func=AF.Ln), [b], [b])
    fw.op("act", lambda g: g.activation(out=ap, in_=ap, func=AF.Exp, scale=-0.5), [b], [b])


def _alt(i, engines=("act", "dve")):
    return engines[i % len(engines)]


def evac(fw, e, out, in_, reads, writes):
    if e == "act":
        return fw.op("act", lambda g: g.activation(out=out, in_=in_, func=AF.Copy), reads, writes)
    return fw.op(e, lambda g: g.tensor_copy(out=out, in_=in_), reads, writes)


class Ctx:
    pass


def phase_w(C):
    nc, fw = C.nc, C.fw
    with ExitStack() as ph:
        wst = [ph.enter_context(nc.sbuf_tensor("wst%d" % i, [128, 16, 128], F32)) for i in range(3)]
        wbf = [ph.enter_context(nc.sbuf_tensor("wbf%d" % i, [128, 16, 128], BF16)) for i in range(3)]
        b_wst = [fw.buf("wst%d" % i) for i in range(3)]
        b_wbf = [fw.buf("wbf%d" % i) for i in range(3)]
        it = 0
        for (src, dst, ntile, ncols) in ((C.w_in, C.wi_bf, 45, INW), (C.w_out, C.wo_bf, 16, D), (C.peer_wq, C.wq_bf, 16, D)):
            w_v = src.rearrange("(dt p) c -> p dt c", p=128)
            for j in range(ntile):
                k = it % 3
                nco = min(128, ncols - j * 128)
                fw.dma("sp", b_wst[k], writes=[b_wst[k]], out=wst[k][:, :, 0:nco],
                       in_=w_v[:, :, j * 128:j * 128 + nco])
                evac(fw, _alt(it, ("act", "dve", "pool")), wbf[k][:, :, 0:nco], wst[k][:, :, 0:nco],
                     [b_wst[k]], [b_wbf[k]])
                fw.dma("sp", b_wbf[k], reads=[b_wbf[k]], out=dst[j, :, :, 0:nco], in_=wbf[k][:, :, 0:nco])
                it += 1
        fw.barrier()


def phase_proj(C):
    nc, fw, ps_tiles, ps_bufs = C.nc, C.fw, C.ps_tiles, C.ps_bufs
    with ExitStack() as ph:
        psb = lambda n, s, dt=F32: ph.enter_context(nc.sbuf_tensor(n, list(s), dt))
        xs = [psb("xs%d" % i, [128, 4, D]) for i in range(2)]
        b_xs = [fw.buf("xs%d" % i) for i in range(2)]
        xT = [psb("xT%d" % i, [128, 16, 512], BF16) for i in range(2)]
        b_xT = [[fw.buf("xT%d_%d" % (i, dt)) for dt in range(16)] for i in range(2)]
        NW = 3
        wt = [psb("wt%d" % i, [128, 16, 128], BF16) for i in range(NW)]
        b_wt = [fw.buf("wt%d" % i) for i in range(NW)]
        NS = 4
        stA = [psb("stA%d" % i, [128, 512], BF16) for i in range(NS)]
        stD = [psb("stD%d" % i, [128, 512], F32) for i in range(NS)]
        b_st = [fw.buf("st%d" % i) for i in range(NS)]
        x_v = C.x.rearrange("(t p) d -> p t d", p=128)
        nblk = 8
        ev_i = wq_i = st_i = 0

        def tiles_for(blk):
            if blk < 4:
                return list(range(45))
            if blk == 4:
                return list(range(8, 36)) + [44]
            return list(range(20, 36)) + [44]

        fw.dma("sp", b_xs[0], writes=[b_xs[0]], out=xs[0][:], in_=x_v[:, 0:4, :])
        for blk in range(nblk):
            k = blk % 2
            if blk + 1 < nblk:
                fw.dma("sp", b_xs[1 - k], writes=[b_xs[1 - k]], out=xs[1 - k][:],
                       in_=x_v[:, (blk + 1) * 4:(blk + 2) * 4, :])
            for dt in range(16):
                pi = C.next_ps()
                for t in range(4):
                    fw.op("pe", lambda g, t=t, dt=dt, pi=pi: g.transpose(
                        out=ps_tiles[pi][:, t * 128:(t + 1) * 128],
                        in_=xs[k][:, t, dt * 128:(dt + 1) * 128], identity=C.ident[:]),
                        reads=[b_xs[k], C.b_ident], writes=[ps_bufs[pi]])
                evac(fw, _alt(ev_i), xT[k][:, dt, :], ps_tiles[pi][:, :], [ps_bufs[pi]], [b_xT[k][dt]])
                ev_i += 1
            for j in tiles_for(blk):
                wk = wq_i % NW
                wq_i += 1
                nco = 128 if j < 44 else 32
                fw.dma("sp", b_wt[wk], writes=[b_wt[wk]], out=wt[wk][:, :, 0:nco], in_=C.wi_bf[j, :, :, 0:nco])
                pi = C.next_ps()
                for dt in range(16):
                    fw.op("pe", lambda g, dt=dt, pi=pi, wk=wk, nco=nco: g.matmul(
                        out=ps_tiles[pi][0:nco, :], lhsT=wt[wk][:, dt, 0:nco], rhs=xT[k][:, dt, :],
                        start=(dt == 0), stop=(dt == 15)),
                        reads=[b_wt[wk], b_xT[k][dt]], writes=[ps_bufs[pi]])
                si = st_i % NS
                st_i += 1
                tok = slice(blk * 512, (blk + 1) * 512)
                if j < 10:
                    evac(fw, _alt(ev_i), stA[si][:, :], ps_tiles[pi][:, :], [ps_bufs[pi]], [b_st[si]])
                    fw.dma("pool", b_st[si], reads=[b_st[si]], out=C.projA[j, :, tok], in_=stA[si][:, :])
                elif j < 44:
                    dst = C.projV[j - 10, :, tok] if j < 12 else C.projD[j - 12, :, tok]
                    evac(fw, _alt(ev_i), stD[si][:, :], ps_tiles[pi][:, :], [ps_bufs[pi]], [b_st[si]])
                    fw.dma("pool", b_st[si], reads=[b_st[si]], out=dst, in_=stD[si][:, :])
                else:
                    evac(fw, _alt(ev_i), stD[si][0:32, :], ps_tiles[pi][0:32, :], [ps_bufs[pi]], [b_st[si]])
                    fw.dma("pool", b_st[si], reads=[b_st[si]], out=C.gates[:, tok], in_=stD[si][0:32, :])
                ev_i += 1
        fw.barrier()


def phase_attn(C):
    nc, fw, ps_tiles, ps_bufs = C.nc, C.fw, C.ps_tiles, C.ps_bufs
    NKB = 17
    with ExitStack() as ph:
        psb = lambda n, s, dt=F32: ph.enter_context(nc.sbuf_tensor(n, list(s), dt))
        qT = psb("qT", [128, 8, OWN], BF16)
        kT = psb("kT", [128, 2, NKB * 128], BF16)
        vT = psb("vTf", [128, 2, NKB * 128], F32)
        V = psb("V", [128, NKB, 2, 128], BF16)
        biasT = psb("biasT_sb", [128, 6, 512], F32)
        esink = psb("esink", [128, 2, 512], F32)
        ones = psb("ones_bf", [128, 128], BF16)
        b_qT, b_kT, b_vT, b_bias, b_esink, b_ones = (fw.buf(n) for n in ("qT", "kT", "vTf", "biasT", "esink", "ones"))
        b_V = [fw.buf("V%d" % i) for i in range(NKB)]
        for h in range(8):
            fw.dma("sp", b_qT, writes=[b_qT], out=qT[:, h, :], in_=C.projA[h, :, 0:OWN])
        for g in range(2):
            fw.dma("sp", b_kT, writes=[b_kT], out=kT[:, g, :], in_=C.projA[8 + g, :, 0:NKB * 128])
            fw.dma("sp", b_vT, writes=[b_vT], out=vT[:, g, :], in_=C.projV[g, :, 0:NKB * 128])
        fw.dma("sp", b_bias, writes=[b_bias], out=biasT[:], in_=C.biasT_d.rearrange("a k c -> k a c"))
        fw.dma("sp", b_esink, writes=[b_esink], out=esink[:], in_=C.sink_d[:, :, :])
        fw.op("act", lambda g_: g_.activation(out=esink[:], in_=esink[:], func=AF.Exp), [b_esink], [b_esink])
        fw.op("dve", lambda g_: g_.memset(ones[:], 1.0), [], [b_ones])
        ev_i = 0
        for kb in range(NKB):
            pi = C.next_ps()
            for g in range(2):
                fw.op("pe", lambda g_, g=g, kb=kb, pi=pi: g_.transpose(
                    out=ps_tiles[pi][:, g * 128:(g + 1) * 128], in_=vT[:, g, kb * 128:(kb + 1) * 128],
                    identity=C.ident[:]), reads=[b_vT, C.b_ident], writes=[ps_bufs[pi]])
            evac(fw, _alt(ev_i), V[:, kb, :, :], ps_tiles[pi][:, 0:256].rearrange("p (g d) -> p g d", g=2),
                 [ps_bufs[pi]], [b_V[kb]])
            ev_i += 1
        NP = 6
        tS = [psb("tS%d" % i, [128, 512], F32) for i in range(NP)]
        pT = [psb("pT%d" % i, [128, 512], BF16) for i in range(NP)]
        b_tS = [fw.buf("tS%d" % i) for i in range(NP)]
        b_pT = [fw.buf("pT%d" % i) for i in range(NP)]
        den = [psb("den%d" % i, [128, 512], F32) for i in range(2)]
        ost = [psb("ost%d" % i, [128, 512], BF16) for i in range(2)]
        b_ost = [fw.buf("ost%d" % i) for i in range(2)]
        b_den = [fw.buf("den%d" % i) for i in range(2)]
        p_i = 0
        scale = 128.0 ** -0.5
        for i in range(16):
            for g in range(2):
                kbs = [kb for kb in (i - 1, i, i + 1) if kb >= 0]
                slots = []
                for kb in kbs:
                    pi = C.next_ps()
                    for hh in range(4):
                        fw.op("pe", lambda g_, g=g, kb=kb, pi=pi, hh=hh, i=i: g_.matmul(
                            out=ps_tiles[pi][:, hh * 128:(hh + 1) * 128], lhsT=kT[:, g, kb * 128:(kb + 1) * 128],
                            rhs=qT[:, 4 * g + hh, i * 128:(i + 1) * 128], start=True, stop=True),
                            reads=[b_kT, b_qT], writes=[ps_bufs[pi]])
                    sl = p_i % NP
                    p_i += 1
                    rel = kb - i + 1
                    fw.op("dve", lambda g_, pi=pi, sl=sl, g=g, rel=rel: g_.scalar_tensor_tensor(
                        out=tS[sl][:], in0=ps_tiles[pi][:, :], scalar=scale, in1=biasT[:, g * 3 + rel, :],
                        op0=ALU.mult, op1=ALU.add), reads=[ps_bufs[pi], b_bias], writes=[b_tS[sl]])
                    fw.op("act", lambda g_, sl=sl: g_.activation(out=pT[sl][:], in_=tS[sl][:], func=AF.Exp),
                          reads=[b_tS[sl]], writes=[b_pT[sl]])
                    slots.append(sl)
                po = C.next_ps()
                pd = C.next_ps()
                for hh in range(4):
                    for n, (kb, sl) in enumerate(zip(kbs, slots)):
                        fw.op("pe", lambda g_, g=g, kb=kb, sl=sl, hh=hh, n=n, po=po: g_.matmul(
                            out=ps_tiles[po][:, hh * 128:(hh + 1) * 128], lhsT=V[:, kb, g, :],
                            rhs=pT[sl][:, hh * 128:(hh + 1) * 128], start=(n == 0), stop=(n == len(kbs) - 1)),
                            reads=[b_V[kb], b_pT[sl]], writes=[ps_bufs[po]])
                for n, sl in enumerate(slots):
                    fw.op("pe", lambda g_, sl=sl, n=n, pd=pd: g_.matmul(
                        out=ps_tiles[pd][:, :], lhsT=ones[:], rhs=pT[sl][:, :], start=(n == 0),
                        stop=(n == len(slots) - 1)), reads=[b_ones, b_pT[sl]], writes=[ps_bufs[pd]])
                dk = (i * 2 + g) % 2
                fw.op("dve", lambda g_, pd=pd, dk=dk, g=g: g_.tensor_tensor(
                    out=den[dk][:], in0=ps_tiles[pd][:, :], in1=esink[:, g, :], op=ALU.add),
                    reads=[ps_bufs[pd], b_esink], writes=[b_den[dk]])
                fw.op("dve", lambda g_, dk=dk: g_.reciprocal(out=den[dk][:], in_=den[dk][:]),
                      reads=[b_den[dk]], writes=[b_den[dk]])
                fw.op("dve", lambda g_, po=po, dk=dk: g_.tensor_tensor(
                    out=ost[dk][:, :], in0=ps_tiles[po][:, :], in1=den[dk][:, :], op=ALU.mult),
                    reads=[ps_bufs[po], b_den[dk]], writes=[b_ost[dk]])
                fw.dma("pool", b_ost[dk], reads=[b_ost[dk]],
                       out=C.mixT_d[:, 4 * g:4 * g + 4, i * 128:(i + 1) * 128],
                       in_=ost[dk][:, :].rearrange("p (h q) -> p h q", h=4))
        fw.barrier()


def phase_dn(C):
    nc, fw, ps_tiles, ps_bufs = C.nc, C.fw, C.ps_tiles, C.ps_bufs
    ident = C.ident
    with ExitStack() as ph:
        psb = lambda n, s, dt=F32: ph.enter_context(nc.sbuf_tensor(n, list(s), dt))
        TRI = psb("TRI_sb", [128, 2, 128])
        AFT = psb("AFT_sb", [128, 2, 128])
        NEGI4 = psb("NEGI4_sb", [128, 2, 512])
        STR4 = psb("STR4_sb", [128, 2, 512])
        I4 = psb("I4_sb", [128, 512])
        cw = psb("cw_sb", [128, 24, 5])
        normw = psb("normw_sb", [128, 128])
        ones_f = psb("ones_f", [128, 128])
        ones_b = psb("ones_b2", [128, 128], BF16)
        b_const = fw.buf("dnconst")
        for (t, src) in ((TRI, C.TRI_d), (AFT, C.AFT_d), (NEGI4, C.NEGI4_d), (STR4, C.STR4_d)):
            fw.dma("sp", b_const, writes=[b_const], out=t[:], in_=src[:, :, :])
        fw.dma("sp", b_const, writes=[b_const], out=I4[:], in_=C.I4_d[:, :])
        fw.dma("sp", b_const, writes=[b_const], out=cw[:], in_=C.cwT_d.rearrange("t p j -> p t j"))
        fw.dma("sp", b_const, writes=[b_const], out=normw[:], in_=C.normw_d[:, :])
        b_ones = fw.buf("dnones")
        fw.op("dve", lambda g: g.memset(ones_f[:], 1.0), [], [b_ones])
        fw.op("dve", lambda g: g.memset(ones_b[:], 1.0), [b_ones], [b_ones])

        beta = psb("beta", [128, 32, 16])
        graw = psb("graw", [128, 32, 16])
        sc_eg = psb("sc_eg", [128, 48, 8])
        sc_negeg = psb("sc_negeg", [128, 48, 8])
        sc_negg = psb("sc_negg", [128, 48, 8])
        sc_ekd = psb("sc_ekd", [128, 48, 8])
        b_beta, b_graw, b_sc = fw.buf("beta"), fw.buf("graw"), fw.buf("sc")
        with ExitStack() as pg:
            gsb = pg.enter_context(nc.sbuf_tensor("gsb", [32, SEQ], F32))
            Gtok = pg.enter_context(nc.sbuf_tensor("Gtok", [128, 32, 32], F32))
            dtb = pg.enter_context(nc.sbuf_tensor("dtb_sb", [128, 32, 16], F32))
            alg = pg.enter_context(nc.sbuf_tensor("alog_sb", [128, 32, 16], F32))
            b_gsb, b_Gtok, b_dtb, b_alg = fw.buf("gsb"), fw.buf("Gtok"), fw.buf("dtb"), fw.buf("alg")
            fw.dma("sp", b_gsb, writes=[b_gsb], out=gsb[:], in_=C.gates[:, :])
            fw.dma("sp", b_dtb, writes=[b_dtb], out=dtb[:], in_=C.dtb_d[:, :, :])
            fw.dma("sp", b_alg, writes=[b_alg], out=alg[:], in_=C.alog_d[:, :, :])
            for half in range(2):
                pi = C.next_ps()
                for cc in range(16):
                    c = half * 16 + cc
                    fw.op("pe", lambda g, pi=pi, cc=cc, c=c: g.transpose(
                        out=ps_tiles[pi][:, cc * 32:(cc + 1) * 32], in_=gsb[0:32, c * 128:(c + 1) * 128],
                        identity=ident[0:32, 0:32]), reads=[b_gsb, C.b_ident], writes=[ps_bufs[pi]])
                evac(fw, "dve", Gtok[:, half * 16:(half + 1) * 16, :],
                     ps_tiles[pi][:, :].rearrange("p (c k) -> p c k", k=32), [ps_bufs[pi]], [b_Gtok])
            fw.op("act", lambda g: g.activation(out=beta[:], in_=Gtok[:, :, 0:16], func=AF.Sigmoid), [b_Gtok], [b_beta])
            fw.op("dve", lambda g: g.tensor_tensor(out=graw[:], in0=Gtok[:, :, 16:32], in1=dtb[:], op=ALU.add),
                  [b_Gtok, b_dtb], [b_graw])
            fw.op("act", lambda g: g.activation(out=graw[:], in_=graw[:], func=AF.Exp), [b_graw], [b_graw])
            fw.op("dve", lambda g: g.tensor_scalar(out=graw[:], in0=graw[:], scalar1=1.0, scalar2=None, op0=ALU.add),
                  [b_graw], [b_graw])
            fw.op("act", lambda g: g.activation(out=graw[:], in_=graw[:], func=AF.Ln), [b_graw], [b_graw])
            fw.op("act", lambda g: g.activation(out=alg[:], in_=alg[:], func=AF.Exp), [b_alg], [b_alg])
            fw.op("dve", lambda g: g.scalar_tensor_tensor(out=graw[:], in0=graw[:], scalar=-1.0, in1=alg[:],
                                                          op0=ALU.mult, op1=ALU.mult), [b_graw, b_alg], [b_graw])
            pg_, pa_ = C.next_ps(), C.next_ps()
            for d in range(2):
                for c in range(16 if d == 0 else 32):
                    dc = c if d == 0 else 16 + c
                    fw.op("pe", lambda g, d=d, c=c, dc=dc: g.matmul(
                        out=ps_tiles[pg_][:, dc * 8:(dc + 1) * 8], lhsT=TRI[:, d, :], rhs=graw[:, c, d * 8:(d + 1) * 8],
                        start=True, stop=True), reads=[b_const, b_graw], writes=[ps_bufs[pg_]])
                    fw.op("pe", lambda g, d=d, c=c, dc=dc: g.matmul(
                        out=ps_tiles[pa_][:, dc * 8:(dc + 1) * 8], lhsT=AFT[:, d, :], rhs=graw[:, c, d * 8:(d + 1) * 8],
                        start=True, stop=True), reads=[b_const, b_graw], writes=[ps_bufs[pa_]])
            v3 = lambda t: t[:, :, :].rearrange("p a b -> p (a b)")
            fw.op("act", lambda g: g.activation(out=v3(sc_eg), in_=ps_tiles[pg_][:, 0:384], func=AF.Exp),
                  [ps_bufs[pg_]], [b_sc])
            fw.op("dve", lambda g: g.tensor_scalar(out=v3(sc_negg), in0=ps_tiles[pg_][:, 0:384], scalar1=-1.0,
                                                   scalar2=None, op0=ALU.mult), [ps_bufs[pg_]], [b_sc])
            fw.op("dve", lambda g: g.tensor_scalar(out=v3(sc_negeg), in0=v3(sc_eg), scalar1=-1.0, scalar2=None,
                                                   op0=ALU.mult), [b_sc], [b_sc])
            fw.op("act", lambda g: g.activation(out=v3(sc_ekd), in_=ps_tiles[pa_][:, 0:384], func=AF.Exp),
                  [ps_bufs[pa_]], [b_sc])
            fw.barrier()

        pad = psb("pad", [128, SEQ + 4])
        acc = psb("acc", [128, SEQ])
        y32 = psb("y32", [128, SEQ])
        b_pad = fw.buf("pad")
        b_acc = [fw.buf("acc0"), fw.buf("acc1")]
        b_y = [fw.buf("y%d" % i) for i in range(8)]
        sq = [psb("sq%d" % i, [128, 512], BF16) for i in range(2)]
        rn = [psb("rn%d" % i, [128, 512]) for i in range(2)]
        b_sq = [fw.buf("sq%d" % i) for i in range(2)]
        b_rn = [fw.buf("rn%d" % i) for i in range(2)]
        QT = psb("QT", [128, OWN], BF16)
        KT = psb("KT", [128, SEQ], BF16)
        Vtok = psb("Vtok", [128, 32, 128], BF16)
        kd = psb("kd", [128, 48, 128], BF16)
        X = psb("Xinv", [128, 48, 128], BF16)
        AT = psb("AT", [128, 32, 128], BF16)
        qgT = psb("qgT", [128, 32, 128], BF16)
        glb = psb("glb", [128, 48])
        zs = psb("zs", [128, OWN])
        obwd = psb("obwd", [128, 16, 128])
        b_QT = [fw.buf("QT%d" % i) for i in range(4)]
        b_KT = [fw.buf("KT%d" % i) for i in range(8)]
        b_Vtok = [fw.buf("Vtok%d" % i) for i in range(8)]
        b_kd = [fw.buf("kd%d" % i) for i in range(48)]
        b_X = [fw.buf("X%d" % i) for i in range(12)]
        b_AT = [fw.buf("AT%d" % i) for i in range(8)]
        b_qgT = [fw.buf("qgT%d" % i) for i in range(8)]
        b_glb = [fw.buf("glb%d" % i) for i in range(12)]
        b_zs = fw.buf("zs")
        b_obwd = [fw.buf("obwd%d" % i) for i in range(16)]
        GB = psb("GB", [128, 512])
        tmpg = psb("tmpg", [128, 512])
        decT = psb("decT", [128, 512])
        EGR = psb("EGR", [128, 512])
        m1 = psb("m1", [128, 512])
        Pm = [psb("Pm%d" % i, [128, 512]) for i in range(2)]
        PT = [psb("PT%d" % i, [128, 512]) for i in range(2)]
        Rn = [psb("Rn%d" % i, [128, 512]) for i in range(2)]
        b_GB, b_tmpg, b_decT, b_EGR, b_m1 = (fw.buf(n) for n in ("GB", "tmpg", "decT", "EGR", "m1"))
        b_Pm = [fw.buf("Pm%d" % i) for i in range(2)]
        b_PT = [fw.buf("PT%d" % i) for i in range(2)]
        b_Rn = [fw.buf("Rn%d" % i) for i in range(2)]
        S32 = psb("S32", [128, 128])
        S16 = psb("S16", [128, 128], BF16)
        b_S32, b_S16 = fw.buf("S32"), fw.buf("S16")
        Rt = [psb("Rt%d" % i, [128, 128], BF16) for i in range(2)]
        vn = [psb("vn%d" % i, [128, 128], BF16) for i in range(2)]
        b_Rt = [fw.buf("Rt%d" % i) for i in range(2)]
        b_vn = [fw.buf("vn%d" % i) for i in range(2)]
        ot = [psb("ot%d" % i, [128, 128]) for i in range(2)]
        osq = [psb("osq%d" % i, [128, 128]) for i in range(2)]
        orr = [psb("orr%d" % i, [128, 2]) for i in range(2)]
        on_ = [psb("on%d" % i, [128, 128]) for i in range(2)]
        mo = [psb("mo%d" % i, [128, 128], BF16) for i in range(2)]
        b_ot = [fw.buf("ot%d" % i) for i in range(2)]
        b_on = [fw.buf("on%d" % i) for i in range(2)]
        b_mo = [fw.buf("mo%d" % i) for i in range(2)]
        fw.op("pool", lambda g: g.memset(pad[:, 0:2], 0.0), [], [b_pad])
        fw.op("pool", lambda g: g.memset(pad[:, SEQ + 2:SEQ + 4], 0.0), [b_pad], [b_pad])
        ev = [0]

        def conv_silu(tile_idx, ct, nload, n):
            fw.dma("sp", b_pad, writes=[b_pad], out=pad[:, 2:2 + nload], in_=C.projD[tile_idx, :, 0:nload])
            hlf = n // 2
            for (lo, hi), e, ba in (((0, hlf), "dve", b_acc[0]), ((hlf, n), "dve", b_acc[1])):
                fw.op(e, lambda g, lo=lo, hi=hi: g.tensor_scalar(
                    out=acc[:, lo:hi], in0=pad[:, lo:hi], scalar1=cw[:, ct, 0:1], scalar2=None, op0=ALU.mult),
                    reads=[b_pad, b_const], writes=[ba])
                for j in range(1, 5):
                    fw.op(e, lambda g, lo=lo, hi=hi, j=j: g.scalar_tensor_tensor(
                        out=acc[:, lo:hi], in0=pad[:, lo + j:hi + j], scalar=cw[:, ct, j:j + 1], in1=acc[:, lo:hi],
                        op0=ALU.mult, op1=ALU.add), reads=[b_pad, b_const, ba], writes=[ba])
            fw.op("act", lambda g: g.activation(out=y32[:, 0:n], in_=acc[:, 0:n], func=AF.Silu),
                  reads=b_acc, writes=b_y[0:n // 512])

        def l2norm(n, is_q):
            for blk in range(n // 512):
                cs = slice(blk * 512, (blk + 1) * 512)
                k = blk % 2
                fw.op("act", lambda g, cs=cs, k=k: g.activation(out=sq[k][:], in_=y32[:, cs], func=AF.Square),
                      reads=[b_y[blk]], writes=[b_sq[k]])
                pi = C.next_ps()
                fw.op("pe", lambda g, pi=pi, k=k: g.matmul(out=ps_tiles[pi][:, :], lhsT=ones_b[:], rhs=sq[k][:],
                                                          start=True, stop=True), reads=[b_ones, b_sq[k]], writes=[ps_bufs[pi]])
                fw.op("dve", lambda g, pi=pi, k=k: g.tensor_scalar(
                    out=rn[k][:], in0=ps_tiles[pi][:, :], scalar1=RMS_EPS, scalar2=None, op0=ALU.add),
                    reads=[ps_bufs[pi]], writes=[b_rn[k]])
                rsqrt_inplace(fw, rn[k][:], b_rn[k])
                if is_q:
                    fw.op("dve", lambda g, cs=cs, k=k: g.scalar_tensor_tensor(
                        out=QT[:, cs], in0=y32[:, cs], scalar=128.0 ** -0.5, in1=rn[k][:], op0=ALU.mult, op1=ALU.mult),
                        reads=[b_y[blk], b_rn[k]], writes=[b_QT[blk]])
                else:
                    fw.op("dve", lambda g, cs=cs, k=k: g.tensor_tensor(out=y32[:, cs], in0=y32[:, cs], in1=rn[k][:],
                                                                      op=ALU.mult), reads=[b_y[blk], b_rn[k]], writes=[b_y[blk]])
                    fw.op("act", lambda g, cs=cs: g.activation(out=KT[:, cs], in_=y32[:, cs], func=AF.Copy),
                          reads=[b_y[blk]], writes=[b_KT[blk]])

        import os
        _nh = int(os.environ.get('DN_HEADS', '8'))
        _sub = os.environ.get('DN_SUB', '123')
        for h in range(_nh):
            fw.dma("sp", b_zs, writes=[b_zs], out=zs[:], in_=C.projD[24 + h, :, 0:OWN])
            fw.op("act", lambda g: g.activation(out=zs[:], in_=zs[:], func=AF.Silu), [b_zs], [b_zs])
            conv_silu(h, h, OWN + 2, OWN)
            l2norm(OWN, True)
            conv_silu(8 + h, 8 + h, SEQ, SEQ)
            l2norm(SEQ, False)
            for c0 in range(0, 32, 4):
                pi = C.next_ps()
                for q in range(4):
                    c = c0 + q
                    fw.op("pe", lambda g, pi=pi, q=q, c=c: g.transpose(
                        out=ps_tiles[pi][:, q * 128:(q + 1) * 128], in_=y32[:, c * 128:(c + 1) * 128], identity=ident[:]),
                        reads=[b_y[c // 4], C.b_ident], writes=[ps_bufs[pi]])
                for q in range(4):
                    c = c0 + q
                    fw.op("dve", lambda g, pi=pi, q=q, c=c: g.tensor_scalar(
                        out=kd[:, 16 + c, :], in0=ps_tiles[pi][:, q * 128:(q + 1) * 128], scalar1=sc_ekd[:, 16 + c, h:h + 1],
                        scalar2=None, op0=ALU.mult), reads=[ps_bufs[pi], b_sc], writes=[b_kd[16 + c]])
                    if c < 16:
                        fw.op("dve", lambda g, pi=pi, q=q, c=c: g.tensor_scalar(
                            out=kd[:, c, :], in0=ps_tiles[pi][:, q * 128:(q + 1) * 128], scalar1=sc_ekd[:, c, h:h + 1],
                            scalar2=None, op0=ALU.mult), reads=[ps_bufs[pi], b_sc], writes=[b_kd[c]])
            conv_silu(16 + h, 16 + h, SEQ, SEQ)
            for c0 in range(0, 32, 4):
                pi = C.next_ps()
                for q in range(4):
                    c = c0 + q
                    fw.op("pe", lambda g, pi=pi, q=q, c=c: g.transpose(
                        out=ps_tiles[pi][:, q * 128:(q + 1) * 128], in_=y32[:, c * 128:(c + 1) * 128], identity=ident[:]),
                        reads=[b_y[c // 4], C.b_ident], writes=[ps_bufs[pi]])
                evac(fw, _alt(ev[0]), Vtok[:, c0:c0 + 4, :], ps_tiles[pi][:, :].rearrange("p (c d) -> p c d", c=4),
                     [ps_bufs[pi]], [b_Vtok[c0 // 4]])
                ev[0] += 1

            for d in ((1, 0) if '2' in _sub else ()):
                for c0 in range(0, 32 if d == 1 else 16, 4):
                    dc0 = c0 if d == 0 else 16 + c0
                    has_out = c0 < 16
                    oc0 = c0 if d == 0 else 16 + c0
                    col = d * 8 + h
                    last = 127 if d == 0 else 0
                    pG = C.next_ps()
                    for q in range(4):
                        c = c0 + q
                        fw.op("pool", lambda g, q=q, c=c: g.tensor_scalar(
                            out=GB[:, q * 128:(q + 1) * 128], in0=ones_f[:], scalar1=graw[:, c, col:col + 1], scalar2=None,
                            op0=ALU.mult), reads=[b_ones, b_graw], writes=[b_GB])
                    for q in range(4):
                        fw.op("pe", lambda g, q=q, pG=pG: g.matmul(
                            out=ps_tiles[pG][:, q * 128:(q + 1) * 128], lhsT=GB[:, q * 128:(q + 1) * 128], rhs=TRI[:, d, :],
                            start=True, stop=True), reads=[b_GB, b_const], writes=[ps_bufs[pG]])
                    fw.op("dve", lambda g, pG=pG: g.tensor_tensor(out=tmpg[:], in0=ps_tiles[pG][:, :], in1=NEGI4[:, d, :],
                                                                 op=ALU.add), reads=[ps_bufs[pG], b_const], writes=[b_tmpg])
                    for q in range(4):
                        fw.op("act", lambda g, q=q: g.activation(
                            out=decT[:, q * 128:(q + 1) * 128], in_=tmpg[:, q * 128:(q + 1) * 128], func=AF.Exp,
                            bias=sc_negg[:, dc0 + q, h:h + 1]), reads=[b_tmpg, b_sc], writes=[b_decT])
                    fw.op("act", lambda g, pG=pG: g.activation(out=EGR[:], in_=ps_tiles[pG][:, :], func=AF.Exp),
                          reads=[ps_bufs[pG]], writes=[b_EGR])
                    fw.op("pool", lambda g: g.tensor_copy(
                        out=glb[:, dc0:dc0 + 4], in_=EGR[:, :].rearrange("p (c i) -> p c i", c=4)[:, :, last]),
                        reads=[b_EGR], writes=[b_glb[dc0 // 4]])
                    pK = C.next_ps()
                    for q in range(4):
                        c = c0 + q
                        fw.op("pe", lambda g, q=q, c=c, pK=pK: g.matmul(
                            out=ps_tiles[pK][:, q * 128:(q + 1) * 128], lhsT=KT[:, c * 128:(c + 1) * 128],
                            rhs=KT[:, c * 128:(c + 1) * 128], start=True, stop=True),
                            reads=[b_KT[c // 4]], writes=[ps_bufs[pK]])
                    for q in range(4):
                        c = c0 + q
                        fw.op("dve", lambda g, q=q, c=c, pK=pK: g.scalar_tensor_tensor(
                            out=m1[:, q * 128:(q + 1) * 128], in0=ps_tiles[pK][:, q * 128:(q + 1) * 128],
                            scalar=beta[:, c, col:col + 1], in1=decT[:, q * 128:(q + 1) * 128], op0=ALU.mult, op1=ALU.mult),
                            reads=[ps_bufs[pK], b_beta, b_decT], writes=[b_m1])
                    fw.op("pool", lambda g: g.tensor_tensor(out=Pm[0][:], in0=m1[:], in1=STR4[:, d, :], op=ALU.mult),
                          reads=[b_m1, b_const], writes=[b_Pm[0]])
                    if has_out:
                        pQ = C.next_ps()
                        for q in range(4):
                            c = c0 + q
                            fw.op("pe", lambda g, q=q, c=c, pQ=pQ: g.matmul(
                                out=ps_tiles[pQ][:, q * 128:(q + 1) * 128], lhsT=KT[:, c * 128:(c + 1) * 128],
                                rhs=QT[:, c * 128:(c + 1) * 128], start=True, stop=True),
                                reads=[b_KT[c // 4], b_QT[c // 4]], writes=[ps_bufs[pQ]])
                        fw.op("dve", lambda g, pQ=pQ: g.tensor_tensor(
                            out=AT[:, oc0:oc0 + 4, :], in0=ps_tiles[pQ][:, :].rearrange("p (c i) -> p c i", c=4),
                            in1=decT[:, :].rearrange("p (c i) -> p c i", c=4), op=ALU.mult),
                            reads=[ps_bufs[pQ], b_decT], writes=[b_AT[oc0 // 4]])
                        fw.op("pool", lambda g: g.tensor_tensor(
                            out=qgT[:, oc0:oc0 + 4, :], in0=QT[:, c0 * 128:(c0 + 4) * 128].rearrange("p (c i) -> p c i", c=4),
                            in1=EGR[:, :].rearrange("p (c i) -> p c i", c=4), op=ALU.mult),
                            reads=[b_QT[c0 // 4], b_EGR], writes=[b_qgT[oc0 // 4]])
                    pT = C.next_ps()
                    for q in range(4):
                        fw.op("pe", lambda g, q=q, pT=pT: g.transpose(
                            out=ps_tiles[pT][:, q * 128:(q + 1) * 128], in_=Pm[0][:, q * 128:(q + 1) * 128], identity=ident[:]),
                            reads=[b_Pm[0], C.b_ident], writes=[ps_bufs[pT]])
                    evac(fw, "act", PT[0][:], ps_tiles[pT][:, :], [ps_bufs[pT]], [b_PT[0]])
                    fw.op("pool", lambda g: g.tensor_tensor(out=Rn[0][:], in0=Pm[0][:], in1=I4[:], op=ALU.add),
                          reads=[b_Pm[0], b_const], writes=[b_Rn[0]])
                    cur = 0
                    for lvl in range(1, 7):
                        nxt = 1 - cur
                        if lvl < 6:
                            pA = C.next_ps()
                            for q in range(4):
                                qs = slice(q * 128, (q + 1) * 128)
                                fw.op("pe", lambda g, qs=qs, pA=pA, cur=cur: g.matmul(
                                    out=ps_tiles[pA][:, qs], lhsT=PT[cur][:, qs], rhs=Pm[cur][:, qs], start=True, stop=True),
                                    reads=[b_PT[cur], b_Pm[cur]], writes=[ps_bufs[pA]])
                        pB = C.next_ps()
                        for q in range(4):
                            qs = slice(q * 128, (q + 1) * 128)
                            fw.op("pe", lambda g, qs=qs, pB=pB, cur=cur: g.matmul(
                                out=ps_tiles[pB][:, qs], lhsT=Pm[cur][:, qs], rhs=PT[cur][:, qs], start=True, stop=True),
                                reads=[b_PT[cur], b_Pm[cur]], writes=[ps_bufs[pB]])
                        evac(fw, "act", PT[nxt][:], ps_tiles[pB][:, :], [ps_bufs[pB]], [b_PT[nxt]])
                        if lvl < 6:
                            evac(fw, "dve", Pm[nxt][:], ps_tiles[pA][:, :], [ps_bufs[pA]], [b_Pm[nxt]])
                        pC = C.next_ps()
                        for q in range(4):
                            qs = slice(q * 128, (q + 1) * 128)
                            fw.op("pe", lambda g, qs=qs, pC=pC, cur=cur, nxt=nxt: g.matmul(
                                out=ps_tiles[pC][:, qs], lhsT=PT[nxt][:, qs], rhs=Rn[cur][:, qs], start=True, stop=True),
                                reads=[b_PT[nxt], b_Rn[cur]], writes=[ps_bufs[pC]])
                        if lvl < 6:
                            fw.op("dve", lambda g, pC=pC, cur=cur, nxt=nxt: g.tensor_tensor(
                                out=Rn[nxt][:], in0=ps_tiles[pC][:, :], in1=Rn[cur][:], op=ALU.add),
                                reads=[ps_bufs[pC], b_Rn[cur]], writes=[b_Rn[nxt]])
                        else:
                            fw.op("dve", lambda g, pC=pC, cur=cur: g.tensor_tensor(
                                out=X[:, dc0:dc0 + 4, :], in0=ps_tiles[pC][:, :].rearrange("p (c i) -> p c i", c=4),
                                in1=Rn[cur][:, :].rearrange("p (c i) -> p c i", c=4), op=ALU.add),
                                reads=[ps_bufs[pC], b_Rn[cur]], writes=[b_X[dc0 // 4]])
                        cur = nxt

            it = 0
            for d in ((1, 0) if '3' in _sub else ()):
                col = d * 8 + h
                fw.op("pool", lambda g: g.memset(S32[:], 0.0), [], [b_S32])
                fw.op("pool", lambda g: g.memset(S16[:], 0.0), [], [b_S16])
                order = list(range(31, -1, -1)) if d == 1 else list(range(16))
                for c in order:
                    dc = c if d == 0 else 16 + c
                    oc = c if d == 0 else 16 + c
                    k = it % 2
                    it += 1
                    cs = slice(c * 128, (c + 1) * 128)
                    p1 = C.next_ps()
                    fw.op("pe", lambda g, p1=p1, cs=cs: g.matmul(out=ps_tiles[p1][:, 0:128], lhsT=KT[:, cs], rhs=S16[:],
                                                                 start=True, stop=True),
                          reads=[b_KT[c // 4], b_S16], writes=[ps_bufs[p1]])
                    fw.op("dve", lambda g, p1=p1, k=k, c=c, dc=dc: g.scalar_tensor_tensor(
                        out=Rt[k][:], in0=ps_tiles[p1][:, 0:128], scalar=sc_negeg[:, dc, h:h + 1], in1=Vtok[:, c, :],
                        op0=ALU.mult, op1=ALU.add), reads=[ps_bufs[p1], b_sc, b_Vtok[c // 4]], writes=[b_Rt[k]])
                    p2 = C.next_ps()
                    fw.op("pe", lambda g, p2=p2, k=k, dc=dc: g.matmul(out=ps_tiles[p2][:, 0:128], lhsT=X[:, dc, :], rhs=Rt[k][:],
                                                                      start=True, stop=True),
                          reads=[b_X[dc // 4], b_Rt[k]], writes=[ps_bufs[p2]])
                    fw.op("dve", lambda g, p2=p2, k=k, c=c: g.tensor_scalar(
                        out=vn[k][:], in0=ps_tiles[p2][:, 0:128], scalar1=beta[:, c, col:col + 1], scalar2=None,
                        op0=ALU.mult), reads=[ps_bufs[p2], b_beta], writes=[b_vn[k]])
                    if c < 16:
                        p3 = C.next_ps()
                        fw.op("pe", lambda g, p3=p3, oc=oc: g.matmul(out=ps_tiles[p3][:, 0:128], lhsT=qgT[:, oc, :], rhs=S16[:],
                                                                     start=True, stop=False),
                              reads=[b_qgT[oc // 4], b_S16], writes=[ps_bufs[p3]])
                        fw.op("pe", lambda g, p3=p3, oc=oc, k=k: g.matmul(out=ps_tiles[p3][:, 0:128], lhsT=AT[:, oc, :], rhs=vn[k][:],
                                                                          start=False, stop=True),
                              reads=[b_AT[oc // 4], b_vn[k]], writes=[ps_bufs[p3]])
                        if d == 1:
                            evac(fw, "act", obwd[:, c, :], ps_tiles[p3][:, 0:128], [ps_bufs[p3]], [b_obwd[c]])
                        else:
                            fw.op("dve", lambda g, p3=p3, k=k, c=c: g.tensor_tensor(
                                out=ot[k][:], in0=ps_tiles[p3][:, 0:128], in1=obwd[:, c, :], op=ALU.add),
                                reads=[ps_bufs[p3], b_obwd[c]], writes=[b_ot[k]])
                            fw.op("pool", lambda g, k=k: g.tensor_tensor(out=osq[k][:], in0=ot[k][:], in1=ot[k][:], op=ALU.mult),
                                  reads=[b_ot[k]], writes=[b_on[k]])
                            fw.op("dve", lambda g, k=k: g.reduce_sum(out=orr[k][:, 0:1], in_=osq[k][:], axis=AX.X),
                                  reads=[b_on[k]], writes=[b_on[k]])
                            fw.op("dve", lambda g, k=k: g.tensor_scalar(out=orr[k][:, 1:2], in0=orr[k][:, 0:1], scalar1=1.0 / 128,
                                                                        scalar2=RMS_EPS, op0=ALU.mult, op1=ALU.add),
                                  reads=[b_on[k]], writes=[b_on[k]])
                            rsqrt_inplace(fw, orr[k][:, 1:2], b_on[k])
                            evac(fw, "dve", orr[k][:, 0:1], orr[k][:, 1:2], [b_on[k]], [b_on[k]])
                            fw.op("dve", lambda g, k=k: g.scalar_tensor_tensor(
                                out=on_[k][:], in0=ot[k][:], scalar=orr[k][:, 0:1], in1=normw[:], op0=ALU.mult, op1=ALU.mult),
                                reads=[b_ot[k], b_on[k], b_const], writes=[b_on[k]])
                            p5 = C.next_ps()
                            fw.op("pe", lambda g, p5=p5, k=k: g.transpose(out=ps_tiles[p5][:, 0:128], in_=on_[k][:], identity=ident[:]),
                                  reads=[b_on[k], C.b_ident], writes=[ps_bufs[p5]])
                            fw.op("dve", lambda g, p5=p5, k=k, cs=cs: g.tensor_tensor(
                                out=mo[k][:], in0=ps_tiles[p5][:, 0:128], in1=zs[:, cs], op=ALU.mult),
                                reads=[ps_bufs[p5], b_zs], writes=[b_mo[k]])
                            fw.dma("sp", b_mo[k], reads=[b_mo[k]], out=C.mixT_d[:, 8 + h, cs], in_=mo[k][:])
                    p4 = C.next_ps()
                    fw.op("pe", lambda g, p4=p4, dc=dc, k=k: g.matmul(out=ps_tiles[p4][:, 0:128], lhsT=kd[:, dc, :], rhs=vn[k][:],
                                                                      start=True, stop=True),
                          reads=[b_kd[dc], b_vn[k]], writes=[ps_bufs[p4]])
                    fw.op("dve", lambda g, p4=p4, dc=dc: g.scalar_tensor_tensor(
                        out=S32[:], in0=S32[:], scalar=glb[:, dc:dc + 1], in1=ps_tiles[p4][:, 0:128], op0=ALU.mult, op1=ALU.add),
                        reads=[b_S32, b_glb[dc // 4], ps_bufs[p4]], writes=[b_S32])
                    evac(fw, "act", S16[:], S32[:], [b_S32], [b_S16])
        fw.barrier()


def layer_norm(fw, t, st, sqs, gbc, bbc, b_t, b_st, b_sqs, b_gb, e_sq="pool", e_add="pool"):
    fw.op("dve", lambda g: g.reduce_sum(out=st[:, 0:1], in_=t, axis=AX.X), [b_t], [b_st])
    fw.op("dve", lambda g: g.tensor_scalar(out=st[:, 1:2], in0=st[:, 0:1], scalar1=-1.0 / D, scalar2=None, op0=ALU.mult),
          [b_st], [b_st])
    fw.op("dve", lambda g: g.tensor_scalar(out=t, in0=t, scalar1=st[:, 1:2], scalar2=None, op0=ALU.add), [b_t, b_st], [b_t])
    if e_sq == "act":
        fw.op("act", lambda g: g.activation(out=sqs, in_=t, func=AF.Square), [b_t], [b_sqs])
    else:
        fw.op(e_sq, lambda g: g.tensor_tensor(out=sqs, in0=t, in1=t, op=ALU.mult), [b_t], [b_sqs])
    fw.op("dve", lambda g: g.reduce_sum(out=st[:, 2:3], in_=sqs, axis=AX.X), [b_sqs], [b_st])
    fw.op("dve", lambda g: g.tensor_scalar(out=st[:, 3:4], in0=st[:, 2:3], scalar1=1.0 / D, scalar2=LN_EPS, op0=ALU.mult,
                                           op1=ALU.add), [b_st], [b_st])
    rsqrt_inplace(fw, st[:, 3:4], b_st)
    evac(fw, "dve", st[:, 2:3], st[:, 3:4], [b_st], [b_st])
    fw.op("dve", lambda g: g.scalar_tensor_tensor(out=t, in0=t, scalar=st[:, 2:3], in1=gbc, op0=ALU.mult, op1=ALU.mult),
          [b_t, b_st, b_gb], [b_t])
    fw.op(e_add, lambda g: g.tensor_tensor(out=t, in0=t, in1=bbc, op=ALU.add), [b_t, b_gb], [b_t])


def phase_out(C):
    nc, fw, ps_tiles, ps_bufs = C.nc, C.fw, C.ps_tiles, C.ps_bufs
    with ExitStack() as ph:
        psb = lambda n, s, dt=F32: ph.enter_context(nc.sbuf_tensor(n, list(s), dt))
        wo = psb("wo", [128, 16, D], BF16)
        b_wo = fw.buf("wo")
        for j in range(16):
            fw.dma("sp", b_wo, writes=[b_wo], out=wo[:, :, j * 128:(j + 1) * 128], in_=C.wo_bf[j, :, :, :])
        g1 = psb("g1", [128, D])
        b1 = psb("b1", [128, D])
        b_gb = fw.buf("gb1")
        fw.dma("sp", b_gb, writes=[b_gb], out=g1[:], in_=C.ln1g_d[:, :])
        fw.dma("sp", b_gb, writes=[b_gb], out=b1[:], in_=C.ln1b_d[:, :])
        mt = [psb("mt%d" % i, [128, 16, 128], BF16) for i in range(2)]
        xt = [psb("xt%d" % i, [128, D]) for i in range(2)]
        tl = [psb("tl%d" % i, [128, D]) for i in range(2)]
        hTs = [psb("hTs%d" % i, [128, 16, 128], BF16) for i in range(2)]
        st = [psb("st%d" % i, [128, 4]) for i in range(2)]
        sqs = psb("sqs", [128, D])
        b_mt = [fw.buf("mt%d" % i) for i in range(2)]
        b_xt = [fw.buf("xt%d" % i) for i in range(2)]
        b_tl = [fw.buf("tl%d" % i) for i in range(2)]
        b_hTs = [fw.buf("hTs%d" % i) for i in range(2)]
        b_st = [fw.buf("lst%d" % i) for i in range(2)]
        b_sqs = fw.buf("sqs")
        for tt in range(16):
            k = tt % 2
            ts_ = slice(tt * 128, (tt + 1) * 128)
            fw.dma("sp", b_mt[k], writes=[b_mt[k]], out=mt[k][:], in_=C.mixT_d[:, :, ts_])
            fw.dma("sp", b_xt[k], writes=[b_xt[k]], out=xt[k][:], in_=C.x[ts_, :])
            banks = [C.next_ps() for _ in range(4)]
            for nb in range(4):
                for kt in range(16):
                    fw.op("pe", lambda g, nb=nb, kt=kt, k=k: g.matmul(
                        out=ps_tiles[banks[nb]][:, :], lhsT=mt[k][:, kt, :], rhs=wo[:, kt, nb * 512:(nb + 1) * 512],
                        start=(kt == 0), stop=(kt == 15)), reads=[b_mt[k], b_wo], writes=[ps_bufs[banks[nb]]])
            for nb in range(4):
                cs = slice(nb * 512, (nb + 1) * 512)
                fw.op("dve", lambda g, nb=nb, cs=cs, k=k: g.scalar_tensor_tensor(
                    out=tl[k][:, cs], in0=xt[k][:, cs], scalar=ALPHA, in1=ps_tiles[banks[nb]][:, :], op0=ALU.mult, op1=ALU.add),
                    reads=[b_xt[k], ps_bufs[banks[nb]]], writes=[b_tl[k]])
            layer_norm(fw, tl[k][:], st[k], sqs[:], g1[:], b1[:], b_tl[k], b_st[k], b_sqs, b_gb)
            fw.dma("pool", b_tl[k], reads=[b_tl[k]], out=C.h_d[ts_, :], in_=tl[k][:])
            for q4 in range(4):
                pi = C.next_ps()
                for q in range(4):
                    dt = q4 * 4 + q
                    fw.op("pe", lambda g, pi=pi, q=q, dt=dt, k=k: g.transpose(
                        out=ps_tiles[pi][:, q * 128:(q + 1) * 128], in_=tl[k][:, dt * 128:(dt + 1) * 128], identity=C.ident[:]),
                        reads=[b_tl[k], C.b_ident], writes=[ps_bufs[pi]])
                evac(fw, "act", hTs[k][:, q4 * 4:(q4 + 1) * 4, :], ps_tiles[pi][:, :].rearrange("p (a b) -> p a b", a=4),
                     [ps_bufs[pi]], [b_hTs[k]])
            fw.dma("pool", b_hTs[k], reads=[b_hTs[k]], out=C.hT_d[:, :, ts_], in_=hTs[k][:])
        fw.barrier()


def phase_peer(C):
    nc, fw, ps_tiles, ps_bufs = C.nc, C.fw, C.ps_tiles, C.ps_bufs
    NEGBIG = -1.0e30
    with ExitStack() as ph:
        psb = lambda n, s, dt=F32: ph.enter_context(nc.sbuf_tensor(n, list(s), dt))
        idx = psb("idx", [128, 16, 128], U32)
        gate = psb("gate", [128, 16, 128])
        b_idx = [fw.buf("idx%d" % i) for i in range(16)]
        b_gate = [fw.buf("gate%d" % i) for i in range(16)]
        with ExitStack() as pa:
            pab = lambda n, s, dt=F32: pa.enter_context(nc.sbuf_tensor(n, list(s), dt))
            wq = pab("wq", [128, 16, D], BF16)
            b_wq = fw.buf("wq")
            for j in range(16):
                fw.dma("sp", b_wq, writes=[b_wq], out=wq[:, :, j * 128:(j + 1) * 128], in_=C.wq_bf[j, :, :, :])
            keysf = pab("keysf", [128, 2, 128])
            keysb = pab("keysb", [128, 2, 128], BF16)
            iota = pab("iota_sb", [128, 8, 16, 16])
            iotam = pab("iotam_sb", [128, 8, 16, 16])
            eq2 = pab("eq2", [128, 8, 16, 16])
            b_keys, b_iota = fw.buf("keys"), fw.buf("iota")
            fw.dma("sp", b_keys, writes=[b_keys], out=keysf[:], in_=C.keysT_d.rearrange("p d k -> d p k"))
            fw.op("dve", lambda g: g.tensor_copy(out=keysb[:], in_=keysf[:]), [b_keys], [b_keys])
            fw.dma("sp", b_iota, writes=[b_iota], out=iota[:], in_=C.iota_d[:, :, :, :])
            fw.dma("sp", b_iota, writes=[b_iota], out=iotam[:], in_=C.iotam_d[:, :, :, :])
            hTt = [pab("hTt%d" % i, [128, 16, 128], BF16) for i in range(2)]
            b_hTt = [fw.buf("hTt%d" % i) for i in range(2)]
            qT = pab("qTp", [128, 16, 128], BF16)
            sc = pab("sc", [128, 16, 128])
            sc2 = pab("sc2", [128, 16, 128])
            sv = pab("sv", [128, 16, 16])
            si = pab("si", [128, 16, 16], U32)
            sif = pab("sif", [128, 16, 16])
            cand = pab("cand", [128, 8, 256])
            cand2 = pab("cand2", [128, 8, 256])
            tsv = pab("tsv", [128, 8, 16])
            tpos = pab("tpos", [128, 8, 16], U32)
            posf = pab("posf", [128, 8, 16])
            cf = pab("cf", [128, 8, 16])
            rf = pab("rf", [128, 8, 16])
            eq = pab("eq", [128, 8, 16, 16])
            i1 = pab("i1", [128, 8, 16])
            i2 = pab("i2", [128, 8, 16])
            ef = pab("ef", [128, 8, 16])
            esm = pab("esm", [128, 8, 16])
            ssum = pab("ssum", [128, 8])
            b_qT = [fw.buf("qTp%d" % i) for i in range(4)]
            b_sc = [fw.buf("sc%d" % i) for i in range(4)]
            b_tk = fw.buf("tk")
            for tt in range(16):
                k = tt % 2
                ts_ = slice(tt * 128, (tt + 1) * 128)
                fw.dma("sp", b_hTt[k], writes=[b_hTt[k]], out=hTt[k][:], in_=C.hT_d[:, :, ts_])
                for q4 in range(4):
                    pi = C.next_ps()
                    for q in range(4):
                        hp = q4 * 4 + q
                        for dt in range(16):
                            fw.op("pe", lambda g, pi=pi, q=q, hp=hp, dt=dt, k=k: g.matmul(
                                out=ps_tiles[pi][:, q * 128:(q + 1) * 128], lhsT=wq[:, dt, hp * 128:(hp + 1) * 128],
                                rhs=hTt[k][:, dt, :], start=(dt == 0), stop=(dt == 15)),
                                reads=[b_wq, b_hTt[k]], writes=[ps_bufs[pi]])
                    evac(fw, "act", qT[:, q4 * 4:(q4 + 1) * 4, :], ps_tiles[pi][:, :].rearrange("p (a b) -> p a b", a=4),
                         [ps_bufs[pi]], [b_qT[q4]])
                for q4 in range(4):
                    pi = C.next_ps()
                    for q in range(4):
                        hp = q4 * 4 + q
                        fw.op("pe", lambda g, pi=pi, q=q, hp=hp: g.matmul(
                            out=ps_tiles[pi][:, q * 128:(q + 1) * 128], lhsT=qT[:, hp, :], rhs=keysb[:, hp % 2, :],
                            start=True, stop=True), reads=[b_qT[q4], b_keys], writes=[ps_bufs[pi]])
                    evac(fw, "act", sc[:, q4 * 4:(q4 + 1) * 4, :], ps_tiles[pi][:, :].rearrange("p (a b) -> p a b", a=4),
                         [ps_bufs[pi]], [b_sc[q4]])
                T = [b_tk]
                for hp in range(16):
                    R_ = [b_sc[hp // 4], b_tk]
                    fw.op("dve", lambda g, hp=hp: g.max(out=sv[:, hp, 0:8], in_=sc[:, hp, :]), R_, T)
                    fw.op("dve", lambda g, hp=hp: g.max_index(out=si[:, hp, 0:8], in_max=sv[:, hp, 0:8], in_values=sc[:, hp, :]), R_, T)
                    fw.op("dve", lambda g, hp=hp: g.match_replace(out=sc2[:, hp, :], in_to_replace=sv[:, hp, 0:8],
                                                                  in_values=sc[:, hp, :], imm_value=NEGBIG), R_, T)
                    fw.op("dve", lambda g, hp=hp: g.max(out=sv[:, hp, 8:16], in_=sc2[:, hp, :]), R_, T)
                    fw.op("dve", lambda g, hp=hp: g.max_index(out=si[:, hp, 8:16], in_max=sv[:, hp, 8:16], in_values=sc2[:, hp, :]), R_, T)
                fw.op("dve", lambda g: g.tensor_copy(out=sif[:], in_=si[:]), T, T)
                sv4 = sv[:, :, :].rearrange("p (h two) r -> p h two r", two=2)
                sif4 = sif[:, :, :].rearrange("p (h two) r -> p h two r", two=2)
                cand4 = cand[:, :, :].rearrange("p h (r c) -> p h r c", c=16)
                fw.op("dve", lambda g: g.tensor_tensor(
                    out=cand4, in0=sv4[:, :, 0, :].unsqueeze(3).to_broadcast([128, 8, 16, 16]),
                    in1=sv4[:, :, 1, :].unsqueeze(2).to_broadcast([128, 8, 16, 16]), op=ALU.add), T, T)
                for hd in range(8):
                    fw.op("dve", lambda g, hd=hd: g.max(out=tsv[:, hd, 0:8], in_=cand[:, hd, :]), T, T)
                    fw.op("dve", lambda g, hd=hd: g.max_index(out=tpos[:, hd, 0:8], in_max=tsv[:, hd, 0:8], in_values=cand[:, hd, :]), T, T)
                    fw.op("dve", lambda g, hd=hd: g.match_replace(out=cand2[:, hd, :], in_to_replace=tsv[:, hd, 0:8],
                                                                  in_values=cand[:, hd, :], imm_value=NEGBIG), T, T)
                    fw.op("dve", lambda g, hd=hd: g.max(out=tsv[:, hd, 8:16], in_=cand2[:, hd, :]), T, T)
                    fw.op("dve", lambda g, hd=hd: g.max_index(out=tpos[:, hd, 8:16], in_max=tsv[:, hd, 8:16], in_values=cand2[:, hd, :]), T, T)
                fw.op("dve", lambda g: g.tensor_tensor(out=esm[:], in0=tsv[:], in1=tsv[:, :, 0:1].to_broadcast([128, 8, 16]),
                                                       op=ALU.subtract), T, T)
                fw.op("act", lambda g: g.activation(out=esm[:], in_=esm[:], func=AF.Exp), T, T)
                fw.op("dve", lambda g: g.reduce_sum(out=ssum[:], in_=esm[:], axis=AX.X), T, T)
                fw.op("dve", lambda g: g.reciprocal(out=ssum[:], in_=ssum[:]), T, T)
                fw.op("dve", lambda g, tt=tt: g.tensor_tensor(
                    out=gate[:, tt, :].rearrange("p (h r) -> p h r", r=16), in0=esm[:],
                    in1=ssum[:, :].unsqueeze(2).to_broadcast([128, 8, 16]), op=ALU.mult), T, [b_tk, b_gate[tt]])
                fw.op("dve", lambda g: g.tensor_copy(out=posf[:], in_=tpos[:]), T, T)
                fl = lambda t: t[:, :, :, :].rearrange("p h a b -> p (h a b)")
                f3 = lambda t: t[:, :, :].rearrange("p h r -> p (h r)")
                fw.op("dve", lambda g: g.tensor_tensor(
                    out=eq[:], in0=posf[:, :, :].unsqueeze(3).to_broadcast([128, 8, 16, 16]), in1=iotam[:], op=ALU.subtract),
                    T + [b_iota], T)
                fw.op("dve", lambda g: g.tensor_scalar(out=fl(eq2), in0=fl(eq), scalar1=0.0, scalar2=None, op0=ALU.is_ge), T, T)
                fw.op("dve", lambda g: g.scalar_tensor_tensor(out=fl(eq), in0=fl(eq), scalar=15.0, in1=fl(eq2), op0=ALU.is_le,
                                                              op1=ALU.mult), T, T)
                fw.op("dve", lambda g: g.tensor_tensor(out=eq2[:], in0=eq[:], in1=iota[:], op=ALU.mult), T + [b_iota], T)
                fw.op("dve", lambda g: g.reduce_sum(out=rf[:], in_=eq2[:], axis=AX.X), T, T)
                fw.op("dve", lambda g: g.tensor_tensor(
                    out=eq2[:], in0=eq[:], in1=sif4[:, :, 0, :].unsqueeze(2).to_broadcast([128, 8, 16, 16]), op=ALU.mult), T, T)
                fw.op("dve", lambda g: g.reduce_sum(out=i1[:], in_=eq2[:], axis=AX.X), T, T)
                fw.op("dve", lambda g: g.scalar_tensor_tensor(out=f3(cf), in0=f3(rf), scalar=-16.0, in1=f3(posf), op0=ALU.mult,
                                                              op1=ALU.add), T, T)
                fw.op("dve", lambda g: g.tensor_tensor(
                    out=eq[:], in0=cf[:, :, :].unsqueeze(3).to_broadcast([128, 8, 16, 16]), in1=iota[:], op=ALU.is_equal),
                    T + [b_iota], T)
                fw.op("dve", lambda g: g.tensor_tensor(
                    out=eq2[:], in0=eq[:], in1=sif4[:, :, 1, :].unsqueeze(2).to_broadcast([128, 8, 16, 16]), op=ALU.mult), T, T)
                fw.op("dve", lambda g: g.reduce_sum(out=i2[:], in_=eq2[:], axis=AX.X), T, T)
                fw.op("dve", lambda g: g.scalar_tensor_tensor(
                    out=ef[:, :, :].rearrange("p h r -> p (h r)"), in0=i1[:, :, :].rearrange("p h r -> p (h r)"), scalar=128.0,
                    in1=i2[:, :, :].rearrange("p h r -> p (h r)"), op0=ALU.mult, op1=ALU.add), T, T)
                fw.op("dve", lambda g, tt=tt: g.tensor_copy(out=idx[:, tt, :], in_=ef[:, :, :].rearrange("p h r -> p (h r)")),
                      T, [b_tk, b_idx[tt]])
            fw.barrier()

        with ExitStack() as pb:
            pbb = lambda n, s, dt=F32: pb.enter_context(nc.sbuf_tensor(n, list(s), dt))
            g2 = pbb("g2", [128, D])
            b2 = pbb("b2", [128, D])
            b_gb = fw.buf("gb2")
            fw.dma("sp", b_gb, writes=[b_gb], out=g2[:], in_=C.ln2g_d[:, :])
            fw.dma("sp", b_gb, writes=[b_gb], out=b2[:], in_=C.ln2b_d[:, :])
            NU, NV = 3, 2
            ht = [pbb("ht%d" % i, [128, D]) for i in range(2)]
            ug = [pbb("ug%d" % i, [128, D]) for i in range(NU)]
            vg = [pbb("vg%d" % i, [128, D]) for i in range(NV)]
            vgb = [pbb("vgb%d" % i, [128, D], BF16) for i in range(2)]
            dg = [pbb("dg%d" % i, [128, 128], BF16) for i in range(2)]
            junk = pbb("junk", [128, D], BF16)
            av = pbb("av", [128, 128])
            wgt = pbb("wgt", [128, 128])
            tl2 = pbb("tl2", [128, D])
            sqs = pbb("sqs2", [128, D])
            st = pbb("st2", [128, 4])
            b_ht = [fw.buf("ht%d" % i) for i in range(2)]
            b_ug = [fw.buf("ug%d" % i) for i in range(NU)]
            b_vg = [fw.buf("vg%d" % i) for i in range(NV)]
            b_vgb = [fw.buf("vgb%d" % i) for i in range(2)]
            b_dg = [fw.buf("dg%d" % i) for i in range(2)]
            b_junk, b_av, b_wgt, b_tl2, b_sqs, b_st = (fw.buf(n) for n in ("junk", "av", "wgt", "tl2", "sqs2", "st2"))
            acc_banks = [4, 5, 6, 7]
            ui = vi = 0
            for tt in range(16):
                k = tt % 2
                ts_ = slice(tt * 128, (tt + 1) * 128)
                fw.dma("sp", b_ht[k], writes=[b_ht[k]], out=ht[k][:], in_=C.h_d[ts_, :])
                for sl in range(128):
                    u = ui % NU
                    ui += 1
                    fw.dma("pool", b_ug[u], reads=[b_idx[tt]], writes=[b_ug[u]],
                           fn=lambda g, u=u, sl=sl, tt=tt: g.indirect_dma_start(
                               out=ug[u][:], out_offset=None, in_=C.peer_u[:, :],
                               in_offset=bass.IndirectOffsetOnAxis(ap=idx[:, tt, sl:sl + 1], axis=0)))
                    fw.op("dve", lambda g, u=u, sl=sl, k=k: g.scalar_tensor_tensor(
                        out=junk[:], in0=ug[u][:], scalar=1.0, in1=ht[k][:], op0=ALU.mult, op1=ALU.mult,
                        accum_out=av[:, sl:sl + 1]), reads=[b_ug[u], b_ht[k]], writes=[b_junk, b_av])
                fw.op("act", lambda g: g.activation(out=wgt[:], in_=av[:], func=AF.Gelu), [b_av], [b_wgt])
                fw.op("dve", lambda g, tt=tt: g.tensor_tensor(out=wgt[:], in0=wgt[:], in1=gate[:, tt, :], op=ALU.mult),
                      [b_wgt, b_gate[tt]], [b_wgt])
                for sl in range(128):
                    v = vi % NV
                    v2 = vi % 2
                    vi += 1
                    fw.dma("pool", b_vg[v], reads=[b_idx[tt]], writes=[b_vg[v]],
                           fn=lambda g, v=v, sl=sl, tt=tt: g.indirect_dma_start(
                               out=vg[v][:], out_offset=None, in_=C.peer_v[:, :],
                               in_offset=bass.IndirectOffsetOnAxis(ap=idx[:, tt, sl:sl + 1], axis=0)))
                    evac(fw, "act", vgb[v2][:], vg[v][:], [b_vg[v]], [b_vgb[v2]])
                    fw.op("dve", lambda g, v2=v2, sl=sl: g.tensor_scalar(
                        out=dg[v2][:], in0=C.ident[:], scalar1=wgt[:, sl:sl + 1], scalar2=None, op0=ALU.mult),
                        reads=[C.b_ident, b_wgt], writes=[b_dg[v2]])
                    for nb in range(4):
                        fw.op("pe", lambda g, nb=nb, v2=v2, sl=sl: g.matmul(
                            out=ps_tiles[acc_banks[nb]][:, :], lhsT=dg[v2][:], rhs=vgb[v2][:, nb * 512:(nb + 1) * 512],
                            start=(sl == 0), stop=(sl == 127)), reads=[b_dg[v2], b_vgb[v2]], writes=[ps_bufs[acc_banks[nb]]])
                for nb in range(4):
                    cs = slice(nb * 512, (nb + 1) * 512)
                    fw.op("dve", lambda g, nb=nb, cs=cs, k=k: g.scalar_tensor_tensor(
                        out=tl2[:, cs], in0=ht[k][:, cs], scalar=ALPHA, in1=ps_tiles[acc_banks[nb]][:, :], op0=ALU.mult,
                        op1=ALU.add), reads=[b_ht[k], ps_bufs[acc_banks[nb]]], writes=[b_tl2])
                layer_norm(fw, tl2[:], st, sqs[:], g2[:], b2[:], b_tl2, b_st, b_sqs, b_gb, e_sq="act", e_add="dve")
                fw.dma("sp", b_tl2, reads=[b_tl2], out=C.y[ts_, :], in_=tl2[:])
        fw.barrier()


def build(stop_after=None, dbg=()):
    nc = bass.Bass("TRN2", target_bir_lowering=False)
    C = Ctx()
    C.nc = nc
    dram_in = lambda n, s, dt=F32: nc.dram_tensor(n, list(s), dt, kind="ExternalInput").ap()
    C.x = dram_in("x", [SEQ, D])
    C.w_in = dram_in("w_in", [D, INW])
    C.w_out = dram_in("w_out", [D, D])
    C.peer_wq = dram_in("peer_wq", [D, D])
    C.ident_d = dram_in("ident", [128, 128])
    C.biasT_d = dram_in("biasT", [6, 128, 512])
    C.sink_d = dram_in("sink_bc", [128, 2, 512])
    C.TRI_d = dram_in("TRI", [128, 2, 128])
    C.AFT_d = dram_in("AFT", [128, 2, 128])
    C.NEGI4_d = dram_in("NEGI4", [128, 2, 512])
    C.STR4_d = dram_in("STR4", [128, 2, 512])
    C.I4_d = dram_in("I4", [128, 512])
    C.cwT_d = dram_in("cwT", [24, 128, 5])
    C.normw_d = dram_in("normw_bc", [128, 128])
    C.dtb_d = dram_in("dtb_bc", [128, 32, 16])
    C.alog_d = dram_in("alog_bc", [128, 32, 16])
    C.ln1g_d = dram_in("ln1g_bc", [128, D])
    C.ln1b_d = dram_in("ln1b_bc", [128, D])
    C.ln2g_d = dram_in("ln2g_bc", [128, D])
    C.ln2b_d = dram_in("ln2b_bc", [128, D])
    C.keysT_d = dram_in("keysT", [2, 128, 128])
    C.iota_d = dram_in("iota16", [128, 8, 16, 16])
    C.iotam_d = dram_in("iota16m", [128, 8, 16, 16])
    if stop_after is None:
        C.peer_u = dram_in("peer_u", [16384, D])
        C.peer_v = dram_in("peer_v", [16384, D])
    C.y = nc.dram_tensor("y", [OWN, D], F32, kind="ExternalOutput").ap()

    def scratch(n, s, dt=F32):
        if n in dbg:
            return nc.dram_tensor(n, list(s), dt, kind="ExternalOutput").ap()
        return nc.dram_tensor(n, list(s), dt).ap()

    C.wi_bf = scratch("wi_bf", [45, 128, 16, 128], BF16)
    C.wo_bf = scratch("wo_bf", [16, 128, 16, 128], BF16)
    C.wq_bf = scratch("wq_bf", [16, 128, 16, 128], BF16)
    C.projA = scratch("projA", [10, 128, 2560], BF16)
    C.projV = scratch("projV", [2, 128, 2560], F32)
    C.projD = scratch("projD", [32, 128, SEQ], F32)
    C.gates = scratch("gates_d", [32, SEQ], F32)
    C.mixT_d = scratch("mixT_d", [128, 16, OWN], BF16)
    C.h_d = scratch("h_d", [OWN, D], F32)
    C.hT_d = scratch("hT_d", [128, 16, OWN], BF16)

    with ExitStack() as es:
        fw = FW(nc, es)
        C.fw = fw
        C.es = es
        sb = lambda n, s, dt=F32: es.enter_context(nc.sbuf_tensor(n, list(s), dt))
        C.ps_tiles = [es.enter_context(nc.psum_tensor("ps%d" % i, [128, 512], F32)) for i in range(8)]
        C.ps_bufs = [fw.buf("ps%d" % i) for i in range(8)]
        C.ps_i = 0

        C.ps_lim = 8

        def next_ps():
            C.ps_i += 1
            return (C.ps_i - 1) % C.ps_lim
        C.next_ps = next_ps
        C.ident = sb("ident_sb", [128, 128])
        C.b_ident = fw.buf("ident")
        fw.dma("sp", C.b_ident, writes=[C.b_ident], out=C.ident[:], in_=C.ident_d[:, :])

        def finish():
            fw.barrier()
            return nc

        import os
        skip = os.environ.get("SKIP", "").split(",")
        if "PW" not in skip:
            phase_w(C)
        if stop_after == "PW":
            return finish()
        if "P1" not in skip:
            phase_proj(C)
        if stop_after == "P1":
            return finish()
        if "P2" not in skip:
            phase_attn(C)
        if stop_after == "P2":
            return finish()
        if "P3" not in skip:
            phase_dn(C)
        if stop_after == "P3":
            return finish()
        if "P4" not in skip:
            phase_out(C)
        if stop_after == "P4":
            return finish()
        C.ps_lim = 4
        phase_peer(C)
        return finish()


def _t5_bucket(rel):
    nb, me = 16, 8
    ret = np.where(rel > 0, nb, 0)
    n = np.abs(rel)
    large = me + (np.log(np.maximum(n, 1).astype(np.float32) / np.float32(me))
                  / np.float32(np.log(128.0 / me)) * np.float32(nb - me)).astype(np.int32)
    large = np.minimum(large, nb - 1)
    return ret + np.where(n < me, n, large)


def _bias_table(rel_bias, flip):
    out = np.empty((2, 3, 128, 4, 128), np.float32)
    kk = np.arange(128)[:, None]
    qq = np.arange(128)[None, :]
    for r in range(3):
        d = (r - 1) * 128 + kk - qq
        true_rel = -d if flip else d
        bk = _t5_bucket(true_rel)
        inside = np.abs(d) <= 128
        for g in range(2):
            for hh in range(4):
                out[g, r, :, hh, :] = np.where(inside, rel_bias[bk, 4 * g + hh], np.float32(NEG))
    return out.reshape(6, 128, 512)


def prep_shared(inp):
    sh = {}
    sh["ident"] = np.eye(128, dtype=np.float32)
    sh["w_out"] = np.ascontiguousarray(inp["w_out"][0])
    sh["peer_wq"] = np.ascontiguousarray(inp["peer_wq"][0])
    sink = inp["attn_sink"][0]
    sh["sink_bc"] = np.ascontiguousarray(np.broadcast_to(
        sink.reshape(1, 2, 4, 1), (128, 2, 4, 128)).reshape(128, 2, 512)).astype(np.float32)
    ii = np.arange(128)
    TRI = np.zeros((128, 2, 128), np.float32)
    AFT = np.zeros((128, 2, 128), np.float32)
    NEGI4 = np.zeros((128, 2, 512), np.float32)
    STR4 = np.zeros((128, 2, 512), np.float32)
    for d in range(2):
        prec_eq = (ii[:, None] <= ii[None, :]) if d == 0 else (ii[:, None] >= ii[None, :])
        prec = (ii[:, None] < ii[None, :]) if d == 0 else (ii[:, None] > ii[None, :])
        TRI[:, d, :] = prec_eq
        AFT[:, d, :] = prec.T
        NEGI4[:, d, :] = np.tile(np.where(prec_eq, 0.0, NEG), (1, 4))
        STR4[:, d, :] = -np.tile(prec, (1, 4)).astype(np.float32)
    sh["TRI"], sh["AFT"], sh["NEGI4"], sh["STR4"] = TRI, AFT, NEGI4, STR4
    sh["I4"] = np.tile(np.eye(128, dtype=np.float32), (1, 4))
    for nm in ("ln1_g", "ln1_b", "ln2_g", "ln2_b"):
        sh[nm.replace("_", "") + "_bc"] = np.ascontiguousarray(np.broadcast_to(inp[nm][0][None, :], (128, D))).astype(np.float32)
    sh["keysT"] = np.ascontiguousarray(inp["peer_keys"][0].transpose(0, 2, 1)).astype(np.float32)
    sh["iota16"] = np.ascontiguousarray(np.broadcast_to(np.arange(16, dtype=np.float32), (128, 8, 16, 16)))
    sh["iota16m"] = np.ascontiguousarray(np.broadcast_to(16.0 * np.arange(16, dtype=np.float32), (128, 8, 16, 16)))
    sh["peer_u"] = np.ascontiguousarray(inp["peer_u"][0])
    sh["peer_v"] = np.ascontiguousarray(inp["peer_v"][0])
    sh["normw_bc"] = np.ascontiguousarray(np.broadcast_to(inp["dn_norm_w"][0][None, :], (128, 128))).astype(np.float32)
    return sh


def prep_core(inp, c, sh):
    b, half = c // 2, c % 2
    flip = half == 1
    m = dict(sh)
    xs = inp["x"][b]
    w_in = inp["w_in"][0]
    if flip:
        xs = xs[::-1]
        perm = np.arange(INW)
        perm[5632:5640], perm[5640:5648] = np.arange(5640, 5648), np.arange(5632, 5640)
        perm[5648:5656], perm[5656:5664] = np.arange(5656, 5664), np.arange(5648, 5656)
        w_in = w_in[:, perm]
    m["x"] = np.ascontiguousarray(xs)
    m["w_in"] = np.ascontiguousarray(w_in)
    m["biasT"] = _bias_table(inp["rel_bias"], flip)
    cw = inp["conv_w"][0]
    a_log = inp["a_log"][0]
    dtb = inp["dt_bias"][0]
    if flip:
        cw = cw[::-1]
        a_log = a_log[::-1]
        dtb = dtb[::-1]
    m["cwT"] = np.ascontiguousarray(cw.T.reshape(24, 128, 5)).astype(np.float32)
    m["dtb_bc"] = np.ascontiguousarray(np.broadcast_to(dtb.reshape(1, 1, 16), (128, 32, 16))).astype(np.float32)
    m["alog_bc"] = np.ascontiguousarray(np.broadcast_to(a_log.reshape(1, 1, 16), (128, 32, 16))).astype(np.float32)
    return m


_NC_CACHE = {}


def kernel(**inputs):
    inp = {k: np.asarray(v) for k, v in inputs.items()}
    sh = prep_shared(inp)
    in_maps = [prep_core(inp, c, sh) for c in range(8)]
    if "nc" not in _NC_CACHE:
        _NC_CACHE["nc"] = build()
    nc = _NC_CACHE["nc"]
    res = run_bass_kernel_spmd(nc, in_maps, core_ids=list(range(8)))
    out = np.empty((4, SEQ, D), np.float32)
    for c in range(8):
        yc = np.asarray(res.results[c]["y"]).astype(np.float32)
        b, half = c // 2, c % 2
        if half == 0:
            out[b, :OWN] = yc
        else:
            out[b, OWN:] = yc[::-1]
    return out
```

```python
import numpy as np
from contextlib import ExitStack
import concourse.bass as bass
import concourse.mybir as mybir
from concourse.bass_utils import run_bass_kernel_spmd

F32 = mybir.dt.float32
BF16 = mybir.dt.bfloat16
U32 = mybir.dt.uint32
I32 = mybir.dt.int32
AF = mybir.ActivationFunctionType
ALU = mybir.AluOpType
AX = mybir.AxisListType

D = 2048
SEQ = 4096
OWN = 2048
INW = 5664
NEG = -30000.0
ALPHA = 2.0 ** 0.25
LN_EPS = 1e-5
RMS_EPS = 1e-6


class Buf:
    __slots__ = ("name", "w", "rs", "dsem", "dcnt")

    def __init__(self, name):
        self.name = name
        self.w = None
        self.rs = []
        self.dsem = None
        self.dcnt = 0


class FW:
    def __init__(self, nc, es):
        self.nc = nc
        self.es = es
        self.eng = {"pe": nc.tensor, "act": nc.scalar, "dve": nc.vector, "pool": nc.gpsimd, "sp": nc.sync}
        self.sem = {k: es.enter_context(nc.semaphore("s_" + k)) for k in self.eng}
        self.cnt = {k: 0 for k in self.eng}
        self.waited = {k: {} for k in self.eng}
        self.dbufs = []
        self.nbuf = 0

    def buf(self, name=None):
        self.nbuf += 1
        return Buf(name or ("b%d" % self.nbuf))

    def _resolve(self, ev):
        if ev[0] == "e":
            return ("e_" + ev[1], self.sem[ev[1]], ev[2])
        b = ev[1]
        return ("d_" + b.name, b.dsem, b.dcnt)

    def _waits(self, e, reads, writes, skip_dma_owner=None):
        evs = []
        for b in reads:
            if b.w is not None:
                evs.append(b.w)
        for b in writes:
            if b.w is not None:
                evs.append(b.w)
            evs.extend(b.rs)
        need = {}
        for ev in evs:
            if ev[0] == "e" and ev[1] == "pe" and e == "pe":
                continue
            if ev[0] == "d" and skip_dma_owner is not None and ev[1] is skip_dma_owner:
                continue
            key, sem, val = self._resolve(ev)
            if need.get(key, (None, 0))[1] < val:
                need[key] = (sem, val)
        for key, (sem, val) in need.items():
            if self.waited[e].get(key, 0) >= val:
                continue
            self.eng[e].wait_ge(sem, val)
            self.waited[e][key] = val

    def op(self, e, fn, reads=(), writes=()):
        self._waits(e, reads, writes)
        ins = fn(self.eng[e])
        self.cnt[e] += 1
        ins.then_inc(self.sem[e], 1)
        ev = ("e", e, self.cnt[e])
        for b in reads:
            b.rs.append(ev)
        for b in writes:
            b.w = ev
            b.rs = []
        return ins

    def _dsem(self, b):
        if b.dsem is None:
            b.dsem = self.es.enter_context(self.nc.semaphore("d_" + b.name))
            self.dbufs.append(b)
        return b.dsem

    def dma(self, q, owner, reads=(), writes=(), fn=None, out=None, in_=None):
        self._dsem(owner)
        self._waits(q, reads, writes, skip_dma_owner=owner)
        if fn is None:
            ins = self.eng[q].dma_start(out=out, in_=in_)
        else:
            ins = fn(self.eng[q])
        owner.dcnt += 16
        ins.then_inc(owner.dsem, 16)
        ev = ("d", owner)
        for b in reads:
            b.rs.append(ev)
        for b in writes:
            b.w = ev
            b.rs = []
        return ins

    def barrier(self):
        for e in self.eng:
            for k in self.eng:
                if k == e or self.cnt[k] == 0:
                    continue
                key = "e_" + k
                if self.waited[e].get(key, 0) < self.cnt[k]:
                    self.eng[e].wait_ge(self.sem[k], self.cnt[k])
                    self.waited[e][key] = self.cnt[k]
            for b in self.dbufs:
                key = "d_" + b.name
                if b.dcnt and self.waited[e].get(key, 0) < b.dcnt:
                    self.eng[e].wait_ge(b.dsem, b.dcnt)
                    self.waited[e][key] = b.dcnt


def rsqrt_inplace(fw, ap, b):
    fw.op("act", lambda g: g.activation(out=ap, in_=ap, func=AF.Ln), [b], [b])
    fw.op("act", lambda g: g.activation(out=ap, in_=ap, func=AF.Exp, scale=-0.5), [b], [b])


def _alt(i, engines=("act", "dve")):
    return engines[i % len(engines)]


def evac(fw, e, out, in_, reads, writes):
    if e == "act":
        return fw.op("act", lambda g: g.activation(out=out, in_=in_, func=AF.Copy), reads, writes)
    return fw.op(e, lambda g: g.tensor_copy(out=out, in_=in_), reads, writes)


class Ctx:
    pass


def phase_w(C):
    nc, fw = C.nc, C.fw
    with ExitStack() as ph:
        wst = [ph.enter_context(nc.sbuf_tensor("wst%d" % i, [128, 16, 128], F32)) for i in range(3)]
        wbf = [ph.enter_context(nc.sbuf_tensor("wbf%d" % i, [128, 16, 128], BF16)) for i in range(3)]
        b_wst = [fw.buf("wst%d" % i) for i in range(3)]
        b_wbf = [fw.buf("wbf%d" % i) for i in range(3)]
        it = 0
        for (src, dst, ntile, ncols) in ((C.w_in, C.wi_bf, 45, INW), (C.w_out, C.wo_bf, 16, D), (C.peer_wq, C.wq_bf, 16, D)):
            w_v = src.rearrange("(dt p) c -> p dt c", p=128)
            for j in range(ntile):
                k = it % 3
                nco = min(128, ncols - j * 128)
                fw.dma("sp", b_wst[k], writes=[b_wst[k]], out=wst[k][:, :, 0:nco],
                       in_=w_v[:, :, j * 128:j * 128 + nco])
                evac(fw, _alt(it, ("act", "dve", "pool")), wbf[k][:, :, 0:nco], wst[k][:, :, 0:nco],
                     [b_wst[k]], [b_wbf[k]])
                fw.dma("sp", b_wbf[k], reads=[b_wbf[k]], out=dst[j, :, :, 0:nco], in_=wbf[k][:, :, 0:nco])
                it += 1
        if C.peer_u is not None:
            ust = [ph.enter_context(nc.sbuf_tensor("ust%d" % i, [128, D], F32)) for i in range(4)]
            ubf = [ph.enter_context(nc.sbuf_tensor("ubf%d" % i, [128, D], BF16)) for i in range(4)]
            b_ust = [fw.buf("ust%d" % i) for i in range(4)]
            b_ubf = [fw.buf("ubf%d" % i) for i in range(4)]
            it = 0
            for (src, dst) in ((C.peer_u, C.u_bf), (C.peer_v, C.v_bf)):
                for r in range(128):
                    k = it % 4
                    rs = slice(r * 128, (r + 1) * 128)
                    fw.dma("sp", b_ust[k], writes=[b_ust[k]], out=ust[k][:], in_=src[rs, :])
                    evac(fw, _alt(it, ("act", "dve")), ubf[k][:], ust[k][:], [b_ust[k]], [b_ubf[k]])
                    fw.dma("pool", b_ubf[k], reads=[b_ubf[k]], out=dst[rs, :], in_=ubf[k][:])
                    it += 1
        fw.barrier()


def phase_proj(C):
    nc, fw, ps_tiles, ps_bufs = C.nc, C.fw, C.ps_tiles, C.ps_bufs
    with ExitStack() as ph:
        psb = lambda n, s, dt=F32: ph.enter_context(nc.sbuf_tensor(n, list(s), dt))
        xs = [psb("xs%d" % i, [128, 4, D]) for i in range(2)]
        b_xs = [fw.buf("xs%d" % i) for i in range(2)]
        xT = [psb("xT%d" % i, [128, 16, 512], BF16) for i in range(2)]
        b_xT = [[fw.buf("xT%d_%d" % (i, dt)) for dt in range(16)] for i in range(2)]
        NW = 3
        wt = [psb("wt%d" % i, [128, 16, 128], BF16) for i in range(NW)]
        b_wt = [fw.buf("wt%d" % i) for i in range(NW)]
        NS = 4
        stA = [psb("stA%d" % i, [128, 512], BF16) for i in range(NS)]
        stD = [psb("stD%d" % i, [128, 512], F32) for i in range(NS)]
        b_st = [fw.buf("st%d" % i) for i in range(NS)]
        x_v = C.x.rearrange("(t p) d -> p t d", p=128)
        nblk = 8
        ev_i = wq_i = st_i = 0

        def tiles_for(blk):
            if blk < 4:
                return list(range(45))
            if blk == 4:
                return list(range(8, 36)) + [44]
            return list(range(20, 36)) + [44]

        fw.dma("sp", b_xs[0], writes=[b_xs[0]], out=xs[0][:], in_=x_v[:, 0:4, :])
        for blk in range(nblk):
            k = blk % 2
            if blk + 1 < nblk:
                fw.dma("sp", b_xs[1 - k], writes=[b_xs[1 - k]], out=xs[1 - k][:],
                       in_=x_v[:, (blk + 1) * 4:(blk + 2) * 4, :])
            for dt in range(16):
                pi = C.next_ps()
                for t in range(4):
                    fw.op("pe", lambda g, t=t, dt=dt, pi=pi: g.transpose(
                        out=ps_tiles[pi][:, t * 128:(t + 1) * 128],
                        in_=xs[k][:, t, dt * 128:(dt + 1) * 128], identity=C.ident[:]),
                        reads=[b_xs[k], C.b_ident], writes=[ps_bufs[pi]])
                evac(fw, _alt(ev_i), xT[k][:, dt, :], ps_tiles[pi][:, :], [ps_bufs[pi]], [b_xT[k][dt]])
                ev_i += 1
            for j in tiles_for(blk):
                wk = wq_i % NW
                wq_i += 1
                nco = 128 if j < 44 else 32
                fw.dma("sp", b_wt[wk], writes=[b_wt[wk]], out=wt[wk][:, :, 0:nco], in_=C.wi_bf[j, :, :, 0:nco])
                pi = C.next_ps()
                for dt in range(16):
                    fw.op("pe", lambda g, dt=dt, pi=pi, wk=wk, nco=nco: g.matmul(
                        out=ps_tiles[pi][0:nco, :], lhsT=wt[wk][:, dt, 0:nco], rhs=xT[k][:, dt, :],
                        start=(dt == 0), stop=(dt == 15)),
                        reads=[b_wt[wk], b_xT[k][dt]], writes=[ps_bufs[pi]])
                si = st_i % NS
                st_i += 1
                tok = slice(blk * 512, (blk + 1) * 512)
                if j < 10:
                    evac(fw, _alt(ev_i), stA[si][:, :], ps_tiles[pi][:, :], [ps_bufs[pi]], [b_st[si]])
                    fw.dma("pool", b_st[si], reads=[b_st[si]], out=C.projA[j, :, tok], in_=stA[si][:, :])
                elif j < 44:
                    dst = C.projV[j - 10, :, tok] if j < 12 else C.projD[j - 12, :, tok]
                    evac(fw, _alt(ev_i), stD[si][:, :], ps_tiles[pi][:, :], [ps_bufs[pi]], [b_st[si]])
                    fw.dma("pool", b_st[si], reads=[b_st[si]], out=dst, in_=stD[si][:, :])
                else:
                    evac(fw, _alt(ev_i), stD[si][0:32, :], ps_tiles[pi][0:32, :], [ps_bufs[pi]], [b_st[si]])
                    fw.dma("pool", b_st[si], reads=[b_st[si]], out=C.gates[:, tok], in_=stD[si][0:32, :])
                ev_i += 1
        fw.barrier()


def phase_attn(C):
    nc, fw, ps_tiles, ps_bufs = C.nc, C.fw, C.ps_tiles, C.ps_bufs
    NKB = 17
    with ExitStack() as ph:
        psb = lambda n, s, dt=F32: ph.enter_context(nc.sbuf_tensor(n, list(s), dt))
        qT = psb("qT", [128, 8, OWN], BF16)
        kT = psb("kT", [128, 2, NKB * 128], BF16)
        vT = psb("vTf", [128, 2, NKB * 128], F32)
        V = psb("V", [128, NKB, 2, 128], BF16)
        biasT = psb("biasT_sb", [128, 6, 512], F32)
        esink = psb("esink", [128, 2, 512], F32)
        ones = psb("ones_bf", [128, 128], BF16)
        b_qT, b_kT, b_vT, b_bias, b_esink, b_ones = (fw.buf(n) for n in ("qT", "kT", "vTf", "biasT", "esink", "ones"))
        b_V = [fw.buf("V%d" % i) for i in range(NKB)]
        for h in range(8):
            fw.dma("sp", b_qT, writes=[b_qT], out=qT[:, h, :], in_=C.projA[h, :, 0:OWN])
        for g in range(2):
            fw.dma("sp", b_kT, writes=[b_kT], out=kT[:, g, :], in_=C.projA[8 + g, :, 0:NKB * 128])
            fw.dma("sp", b_vT, writes=[b_vT], out=vT[:, g, :], in_=C.projV[g, :, 0:NKB * 128])
        fw.dma("sp", b_bias, writes=[b_bias], out=biasT[:], in_=C.biasT_d.rearrange("a k c -> k a c"))
        fw.dma("sp", b_esink, writes=[b_esink], out=esink[:], in_=C.sink_d[:, :, :])
        fw.op("act", lambda g_: g_.activation(out=esink[:], in_=esink[:], func=AF.Exp), [b_esink], [b_esink])
        fw.op("dve", lambda g_: g_.memset(ones[:], 1.0), [], [b_ones])
        ev_i = 0
        for kb in range(NKB):
            pi = C.next_ps()
            for g in range(2):
                fw.op("pe", lambda g_, g=g, kb=kb, pi=pi: g_.transpose(
                    out=ps_tiles[pi][:, g * 128:(g + 1) * 128], in_=vT[:, g, kb * 128:(kb + 1) * 128],
                    identity=C.ident[:]), reads=[b_vT, C.b_ident], writes=[ps_bufs[pi]])
            evac(fw, _alt(ev_i), V[:, kb, :, :], ps_tiles[pi][:, 0:256].rearrange("p (g d) -> p g d", g=2),
                 [ps_bufs[pi]], [b_V[kb]])
            ev_i += 1
        NP = 6
        tS = [psb("tS%d" % i, [128, 512], F32) for i in range(NP)]
        pT = [psb("pT%d" % i, [128, 512], BF16) for i in range(NP)]
        b_tS = [fw.buf("tS%d" % i) for i in range(NP)]
        b_pT = [fw.buf("pT%d" % i) for i in range(NP)]
        den = [psb("den%d" % i, [128, 512], F32) for i in range(2)]
        ost = [psb("ost%d" % i, [128, 512], BF16) for i in range(2)]
        b_ost = [fw.buf("ost%d" % i) for i in range(2)]
        b_den = [fw.buf("den%d" % i) for i in range(2)]
        p_i = 0
        scale = 128.0 ** -0.5
        for i in range(16):
            for g in range(2):
                kbs = [kb for kb in (i - 1, i, i + 1) if kb >= 0]
                slots = []
                for kb in kbs:
                    pi = C.next_ps()
                    for hh in range(4):
                        fw.op("pe", lambda g_, g=g, kb=kb, pi=pi, hh=hh, i=i: g_.matmul(
                            out=ps_tiles[pi][:, hh * 128:(hh + 1) * 128], lhsT=kT[:, g, kb * 128:(kb + 1) * 128],
                            rhs=qT[:, 4 * g + hh, i * 128:(i + 1) * 128], start=True, stop=True),
                            reads=[b_kT, b_qT], writes=[ps_bufs[pi]])
                    sl = p_i % NP
                    p_i += 1
                    rel = kb - i + 1
                    fw.op("dve", lambda g_, pi=pi, sl=sl, g=g, rel=rel: g_.scalar_tensor_tensor(
                        out=tS[sl][:], in0=ps_tiles[pi][:, :], scalar=scale, in1=biasT[:, g * 3 + rel, :],
                        op0=ALU.mult, op1=ALU.add), reads=[ps_bufs[pi], b_bias], writes=[b_tS[sl]])
                    fw.op("act", lambda g_, sl=sl: g_.activation(out=pT[sl][:], in_=tS[sl][:], func=AF.Exp),
                          reads=[b_tS[sl]], writes=[b_pT[sl]])
                    slots.append(sl)
                po = C.next_ps()
                pd = C.next_ps()
                for hh in range(4):
                    for n, (kb, sl) in enumerate(zip(kbs, slots)):
                        fw.op("pe", lambda g_, g=g, kb=kb, sl=sl, hh=hh, n=n, po=po: g_.matmul(
                            out=ps_tiles[po][:, hh * 128:(hh + 1) * 128], lhsT=V[:, kb, g, :],
                            rhs=pT[sl][:, hh * 128:(hh + 1) * 128], start=(n == 0), stop=(n == len(kbs) - 1)),
                            reads=[b_V[kb], b_pT[sl]], writes=[ps_bufs[po]])
                for n, sl in enumerate(slots):
                    fw.op("pe", lambda g_, sl=sl, n=n, pd=pd: g_.matmul(
                        out=ps_tiles[pd][:, :], lhsT=ones[:], rhs=pT[sl][:, :], start=(n == 0),
                        stop=(n == len(slots) - 1)), reads=[b_ones, b_pT[sl]], writes=[ps_bufs[pd]])
                dk = (i * 2 + g) % 2
                fw.op("dve", lambda g_, pd=pd, dk=dk, g=g: g_.tensor_tensor(
                    out=den[dk][:], in0=ps_tiles[pd][:, :], in1=esink[:, g, :], op=ALU.add),
                    reads=[ps_bufs[pd], b_esink], writes=[b_den[dk]])
                fw.op("dve", lambda g_, dk=dk: g_.reciprocal(out=den[dk][:], in_=den[dk][:]),
                      reads=[b_den[dk]], writes=[b_den[dk]])
                fw.op("dve", lambda g_, po=po, dk=dk: g_.tensor_tensor(
                    out=ost[dk][:, :], in0=ps_tiles[po][:, :], in1=den[dk][:, :], op=ALU.mult),
                    reads=[ps_bufs[po], b_den[dk]], writes=[b_ost[dk]])
                fw.dma("pool", b_ost[dk], reads=[b_ost[dk]],
                       out=C.mixT_d[:, 4 * g:4 * g + 4, i * 128:(i + 1) * 128],
                       in_=ost[dk][:, :].rearrange("p (h q) -> p h q", h=4))
        fw.barrier()


def phase_dn(C):
    nc, fw, ps_tiles, ps_bufs = C.nc, C.fw, C.ps_tiles, C.ps_bufs
    ident = C.ident
    with ExitStack() as ph:
        psb = lambda n, s, dt=F32: ph.enter_context(nc.sbuf_tensor(n, list(s), dt))
        TRI = psb("TRI_sb", [128, 2, 128])
        AFT = psb("AFT_sb", [128, 2, 128])
        NEGI4 = psb("NEGI4_sb", [128, 2, 512])
        STR4 = psb("STR4_sb", [128, 2, 512])
        I4 = psb("I4_sb", [128, 512])
        cw = psb("cw_sb", [128, 24, 5])
        normw = psb("normw_sb", [128, 128])
        ones_f = psb("ones_f", [128, 128])
        ones_b = psb("ones_b2", [128, 128], BF16)
        b_const = fw.buf("dnconst")
        for (t, src) in ((TRI, C.TRI_d), (AFT, C.AFT_d), (NEGI4, C.NEGI4_d), (STR4, C.STR4_d)):
            fw.dma("sp", b_const, writes=[b_const], out=t[:], in_=src[:, :, :])
        fw.dma("sp", b_const, writes=[b_const], out=I4[:], in_=C.I4_d[:, :])
        fw.dma("sp", b_const, writes=[b_const], out=cw[:], in_=C.cwT_d.rearrange("t p j -> p t j"))
        fw.dma("sp", b_const, writes=[b_const], out=normw[:], in_=C.normw_d[:, :])
        b_ones = fw.buf("dnones")
        fw.op("dve", lambda g: g.memset(ones_f[:], 1.0), [], [b_ones])
        fw.op("dve", lambda g: g.memset(ones_b[:], 1.0), [b_ones], [b_ones])

        beta = psb("beta", [128, 32, 16])
        graw = psb("graw", [128, 32, 16])
        sc_eg = psb("sc_eg", [128, 48, 8])
        sc_negeg = psb("sc_negeg", [128, 48, 8])
        sc_negg = psb("sc_negg", [128, 48, 8])
        sc_ekd = psb("sc_ekd", [128, 48, 8])
        b_beta, b_graw, b_sc = fw.buf("beta"), fw.buf("graw"), fw.buf("sc")
        with ExitStack() as pg:
            gsb = pg.enter_context(nc.sbuf_tensor("gsb", [32, SEQ], F32))
            Gtok = pg.enter_context(nc.sbuf_tensor("Gtok", [128, 32, 32], F32))
            dtb = pg.enter_context(nc.sbuf_tensor("dtb_sb", [128, 32, 16], F32))
            alg = pg.enter_context(nc.sbuf_tensor("alog_sb", [128, 32, 16], F32))
            b_gsb, b_Gtok, b_dtb, b_alg = fw.buf("gsb"), fw.buf("Gtok"), fw.buf("dtb"), fw.buf("alg")
            fw.dma("sp", b_gsb, writes=[b_gsb], out=gsb[:], in_=C.gates[:, :])
            fw.dma("sp", b_dtb, writes=[b_dtb], out=dtb[:], in_=C.dtb_d[:, :, :])
            fw.dma("sp", b_alg, writes=[b_alg], out=alg[:], in_=C.alog_d[:, :, :])
            for half in range(2):
                pi = C.next_ps()
                for cc in range(16):
                    c = half * 16 + cc
                    fw.op("pe", lambda g, pi=pi, cc=cc, c=c: g.transpose(
                        out=ps_tiles[pi][:, cc * 32:(cc + 1) * 32], in_=gsb[0:32, c * 128:(c + 1) * 128],
                        identity=ident[0:32, 0:32]), reads=[b_gsb, C.b_ident], writes=[ps_bufs[pi]])
                evac(fw, "dve", Gtok[:, half * 16:(half + 1) * 16, :],
                     ps_tiles[pi][:, :].rearrange("p (c k) -> p c k", k=32), [ps_bufs[pi]], [b_Gtok])
            fw.op("act", lambda g: g.activation(out=beta[:], in_=Gtok[:, :, 0:16], func=AF.Sigmoid), [b_Gtok], [b_beta])
            fw.op("dve", lambda g: g.tensor_tensor(out=graw[:], in0=Gtok[:, :, 16:32], in1=dtb[:], op=ALU.add),
                  [b_Gtok, b_dtb], [b_graw])
            fw.op("act", lambda g: g.activation(out=graw[:], in_=graw[:], func=AF.Exp), [b_graw], [b_graw])
            fw.op("dve", lambda g: g.tensor_scalar(out=graw[:], in0=graw[:], scalar1=1.0, scalar2=None, op0=ALU.add),
                  [b_graw], [b_graw])
            fw.op("act", lambda g: g.activation(out=graw[:], in_=graw[:], func=AF.Ln), [b_graw], [b_graw])
            fw.op("act", lambda g: g.activation(out=alg[:], in_=alg[:], func=AF.Exp), [b_alg], [b_alg])
            fw.op("dve", lambda g: g.scalar_tensor_tensor(out=graw[:], in0=graw[:], scalar=-1.0, in1=alg[:],
                                                          op0=ALU.mult, op1=ALU.mult), [b_graw, b_alg], [b_graw])
            pg_, pa_ = C.next_ps(), C.next_ps()
            for d in range(2):
                for c in range(16 if d == 0 else 32):
                    dc = c if d == 0 else 16 + c
                    fw.op("pe", lambda g, d=d, c=c, dc=dc: g.matmul(
                        out=ps_tiles[pg_][:, dc * 8:(dc + 1) * 8], lhsT=TRI[:, d, :], rhs=graw[:, c, d * 8:(d + 1) * 8],
                        start=True, stop=True), reads=[b_const, b_graw], writes=[ps_bufs[pg_]])
                    fw.op("pe", lambda g, d=d, c=c, dc=dc: g.matmul(
                        out=ps_tiles[pa_][:, dc * 8:(dc + 1) * 8], lhsT=AFT[:, d, :], rhs=graw[:, c, d * 8:(d + 1) * 8],
                        start=True, stop=True), reads=[b_const, b_graw], writes=[ps_bufs[pa_]])
            v3 = lambda t: t[:, :, :].rearrange("p a b -> p (a b)")
            fw.op("act", lambda g: g.activation(out=v3(sc_eg), in_=ps_tiles[pg_][:, 0:384], func=AF.Exp),
                  [ps_bufs[pg_]], [b_sc])
            fw.op("dve", lambda g: g.tensor_scalar(out=v3(sc_negg), in0=ps_tiles[pg_][:, 0:384], scalar1=-1.0,
                                                   scalar2=None, op0=ALU.mult), [ps_bufs[pg_]], [b_sc])
            fw.op("dve", lambda g: g.tensor_scalar(out=v3(sc_negeg), in0=v3(sc_eg), scalar1=-1.0, scalar2=None,
                                                   op0=ALU.mult), [b_sc], [b_sc])
            fw.op("act", lambda g: g.activation(out=v3(sc_ekd), in_=ps_tiles[pa_][:, 0:384], func=AF.Exp),
                  [ps_bufs[pa_]], [b_sc])
            fw.barrier()

        pad = psb("pad", [128, SEQ + 4])
        acc = psb("acc", [128, SEQ])
        y32 = psb("y32", [128, SEQ])
        b_pad = fw.buf("pad")
        b_acc = [fw.buf("acc0"), fw.buf("acc1")]
        b_y = [fw.buf("y%d" % i) for i in range(8)]
        sq = [psb("sq%d" % i, [128, 512], BF16) for i in range(2)]
        rn = [psb("rn%d" % i, [128, 512]) for i in range(2)]
        b_sq = [fw.buf("sq%d" % i) for i in range(2)]
        b_rn = [fw.buf("rn%d" % i) for i in range(2)]
        QT = psb("QT", [128, OWN], BF16)
        KT = psb("KT", [128, SEQ], BF16)
        Vtok = psb("Vtok", [128, 32, 128], BF16)
        kd = psb("kd", [128, 48, 128], BF16)
        X = psb("Xinv", [128, 48, 128], BF16)
        AT = psb("AT", [128, 32, 128], BF16)
        qgT = psb("qgT", [128, 32, 128], BF16)
        glb = psb("glb", [128, 48])
        zs = psb("zs", [128, OWN])
        obwd = psb("obwd", [128, 16, 128])
        b_QT = [fw.buf("QT%d" % i) for i in range(4)]
        b_KT = [fw.buf("KT%d" % i) for i in range(8)]
        b_Vtok = [fw.buf("Vtok%d" % i) for i in range(8)]
        b_kd = [fw.buf("kd%d" % i) for i in range(48)]
        b_X = [fw.buf("X%d" % i) for i in range(12)]
        b_AT = [fw.buf("AT%d" % i) for i in range(8)]
        b_qgT = [fw.buf("qgT%d" % i) for i in range(8)]
        b_glb = [fw.buf("glb%d" % i) for i in range(12)]
        b_zs = fw.buf("zs")
        b_obwd = [fw.buf("obwd%d" % i) for i in range(16)]
        GB = psb("GB", [128, 512])
        tmpg = psb("tmpg", [128, 512])
        decT = psb("decT", [128, 512])
        EGR = psb("EGR", [128, 512])
        m1 = psb("m1", [128, 512])
        Pm = [psb("Pm%d" % i, [128, 512]) for i in range(2)]
        PT = [psb("PT%d" % i, [128, 512]) for i in range(2)]
        Rn = [psb("Rn%d" % i, [128, 512]) for i in range(2)]
        b_GB, b_tmpg, b_decT, b_EGR, b_m1 = (fw.buf(n) for n in ("GB", "tmpg", "decT", "EGR", "m1"))
        b_Pm = [fw.buf("Pm%d" % i) for i in range(2)]
        b_PT = [fw.buf("PT%d" % i) for i in range(2)]
        b_Rn = [fw.buf("Rn%d" % i) for i in range(2)]
        S32 = psb("S32", [128, 128])
        S16 = psb("S16", [128, 128], BF16)
        b_S32, b_S16 = fw.buf("S32"), fw.buf("S16")
        Rt = [psb("Rt%d" % i, [128, 128], BF16) for i in range(2)]
        vn = [psb("vn%d" % i, [128, 128], BF16) for i in range(2)]
        b_Rt = [fw.buf("Rt%d" % i) for i in range(2)]
        b_vn = [fw.buf("vn%d" % i) for i in range(2)]
        ot = [psb("ot%d" % i, [128, 128]) for i in range(2)]
        osq = [psb("osq%d" % i, [128, 128]) for i in range(2)]
        orr = [psb("orr%d" % i, [128, 2]) for i in range(2)]
        on_ = [psb("on%d" % i, [128, 128]) for i in range(2)]
        mo = [psb("mo%d" % i, [128, 128], BF16) for i in range(2)]
        b_ot = [fw.buf("ot%d" % i) for i in range(2)]
        b_on = [fw.buf("on%d" % i) for i in range(2)]
        b_mo = [fw.buf("mo%d" % i) for i in range(2)]
        fw.op("pool", lambda g: g.memset(pad[:, 0:2], 0.0), [], [b_pad])
        fw.op("pool", lambda g: g.memset(pad[:, SEQ + 2:SEQ + 4], 0.0), [b_pad], [b_pad])
        ev = [0]

        def conv_silu(tile_idx, ct, nload, n):
            fw.dma("sp", b_pad, writes=[b_pad], out=pad[:, 2:2 + nload], in_=C.projD[tile_idx, :, 0:nload])
            hlf = n // 2
            for (lo, hi), e, ba in (((0, hlf), "dve", b_acc[0]), ((hlf, n), "dve", b_acc[1])):
                fw.op(e, lambda g, lo=lo, hi=hi: g.tensor_scalar(
                    out=acc[:, lo:hi], in0=pad[:, lo:hi], scalar1=cw[:, ct, 0:1], scalar2=None, op0=ALU.mult),
                    reads=[b_pad, b_const], writes=[ba])
                for j in range(1, 5):
                    fw.op(e, lambda g, lo=lo, hi=hi, j=j: g.scalar_tensor_tensor(
                        out=acc[:, lo:hi], in0=pad[:, lo + j:hi + j], scalar=cw[:, ct, j:j + 1], in1=acc[:, lo:hi],
                        op0=ALU.mult, op1=ALU.add), reads=[b_pad, b_const, ba], writes=[ba])
            fw.op("act", lambda g: g.activation(out=y32[:, 0:n], in_=acc[:, 0:n], func=AF.Silu),
                  reads=b_acc, writes=b_y[0:n // 512])

        def l2norm(n, is_q):
            for blk in range(n // 512):
                cs = slice(blk * 512, (blk + 1) * 512)
                k = blk % 2
                fw.op("act", lambda g, cs=cs, k=k: g.activation(out=sq[k][:], in_=y32[:, cs], func=AF.Square),
                      reads=[b_y[blk]], writes=[b_sq[k]])
                pi = C.next_ps()
                fw.op("pe", lambda g, pi=pi, k=k: g.matmul(out=ps_tiles[pi][:, :], lhsT=ones_b[:], rhs=sq[k][:],
                                                          start=True, stop=True), reads=[b_ones, b_sq[k]], writes=[ps_bufs[pi]])
                fw.op("dve", lambda g, pi=pi, k=k: g.tensor_scalar(
                    out=rn[k][:], in0=ps_tiles[pi][:, :], scalar1=RMS_EPS, scalar2=None, op0=ALU.add),
                    reads=[ps_bufs[pi]], writes=[b_rn[k]])
                rsqrt_inplace(fw, rn[k][:], b_rn[k])
                if is_q:
                    fw.op("dve", lambda g, cs=cs, k=k: g.scalar_tensor_tensor(
                        out=QT[:, cs], in0=y32[:, cs], scalar=128.0 ** -0.5, in1=rn[k][:], op0=ALU.mult, op1=ALU.mult),
                        reads=[b_y[blk], b_rn[k]], writes=[b_QT[blk]])
                else:
                    fw.op("dve", lambda g, cs=cs, k=k: g.tensor_tensor(out=y32[:, cs], in0=y32[:, cs], in1=rn[k][:],
                                                                      op=ALU.mult), reads=[b_y[blk], b_rn[k]], writes=[b_y[blk]])
                    fw.op("act", lambda g, cs=cs: g.activation(out=KT[:, cs], in_=y32[:, cs], func=AF.Copy),
                          reads=[b_y[blk]], writes=[b_KT[blk]])

        import os
        _nh = int(os.environ.get('DN_HEADS', '8'))
        _sub = os.environ.get('DN_SUB', '123')
        for h in range(_nh):
            fw.dma("sp", b_zs, writes=[b_zs], out=zs[:], in_=C.projD[24 + h, :, 0:OWN])
            fw.op("act", lambda g: g.activation(out=zs[:], in_=zs[:], func=AF.Silu), [b_zs], [b_zs])
            conv_silu(h, h, OWN + 2, OWN)
            l2norm(OWN, True)
            conv_silu(8 + h, 8 + h, SEQ, SEQ)
            l2norm(SEQ, False)
            for c0 in range(0, 32, 4):
                pi = C.next_ps()
                for q in range(4):
                    c = c0 + q
                    fw.op("pe", lambda g, pi=pi, q=q, c=c: g.transpose(
                        out=ps_tiles[pi][:, q * 128:(q + 1) * 128], in_=y32[:, c * 128:(c + 1) * 128], identity=ident[:]),
                        reads=[b_y[c // 4], C.b_ident], writes=[ps_bufs[pi]])
                for q in range(4):
                    c = c0 + q
                    fw.op("dve", lambda g, pi=pi, q=q, c=c: g.tensor_scalar(
                        out=kd[:, 16 + c, :], in0=ps_tiles[pi][:, q * 128:(q + 1) * 128], scalar1=sc_ekd[:, 16 + c, h:h + 1],
                        scalar2=None, op0=ALU.mult), reads=[ps_bufs[pi], b_sc], writes=[b_kd[16 + c]])
                    if c < 16:
                        fw.op("dve", lambda g, pi=pi, q=q, c=c: g.tensor_scalar(
                            out=kd[:, c, :], in0=ps_tiles[pi][:, q * 128:(q + 1) * 128], scalar1=sc_ekd[:, c, h:h + 1],
                            scalar2=None, op0=ALU.mult), reads=[ps_bufs[pi], b_sc], writes=[b_kd[c]])
            conv_silu(16 + h, 16 + h, SEQ, SEQ)
            for c0 in range(0, 32, 4):
                pi = C.next_ps()
                for q in range(4):
                    c = c0 + q
                    fw.op("pe", lambda g, pi=pi, q=q, c=c: g.transpose(
                        out=ps_tiles[pi][:, q * 128:(q + 1) * 128], in_=y32[:, c * 128:(c + 1) * 128], identity=ident[:]),
                        reads=[b_y[c // 4], C.b_ident], writes=[ps_bufs[pi]])
                evac(fw, _alt(ev[0]), Vtok[:, c0:c0 + 4, :], ps_tiles[pi][:, :].rearrange("p (c d) -> p c d", c=4),
                     [ps_bufs[pi]], [b_Vtok[c0 // 4]])
                ev[0] += 1

            for d in ((1, 0) if '2' in _sub else ()):
                for c0 in range(0, 32 if d == 1 else 16, 4):
                    dc0 = c0 if d == 0 else 16 + c0
                    has_out = c0 < 16
                    oc0 = c0 if d == 0 else 16 + c0
                    col = d * 8 + h
                    last = 127 if d == 0 else 0
                    pG = C.next_ps()
                    for q in range(4):
                        c = c0 + q
                        fw.op("pool", lambda g, q=q, c=c: g.tensor_scalar(
                            out=GB[:, q * 128:(q + 1) * 128], in0=ones_f[:], scalar1=graw[:, c, col:col + 1], scalar2=None,
                            op0=ALU.mult), reads=[b_ones, b_graw], writes=[b_GB])
                    for q in range(4):
                        fw.op("pe", lambda g, q=q, pG=pG: g.matmul(
                            out=ps_tiles[pG][:, q * 128:(q + 1) * 128], lhsT=GB[:, q * 128:(q + 1) * 128], rhs=TRI[:, d, :],
                            start=True, stop=True), reads=[b_GB, b_const], writes=[ps_bufs[pG]])
                    fw.op("dve", lambda g, pG=pG: g.tensor_tensor(out=tmpg[:], in0=ps_tiles[pG][:, :], in1=NEGI4[:, d, :],
                                                                 op=ALU.add), reads=[ps_bufs[pG], b_const], writes=[b_tmpg])
                    for q in range(4):
                        fw.op("act", lambda g, q=q: g.activation(
                            out=decT[:, q * 128:(q + 1) * 128], in_=tmpg[:, q * 128:(q + 1) * 128], func=AF.Exp,
                            bias=sc_negg[:, dc0 + q, h:h + 1]), reads=[b_tmpg, b_sc], writes=[b_decT])
                    fw.op("act", lambda g, pG=pG: g.activation(out=EGR[:], in_=ps_tiles[pG][:, :], func=AF.Exp),
                          reads=[ps_bufs[pG]], writes=[b_EGR])
                    fw.op("pool", lambda g: g.tensor_copy(
                        out=glb[:, dc0:dc0 + 4], in_=EGR[:, :].rearrange("p (c i) -> p c i", c=4)[:, :, last]),
                        reads=[b_EGR], writes=[b_glb[dc0 // 4]])
                    pK = C.next_ps()
                    for q in range(4):
                        c = c0 + q
                        fw.op("pe", lambda g, q=q, c=c, pK=pK: g.matmul(
                            out=ps_tiles[pK][:, q * 128:(q + 1) * 128], lhsT=KT[:, c * 128:(c + 1) * 128],
                            rhs=KT[:, c * 128:(c + 1) * 128], start=True, stop=True),
                            reads=[b_KT[c // 4]], writes=[ps_bufs[pK]])
                    for q in range(4):
                        c = c0 + q
                        fw.op("dve", lambda g, q=q, c=c, pK=pK: g.scalar_tensor_tensor(
                            out=m1[:, q * 128:(q + 1) * 128], in0=ps_tiles[pK][:, q * 128:(q + 1) * 128],
                            scalar=beta[:, c, col:col + 1], in1=decT[:, q * 128:(q + 1) * 128], op0=ALU.mult, op1=ALU.mult),
                            reads=[ps_bufs[pK], b_beta, b_decT], writes=[b_m1])
                    fw.op("pool", lambda g: g.tensor_tensor(out=Pm[0][:], in0=m1[:], in1=STR4[:, d, :], op=ALU.mult),
                          reads=[b_m1, b_const], writes=[b_Pm[0]])
                    if has_out:
                        pQ = C.next_ps()
                        for q in range(4):
                            c = c0 + q
                            fw.op("pe", lambda g, q=q, c=c, pQ=pQ: g.matmul(
                                out=ps_tiles[pQ][:, q * 128:(q + 1) * 128], lhsT=KT[:, c * 128:(c + 1) * 128],
                                rhs=QT[:, c * 128:(c + 1) * 128], start=True, stop=True),
                                reads=[b_KT[c // 4], b_QT[c // 4]], writes=[ps_bufs[pQ]])
                        fw.op("dve", lambda g, pQ=pQ: g.tensor_tensor(
                            out=AT[:, oc0:oc0 + 4, :], in0=ps_tiles[pQ][:, :].rearrange("p (c i) -> p c i", c=4),
                            in1=decT[:, :].rearrange("p (c i) -> p c i", c=4), op=ALU.mult),
                            reads=[ps_bufs[pQ], b_decT], writes=[b_AT[oc0 // 4]])
                        fw.op("pool", lambda g: g.tensor_tensor(
                            out=qgT[:, oc0:oc0 + 4, :], in0=QT[:, c0 * 128:(c0 + 4) * 128].rearrange("p (c i) -> p c i", c=4),
                            in1=EGR[:, :].rearrange("p (c i) -> p c i", c=4), op=ALU.mult),
                            reads=[b_QT[c0 // 4], b_EGR], writes=[b_qgT[oc0 // 4]])
                    pT = C.next_ps()
                    for q in range(4):
                        fw.op("pe", lambda g, q=q, pT=pT: g.transpose(
                            out=ps_tiles[pT][:, q * 128:(q + 1) * 128], in_=Pm[0][:, q * 128:(q + 1) * 128], identity=ident[:]),
                            reads=[b_Pm[0], C.b_ident], writes=[ps_bufs[pT]])
                    evac(fw, "act", PT[0][:], ps_tiles[pT][:, :], [ps_bufs[pT]], [b_PT[0]])
                    fw.op("pool", lambda g: g.tensor_tensor(out=Rn[0][:], in0=Pm[0][:], in1=I4[:], op=ALU.add),
                          reads=[b_Pm[0], b_const], writes=[b_Rn[0]])
                    cur = 0
                    for lvl in range(1, 7):
                        nxt = 1 - cur
                        if lvl < 6:
                            pA = C.next_ps()
                            for q in range(4):
                                qs = slice(q * 128, (q + 1) * 128)
                                fw.op("pe", lambda g, qs=qs, pA=pA, cur=cur: g.matmul(
                                    out=ps_tiles[pA][:, qs], lhsT=PT[cur][:, qs], rhs=Pm[cur][:, qs], start=True, stop=True),
                                    reads=[b_PT[cur], b_Pm[cur]], writes=[ps_bufs[pA]])
                        pB = C.next_ps()
                        for q in range(4):
                            qs = slice(q * 128, (q + 1) * 128)
                            fw.op("pe", lambda g, qs=qs, pB=pB, cur=cur: g.matmul(
                                out=ps_tiles[pB][:, qs], lhsT=Pm[cur][:, qs], rhs=PT[cur][:, qs], start=True, stop=True),
                                reads=[b_PT[cur], b_Pm[cur]], writes=[ps_bufs[pB]])
                        evac(fw, "act", PT[nxt][:], ps_tiles[pB][:, :], [ps_bufs[pB]], [b_PT[nxt]])
                        if lvl < 6:
                            evac(fw, "dve", Pm[nxt][:], ps_tiles[pA][:, :], [ps_bufs[pA]], [b_Pm[nxt]])
                        pC = C.next_ps()
                        for q in range(4):
                            qs = slice(q * 128, (q + 1) * 128)
                            fw.op("pe", lambda g, qs=qs, pC=pC, cur=cur, nxt=nxt: g.matmul(
                                out=ps_tiles[pC][:, qs], lhsT=PT[nxt][:, qs], rhs=Rn[cur][:, qs], start=True, stop=True),
                                reads=[b_PT[nxt], b_Rn[cur]], writes=[ps_bufs[pC]])
                        if lvl < 6:
                            fw.op("dve", lambda g, pC=pC, cur=cur, nxt=nxt: g.tensor_tensor(
                                out=Rn[nxt][:], in0=ps_tiles[pC][:, :], in1=Rn[cur][:], op=ALU.add),
                                reads=[ps_bufs[pC], b_Rn[cur]], writes=[b_Rn[nxt]])
                        else:
                            fw.op("dve", lambda g, pC=pC, cur=cur: g.tensor_tensor(
                                out=X[:, dc0:dc0 + 4, :], in0=ps_tiles[pC][:, :].rearrange("p (c i) -> p c i", c=4),
                                in1=Rn[cur][:, :].rearrange("p (c i) -> p c i", c=4), op=ALU.add),
                                reads=[ps_bufs[pC], b_Rn[cur]], writes=[b_X[dc0 // 4]])
                        cur = nxt

            it = 0
            for d in ((1, 0) if '3' in _sub else ()):
                col = d * 8 + h
                fw.op("pool", lambda g: g.memset(S32[:], 0.0), [], [b_S32])
                fw.op("pool", lambda g: g.memset(S16[:], 0.0), [], [b_S16])
                order = list(range(31, -1, -1)) if d == 1 else list(range(16))
                for c in order:
                    dc = c if d == 0 else 16 + c
                    oc = c if d == 0 else 16 + c
                    k = it % 2
                    it += 1
                    cs = slice(c * 128, (c + 1) * 128)
                    p1 = C.next_ps()
                    fw.op("pe", lambda g, p1=p1, cs=cs: g.matmul(out=ps_tiles[p1][:, 0:128], lhsT=KT[:, cs], rhs=S16[:],
                                                                 start=True, stop=True),
                          reads=[b_KT[c // 4], b_S16], writes=[ps_bufs[p1]])
                    fw.op("dve", lambda g, p1=p1, k=k, c=c, dc=dc: g.scalar_tensor_tensor(
                        out=Rt[k][:], in0=ps_tiles[p1][:, 0:128], scalar=sc_negeg[:, dc, h:h + 1], in1=Vtok[:, c, :],
                        op0=ALU.mult, op1=ALU.add), reads=[ps_bufs[p1], b_sc, b_Vtok[c // 4]], writes=[b_Rt[k]])
                    p2 = C.next_ps()
                    fw.op("pe", lambda g, p2=p2, k=k, dc=dc: g.matmul(out=ps_tiles[p2][:, 0:128], lhsT=X[:, dc, :], rhs=Rt[k][:],
                                                                      start=True, stop=True),
                          reads=[b_X[dc // 4], b_Rt[k]], writes=[ps_bufs[p2]])
                    fw.op("dve", lambda g, p2=p2, k=k, c=c: g.tensor_scalar(
                        out=vn[k][:], in0=ps_tiles[p2][:, 0:128], scalar1=beta[:, c, col:col + 1], scalar2=None,
                        op0=ALU.mult), reads=[ps_bufs[p2], b_beta], writes=[b_vn[k]])
                    if c < 16:
                        p3 = C.next_ps()
                        fw.op("pe", lambda g, p3=p3, oc=oc: g.matmul(out=ps_tiles[p3][:, 0:128], lhsT=qgT[:, oc, :], rhs=S16[:],
                                                                     start=True, stop=False),
                              reads=[b_qgT[oc // 4], b_S16], writes=[ps_bufs[p3]])
                        fw.op("pe", lambda g, p3=p3, oc=oc, k=k: g.matmul(out=ps_tiles[p3][:, 0:128], lhsT=AT[:, oc, :], rhs=vn[k][:],
                                                                          start=False, stop=True),
                              reads=[b_AT[oc // 4], b_vn[k]], writes=[ps_bufs[p3]])
                        if d == 1:
                            evac(fw, "act", obwd[:, c, :], ps_tiles[p3][:, 0:128], [ps_bufs[p3]], [b_obwd[c]])
                        else:
                            fw.op("dve", lambda g, p3=p3, k=k, c=c: g.tensor_tensor(
                                out=ot[k][:], in0=ps_tiles[p3][:, 0:128], in1=obwd[:, c, :], op=ALU.add),
                                reads=[ps_bufs[p3], b_obwd[c]], writes=[b_ot[k]])
                            fw.op("pool", lambda g, k=k: g.tensor_tensor(out=osq[k][:], in0=ot[k][:], in1=ot[k][:], op=ALU.mult),
                                  reads=[b_ot[k]], writes=[b_on[k]])
                            fw.op("dve", lambda g, k=k: g.reduce_sum(out=orr[k][:, 0:1], in_=osq[k][:], axis=AX.X),
                                  reads=[b_on[k]], writes=[b_on[k]])
                            fw.op("dve", lambda g, k=k: g.tensor_scalar(out=orr[k][:, 1:2], in0=orr[k][:, 0:1], scalar1=1.0 / 128,
                                                                        scalar2=RMS_EPS, op0=ALU.mult, op1=ALU.add),
                                  reads=[b_on[k]], writes=[b_on[k]])
                            rsqrt_inplace(fw, orr[k][:, 1:2], b_on[k])
                            evac(fw, "dve", orr[k][:, 0:1], orr[k][:, 1:2], [b_on[k]], [b_on[k]])
                            fw.op("dve", lambda g, k=k: g.scalar_tensor_tensor(
                                out=on_[k][:], in0=ot[k][:], scalar=orr[k][:, 0:1], in1=normw[:], op0=ALU.mult, op1=ALU.mult),
                                reads=[b_ot[k], b_on[k], b_const], writes=[b_on[k]])
                            p5 = C.next_ps()
                            fw.op("pe", lambda g, p5=p5, k=k: g.transpose(out=ps_tiles[p5][:, 0:128], in_=on_[k][:], identity=ident[:]),
                                  reads=[b_on[k], C.b_ident], writes=[ps_bufs[p5]])
                            fw.op("dve", lambda g, p5=p5, k=k, cs=cs: g.tensor_tensor(
                                out=mo[k][:], in0=ps_tiles[p5][:, 0:128], in1=zs[:, cs], op=ALU.mult),
                                reads=[ps_bufs[p5], b_zs], writes=[b_mo[k]])
                            fw.dma("sp", b_mo[k], reads=[b_mo[k]], out=C.mixT_d[:, 8 + h, cs], in_=mo[k][:])
                    p4 = C.next_ps()
                    fw.op("pe", lambda g, p4=p4, dc=dc, k=k: g.matmul(out=ps_tiles[p4][:, 0:128], lhsT=kd[:, dc, :], rhs=vn[k][:],
                                                                      start=True, stop=True),
                          reads=[b_kd[dc], b_vn[k]], writes=[ps_bufs[p4]])
                    fw.op("dve", lambda g, p4=p4, dc=dc: g.scalar_tensor_tensor(
                        out=S32[:], in0=S32[:], scalar=glb[:, dc:dc + 1], in1=ps_tiles[p4][:, 0:128], op0=ALU.mult, op1=ALU.add),
                        reads=[b_S32, b_glb[dc // 4], ps_bufs[p4]], writes=[b_S32])
                    evac(fw, "act", S16[:], S32[:], [b_S32], [b_S16])
        fw.barrier()


def layer_norm(fw, t, st, sqs, gbc, bbc, b_t, b_st, b_sqs, b_gb, e_sq="pool", e_add="pool"):
    fw.op("dve", lambda g: g.reduce_sum(out=st[:, 0:1], in_=t, axis=AX.X), [b_t], [b_st])
    fw.op("dve", lambda g: g.tensor_scalar(out=st[:, 1:2], in0=st[:, 0:1], scalar1=-1.0 / D, scalar2=None, op0=ALU.mult),
          [b_st], [b_st])
    fw.op("dve", lambda g: g.tensor_scalar(out=t, in0=t, scalar1=st[:, 1:2], scalar2=None, op0=ALU.add), [b_t, b_st], [b_t])
    if e_sq == "act":
        fw.op("act", lambda g: g.activation(out=sqs, in_=t, func=AF.Square), [b_t], [b_sqs])
    else:
        fw.op(e_sq, lambda g: g.tensor_tensor(out=sqs, in0=t, in1=t, op=ALU.mult), [b_t], [b_sqs])
    fw.op("dve", lambda g: g.reduce_sum(out=st[:, 2:3], in_=sqs, axis=AX.X), [b_sqs], [b_st])
    fw.op("dve", lambda g: g.tensor_scalar(out=st[:, 3:4], in0=st[:, 2:3], scalar1=1.0 / D, scalar2=LN_EPS, op0=ALU.mult,
                                           op1=ALU.add), [b_st], [b_st])
    rsqrt_inplace(fw, st[:, 3:4], b_st)
    evac(fw, "dve", st[:, 2:3], st[:, 3:4], [b_st], [b_st])
    fw.op("dve", lambda g: g.scalar_tensor_tensor(out=t, in0=t, scalar=st[:, 2:3], in1=gbc, op0=ALU.mult, op1=ALU.mult),
          [b_t, b_st, b_gb], [b_t])
    fw.op(e_add, lambda g: g.tensor_tensor(out=t, in0=t, in1=bbc, op=ALU.add), [b_t, b_gb], [b_t])


def phase_out(C):
    nc, fw, ps_tiles, ps_bufs = C.nc, C.fw, C.ps_tiles, C.ps_bufs
    with ExitStack() as ph:
        psb = lambda n, s, dt=F32: ph.enter_context(nc.sbuf_tensor(n, list(s), dt))
        wo = psb("wo", [128, 16, D], BF16)
        b_wo = fw.buf("wo")
        for j in range(16):
            fw.dma("sp", b_wo, writes=[b_wo], out=wo[:, :, j * 128:(j + 1) * 128], in_=C.wo_bf[j, :, :, :])
        g1 = psb("g1", [128, D])
        b1 = psb("b1", [128, D])
        b_gb = fw.buf("gb1")
        fw.dma("sp", b_gb, writes=[b_gb], out=g1[:], in_=C.ln1g_d[:, :])
        fw.dma("sp", b_gb, writes=[b_gb], out=b1[:], in_=C.ln1b_d[:, :])
        mt = [psb("mt%d" % i, [128, 16, 128], BF16) for i in range(2)]
        xt = [psb("xt%d" % i, [128, D]) for i in range(2)]
        tl = [psb("tl%d" % i, [128, D]) for i in range(2)]
        hTs = [psb("hTs%d" % i, [128, 16, 128], BF16) for i in range(2)]
        st = [psb("st%d" % i, [128, 4]) for i in range(2)]
        sqs = psb("sqs", [128, D])
        b_mt = [fw.buf("mt%d" % i) for i in range(2)]
        b_xt = [fw.buf("xt%d" % i) for i in range(2)]
        b_tl = [fw.buf("tl%d" % i) for i in range(2)]
        b_hTs = [fw.buf("hTs%d" % i) for i in range(2)]
        b_st = [fw.buf("lst%d" % i) for i in range(2)]
        b_sqs = fw.buf("sqs")
        for tt in range(16):
            k = tt % 2
            ts_ = slice(tt * 128, (tt + 1) * 128)
            fw.dma("sp", b_mt[k], writes=[b_mt[k]], out=mt[k][:], in_=C.mixT_d[:, :, ts_])
            fw.dma("sp", b_xt[k], writes=[b_xt[k]], out=xt[k][:], in_=C.x[ts_, :])
            banks = [C.next_ps() for _ in range(4)]
            for nb in range(4):
                for kt in range(16):
                    fw.op("pe", lambda g, nb=nb, kt=kt, k=k: g.matmul(
                        out=ps_tiles[banks[nb]][:, :], lhsT=mt[k][:, kt, :], rhs=wo[:, kt, nb * 512:(nb + 1) * 512],
                        start=(kt == 0), stop=(kt == 15)), reads=[b_mt[k], b_wo], writes=[ps_bufs[banks[nb]]])
            for nb in range(4):
                cs = slice(nb * 512, (nb + 1) * 512)
                fw.op("dve", lambda g, nb=nb, cs=cs, k=k: g.scalar_tensor_tensor(
                    out=tl[k][:, cs], in0=xt[k][:, cs], scalar=ALPHA, in1=ps_tiles[banks[nb]][:, :], op0=ALU.mult, op1=ALU.add),
                    reads=[b_xt[k], ps_bufs[banks[nb]]], writes=[b_tl[k]])
            layer_norm(fw, tl[k][:], st[k], sqs[:], g1[:], b1[:], b_tl[k], b_st[k], b_sqs, b_gb)
            fw.dma("pool", b_tl[k], reads=[b_tl[k]], out=C.h_d[ts_, :], in_=tl[k][:])
            for q4 in range(4):
                pi = C.next_ps()
                for q in range(4):
                    dt = q4 * 4 + q
                    fw.op("pe", lambda g, pi=pi, q=q, dt=dt, k=k: g.transpose(
                        out=ps_tiles[pi][:, q * 128:(q + 1) * 128], in_=tl[k][:, dt * 128:(dt + 1) * 128], identity=C.ident[:]),
                        reads=[b_tl[k], C.b_ident], writes=[ps_bufs[pi]])
                evac(fw, "act", hTs[k][:, q4 * 4:(q4 + 1) * 4, :], ps_tiles[pi][:, :].rearrange("p (a b) -> p a b", a=4),
                     [ps_bufs[pi]], [b_hTs[k]])
            fw.dma("pool", b_hTs[k], reads=[b_hTs[k]], out=C.hT_d[:, :, ts_], in_=hTs[k][:])
        fw.barrier()


def phase_peer(C):
    nc, fw, ps_tiles, ps_bufs = C.nc, C.fw, C.ps_tiles, C.ps_bufs
    NEGBIG = -1.0e30
    with ExitStack() as ph:
        psb = lambda n, s, dt=F32: ph.enter_context(nc.sbuf_tensor(n, list(s), dt))
        idx = psb("idx", [128, 16, 128], U32)
        gate = psb("gate", [128, 16, 128])
        b_idx = [fw.buf("idx%d" % i) for i in range(16)]
        b_gate = [fw.buf("gate%d" % i) for i in range(16)]
        with ExitStack() as pa:
            pab = lambda n, s, dt=F32: pa.enter_context(nc.sbuf_tensor(n, list(s), dt))
            wq = pab("wq", [128, 16, D], BF16)
            b_wq = fw.buf("wq")
            for j in range(16):
                fw.dma("sp", b_wq, writes=[b_wq], out=wq[:, :, j * 128:(j + 1) * 128], in_=C.wq_bf[j, :, :, :])
            keysf = pab("keysf", [128, 2, 128])
            keysb = pab("keysb", [128, 2, 128], BF16)
            iota = pab("iota_sb", [128, 8, 16, 16])
            iotam = pab("iotam_sb", [128, 8, 16, 16])
            eq2 = pab("eq2", [128, 8, 16, 16])
            b_keys, b_iota = fw.buf("keys"), fw.buf("iota")
            fw.dma("sp", b_keys, writes=[b_keys], out=keysf[:], in_=C.keysT_d.rearrange("p d k -> d p k"))
            fw.op("dve", lambda g: g.tensor_copy(out=keysb[:], in_=keysf[:]), [b_keys], [b_keys])
            fw.dma("sp", b_iota, writes=[b_iota], out=iota[:], in_=C.iota_d[:, :, :, :])
            fw.dma("sp", b_iota, writes=[b_iota], out=iotam[:], in_=C.iotam_d[:, :, :, :])
            hTt = [pab("hTt%d" % i, [128, 16, 128], BF16) for i in range(2)]
            b_hTt = [fw.buf("hTt%d" % i) for i in range(2)]
            qT = pab("qTp", [128, 16, 128], BF16)
            sc = pab("sc", [128, 16, 128])
            sc2 = pab("sc2", [128, 16, 128])
            sv = pab("sv", [128, 16, 16])
            si = pab("si", [128, 16, 16], U32)
            sif = pab("sif", [128, 16, 16])
            cand = pab("cand", [128, 8, 256])
            cand2 = pab("cand2", [128, 8, 256])
            tsv = pab("tsv", [128, 8, 16])
            tpos = pab("tpos", [128, 8, 16], U32)
            posf = pab("posf", [128, 8, 16])
            cf = pab("cf", [128, 8, 16])
            rf = pab("rf", [128, 8, 16])
            eq = pab("eq", [128, 8, 16, 16])
            i1 = pab("i1", [128, 8, 16])
            i2 = pab("i2", [128, 8, 16])
            ef = pab("ef", [128, 8, 16])
            esm = pab("esm", [128, 8, 16])
            ssum = pab("ssum", [128, 8])
            b_qT = [fw.buf("qTp%d" % i) for i in range(4)]
            b_sc = [fw.buf("sc%d" % i) for i in range(4)]
            b_tk = fw.buf("tk")
            for tt in range(16):
                k = tt % 2
                ts_ = slice(tt * 128, (tt + 1) * 128)
                fw.dma("sp", b_hTt[k], writes=[b_hTt[k]], out=hTt[k][:], in_=C.hT_d[:, :, ts_])
                for q4 in range(4):
                    pi = C.next_ps()
                    for q in range(4):
                        hp = q4 * 4 + q
                        for dt in range(16):
                            fw.op("pe", lambda g, pi=pi, q=q, hp=hp, dt=dt, k=k: g.matmul(
                                out=ps_tiles[pi][:, q * 128:(q + 1) * 128], lhsT=wq[:, dt, hp * 128:(hp + 1) * 128],
                                rhs=hTt[k][:, dt, :], start=(dt == 0), stop=(dt == 15)),
                                reads=[b_wq, b_hTt[k]], writes=[ps_bufs[pi]])
                    evac(fw, "act", qT[:, q4 * 4:(q4 + 1) * 4, :], ps_tiles[pi][:, :].rearrange("p (a b) -> p a b", a=4),
                         [ps_bufs[pi]], [b_qT[q4]])
                for q4 in range(4):
                    pi = C.next_ps()
                    for q in range(4):
                        hp = q4 * 4 + q
                        fw.op("pe", lambda g, pi=pi, q=q, hp=hp: g.matmul(
                            out=ps_tiles[pi][:, q * 128:(q + 1) * 128], lhsT=qT[:, hp, :], rhs=keysb[:, hp % 2, :],
                            start=True, stop=True), reads=[b_qT[q4], b_keys], writes=[ps_bufs[pi]])
                    evac(fw, "act", sc[:, q4 * 4:(q4 + 1) * 4, :], ps_tiles[pi][:, :].rearrange("p (a b) -> p a b", a=4),
                         [ps_bufs[pi]], [b_sc[q4]])
                T = [b_tk]
                for hp in range(16):
                    R_ = [b_sc[hp // 4], b_tk]
                    fw.op("dve", lambda g, hp=hp: g.max(out=sv[:, hp, 0:8], in_=sc[:, hp, :]), R_, T)
                    fw.op("dve", lambda g, hp=hp: g.max_index(out=si[:, hp, 0:8], in_max=sv[:, hp, 0:8], in_values=sc[:, hp, :]), R_, T)
                    fw.op("dve", lambda g, hp=hp: g.match_replace(out=sc2[:, hp, :], in_to_replace=sv[:, hp, 0:8],
                                                                  in_values=sc[:, hp, :], imm_value=NEGBIG), R_, T)
                    fw.op("dve", lambda g, hp=hp: g.max(out=sv[:, hp, 8:16], in_=sc2[:, hp, :]), R_, T)
                    fw.op("dve", lambda g, hp=hp: g.max_index(out=si[:, hp, 8:16], in_max=sv[:, hp, 8:16], in_values=sc2[:, hp, :]), R_, T)
                fw.op("dve", lambda g: g.tensor_copy(out=sif[:], in_=si[:]), T, T)
                sv4 = sv[:, :, :].rearrange("p (h two) r -> p h two r", two=2)
                sif4 = sif[:, :, :].rearrange("p (h two) r -> p h two r", two=2)
                cand4 = cand[:, :, :].rearrange("p h (r c) -> p h r c", c=16)
                fw.op("dve", lambda g: g.tensor_tensor(
                    out=cand4, in0=sv4[:, :, 0, :].unsqueeze(3).to_broadcast([128, 8, 16, 16]),
                    in1=sv4[:, :, 1, :].unsqueeze(2).to_broadcast([128, 8, 16, 16]), op=ALU.add), T, T)
                for hd in range(8):
                    fw.op("dve", lambda g, hd=hd: g.max(out=tsv[:, hd, 0:8], in_=cand[:, hd, :]), T, T)
                    fw.op("dve", lambda g, hd=hd: g.max_index(out=tpos[:, hd, 0:8], in_max=tsv[:, hd, 0:8], in_values=cand[:, hd, :]), T, T)
                    fw.op("dve", lambda g, hd=hd: g.match_replace(out=cand2[:, hd, :], in_to_replace=tsv[:, hd, 0:8],
                                                                  in_values=cand[:, hd, :], imm_value=NEGBIG), T, T)
                    fw.op("dve", lambda g, hd=hd: g.max(out=tsv[:, hd, 8:16], in_=cand2[:, hd, :]), T, T)
                    fw.op("dve", lambda g, hd=hd: g.max_index(out=tpos[:, hd, 8:16], in_max=tsv[:, hd, 8:16], in_values=cand2[:, hd, :]), T, T)
                fw.op("dve", lambda g: g.tensor_tensor(out=esm[:], in0=tsv[:], in1=tsv[:, :, 0:1].to_broadcast([128, 8, 16]),
                                                       op=ALU.subtract), T, T)
                fw.op("act", lambda g: g.activation(out=esm[:], in_=esm[:], func=AF.Exp), T, T)
                fw.op("dve", lambda g: g.reduce_sum(out=ssum[:], in_=esm[:], axis=AX.X), T, T)
                fw.op("dve", lambda g: g.reciprocal(out=ssum[:], in_=ssum[:]), T, T)
                fw.op("dve", lambda g, tt=tt: g.tensor_tensor(
                    out=gate[:, tt, :].rearrange("p (h r) -> p h r", r=16), in0=esm[:],
                    in1=ssum[:, :].unsqueeze(2).to_broadcast([128, 8, 16]), op=ALU.mult), T, [b_tk, b_gate[tt]])
                fw.op("dve", lambda g: g.tensor_copy(out=posf[:], in_=tpos[:]), T, T)
                fl = lambda t: t[:, :, :, :].rearrange("p h a b -> p (h a b)")
                f3 = lambda t: t[:, :, :].rearrange("p h r -> p (h r)")
                fw.op("dve", lambda g: g.tensor_tensor(
                    out=eq[:], in0=posf[:, :, :].unsqueeze(3).to_broadcast([128, 8, 16, 16]), in1=iotam[:], op=ALU.subtract),
                    T + [b_iota], T)
                fw.op("dve", lambda g: g.tensor_scalar(out=fl(eq2), in0=fl(eq), scalar1=0.0, scalar2=None, op0=ALU.is_ge), T, T)
                fw.op("dve", lambda g: g.scalar_tensor_tensor(out=fl(eq), in0=fl(eq), scalar=15.0, in1=fl(eq2), op0=ALU.is_le,
                                                              op1=ALU.mult), T, T)
                fw.op("dve", lambda g: g.tensor_tensor(out=eq2[:], in0=eq[:], in1=iota[:], op=ALU.mult), T + [b_iota], T)
                fw.op("dve", lambda g: g.reduce_sum(out=rf[:], in_=eq2[:], axis=AX.X), T, T)
                fw.op("dve", lambda g: g.tensor_tensor(
                    out=eq2[:], in0=eq[:], in1=sif4[:, :, 0, :].unsqueeze(2).to_broadcast([128, 8, 16, 16]), op=ALU.mult), T, T)
                fw.op("dve", lambda g: g.reduce_sum(out=i1[:], in_=eq2[:], axis=AX.X), T, T)
                fw.op("dve", lambda g: g.scalar_tensor_tensor(out=f3(cf), in0=f3(rf), scalar=-16.0, in1=f3(posf), op0=ALU.mult,
                                                              op1=ALU.add), T, T)
                fw.op("dve", lambda g: g.tensor_tensor(
                    out=eq[:], in0=cf[:, :, :].unsqueeze(3).to_broadcast([128, 8, 16, 16]), in1=iota[:], op=ALU.is_equal),
                    T + [b_iota], T)
                fw.op("dve", lambda g: g.tensor_tensor(
                    out=eq2[:], in0=eq[:], in1=sif4[:, :, 1, :].unsqueeze(2).to_broadcast([128, 8, 16, 16]), op=ALU.mult), T, T)
                fw.op("dve", lambda g: g.reduce_sum(out=i2[:], in_=eq2[:], axis=AX.X), T, T)
                fw.op("dve", lambda g: g.scalar_tensor_tensor(
                    out=ef[:, :, :].rearrange("p h r -> p (h r)"), in0=i1[:, :, :].rearrange("p h r -> p (h r)"), scalar=128.0,
                    in1=i2[:, :, :].rearrange("p h r -> p (h r)"), op0=ALU.mult, op1=ALU.add), T, T)
                fw.op("dve", lambda g, tt=tt: g.tensor_copy(out=idx[:, tt, :], in_=ef[:, :, :].rearrange("p h r -> p (h r)")),
                      T, [b_tk, b_idx[tt]])
            fw.barrier()

        with ExitStack() as pb:
            pbb = lambda n, s, dt=F32: pb.enter_context(nc.sbuf_tensor(n, list(s), dt))
            g2 = pbb("g2", [128, D])
            b2 = pbb("b2", [128, D])
            b_gb = fw.buf("gb2")
            fw.dma("sp", b_gb, writes=[b_gb], out=g2[:], in_=C.ln2g_d[:, :])
            fw.dma("sp", b_gb, writes=[b_gb], out=b2[:], in_=C.ln2b_d[:, :])
            NU, NV = 8, 8
            ht = [pbb("ht%d" % i, [128, D]) for i in range(2)]
            ug = [pbb("ug%d" % i, [128, D], BF16) for i in range(NU)]
            vg = [pbb("vg%d" % i, [128, D], BF16) for i in range(NV)]
            dg = [pbb("dg%d" % i, [128, 128], BF16) for i in range(NV)]
            junk = pbb("junk", [128, D], BF16)
            av = pbb("av", [128, 128])
            wgt = pbb("wgt", [128, 128])
            tl2 = pbb("tl2", [128, D])
            sqs = pbb("sqs2", [128, D])
            st = pbb("st2", [128, 4])
            b_ht = [fw.buf("ht%d" % i) for i in range(2)]
            b_ug = [fw.buf("ug%d" % i) for i in range(NU)]
            b_vg = [fw.buf("vg%d" % i) for i in range(NV)]
            b_dg = [fw.buf("dg%d" % i) for i in range(NV)]
            b_junk, b_av, b_wgt, b_tl2, b_sqs, b_st = (fw.buf(n) for n in ("junk", "av", "wgt", "tl2", "sqs2", "st2"))
            acc_banks = [4, 5, 6, 7]
            ui = vi = 0
            for tt in range(16):
                k = tt % 2
                ts_ = slice(tt * 128, (tt + 1) * 128)
                fw.dma("sp", b_ht[k], writes=[b_ht[k]], out=ht[k][:], in_=C.h_d[ts_, :])
                for sl in range(128):
                    u = ui % NU
                    ui += 1
                    fw.dma("pool", b_ug[u], reads=[b_idx[tt]], writes=[b_ug[u]],
                           fn=lambda g, u=u, sl=sl, tt=tt: g.indirect_dma_start(
                               out=ug[u][:], out_offset=None, in_=C.u_bf[:, :],
                               in_offset=bass.IndirectOffsetOnAxis(ap=idx[:, tt, sl:sl + 1], axis=0)))
                    fw.op("dve", lambda g, u=u, sl=sl, k=k: g.scalar_tensor_tensor(
                        out=junk[:], in0=ug[u][:], scalar=1.0, in1=ht[k][:], op0=ALU.mult, op1=ALU.mult,
                        accum_out=av[:, sl:sl + 1]), reads=[b_ug[u], b_ht[k]], writes=[b_junk, b_av])
                fw.op("act", lambda g: g.activation(out=wgt[:], in_=av[:], func=AF.Gelu), [b_av], [b_wgt])
                fw.op("dve", lambda g, tt=tt: g.tensor_tensor(out=wgt[:], in0=wgt[:], in1=gate[:, tt, :], op=ALU.mult),
                      [b_wgt, b_gate[tt]], [b_wgt])
                for sl in range(128):
                    v = vi % NV
                    vi += 1
                    fw.dma("pool", b_vg[v], reads=[b_idx[tt]], writes=[b_vg[v]],
                           fn=lambda g, v=v, sl=sl, tt=tt: g.indirect_dma_start(
                               out=vg[v][:], out_offset=None, in_=C.v_bf[:, :],
                               in_offset=bass.IndirectOffsetOnAxis(ap=idx[:, tt, sl:sl + 1], axis=0)))
                    fw.op("dve", lambda g, v=v, sl=sl: g.tensor_scalar(
                        out=dg[v][:], in0=C.ident[:], scalar1=wgt[:, sl:sl + 1], scalar2=None, op0=ALU.mult),
                        reads=[C.b_ident, b_wgt], writes=[b_dg[v]])
                    for nb in range(4):
                        fw.op("pe", lambda g, nb=nb, v=v, sl=sl: g.matmul(
                            out=ps_tiles[acc_banks[nb]][:, :], lhsT=dg[v][:], rhs=vg[v][:, nb * 512:(nb + 1) * 512],
                            start=(sl == 0), stop=(sl == 127)), reads=[b_dg[v], b_vg[v]], writes=[ps_bufs[acc_banks[nb]]])
                for nb in range(4):
                    cs = slice(nb * 512, (nb + 1) * 512)
                    fw.op("dve", lambda g, nb=nb, cs=cs, k=k: g.scalar_tensor_tensor(
                        out=tl2[:, cs], in0=ht[k][:, cs], scalar=ALPHA, in1=ps_tiles[acc_banks[nb]][:, :], op0=ALU.mult,
                        op1=ALU.add), reads=[b_ht[k], ps_bufs[acc_banks[nb]]], writes=[b_tl2])
                layer_norm(fw, tl2[:], st, sqs[:], g2[:], b2[:], b_tl2, b_st, b_sqs, b_gb, e_sq="act", e_add="dve")
                fw.dma("sp", b_tl2, reads=[b_tl2], out=C.y[ts_, :], in_=tl2[:])
        fw.barrier()


def build(stop_after=None, dbg=()):
    nc = bass.Bass("TRN2", target_bir_lowering=False)
    C = Ctx()
    C.nc = nc
    dram_in = lambda n, s, dt=F32: nc.dram_tensor(n, list(s), dt, kind="ExternalInput").ap()
    C.x = dram_in("x", [SEQ, D])
    C.w_in = dram_in("w_in", [D, INW])
    C.w_out = dram_in("w_out", [D, D])
    C.peer_wq = dram_in("peer_wq", [D, D])
    C.ident_d = dram_in("ident", [128, 128])
    C.biasT_d = dram_in("biasT", [6, 128, 512])
    C.sink_d = dram_in("sink_bc", [128, 2, 512])
    C.TRI_d = dram_in("TRI", [128, 2, 128])
    C.AFT_d = dram_in("AFT", [128, 2, 128])
    C.NEGI4_d = dram_in("NEGI4", [128, 2, 512])
    C.STR4_d = dram_in("STR4", [128, 2, 512])
    C.I4_d = dram_in("I4", [128, 512])
    C.cwT_d = dram_in("cwT", [24, 128, 5])
    C.normw_d = dram_in("normw_bc", [128, 128])
    C.dtb_d = dram_in("dtb_bc", [128, 32, 16])
    C.alog_d = dram_in("alog_bc", [128, 32, 16])
    C.ln1g_d = dram_in("ln1g_bc", [128, D])
    C.ln1b_d = dram_in("ln1b_bc", [128, D])
    C.ln2g_d = dram_in("ln2g_bc", [128, D])
    C.ln2b_d = dram_in("ln2b_bc", [128, D])
    C.keysT_d = dram_in("keysT", [2, 128, 128])
    C.iota_d = dram_in("iota16", [128, 8, 16, 16])
    C.iotam_d = dram_in("iota16m", [128, 8, 16, 16])
    C.peer_u = C.peer_v = None
    if stop_after is None:
        C.peer_u = dram_in("peer_u", [16384, D])
        C.peer_v = dram_in("peer_v", [16384, D])
    C.y = nc.dram_tensor("y", [OWN, D], F32, kind="ExternalOutput").ap()

    def scratch(n, s, dt=F32):
        if n in dbg:
            return nc.dram_tensor(n, list(s), dt, kind="ExternalOutput").ap()
        return nc.dram_tensor(n, list(s), dt).ap()

    C.wi_bf = scratch("wi_bf", [45, 128, 16, 128], BF16)
    C.wo_bf = scratch("wo_bf", [16, 128, 16, 128], BF16)
    C.wq_bf = scratch("wq_bf", [16, 128, 16, 128], BF16)
    C.projA = scratch("projA", [10, 128, 2560], BF16)
    C.projV = scratch("projV", [2, 128, 2560], F32)
    C.projD = scratch("projD", [32, 128, SEQ], F32)
    C.gates = scratch("gates_d", [32, SEQ], F32)
    C.mixT_d = scratch("mixT_d", [128, 16, OWN], BF16)
    C.h_d = scratch("h_d", [OWN, D], F32)
    C.u_bf = scratch("u_bf", [16384, D], BF16)
    C.v_bf = scratch("v_bf", [16384, D], BF16)
    C.hT_d = scratch("hT_d", [128, 16, OWN], BF16)

    with ExitStack() as es:
        fw = FW(nc, es)
        C.fw = fw
        C.es = es
        sb = lambda n, s, dt=F32: es.enter_context(nc.sbuf_tensor(n, list(s), dt))
        C.ps_tiles = [es.enter_context(nc.psum_tensor("ps%d" % i, [128, 512], F32)) for i in range(8)]
        C.ps_bufs = [fw.buf("ps%d" % i) for i in range(8)]
        C.ps_i = 0

        C.ps_lim = 8

        def next_ps():
            C.ps_i += 1
            return (C.ps_i - 1) % C.ps_lim
        C.next_ps = next_ps
        C.ident = sb("ident_sb", [128, 128])
        C.b_ident = fw.buf("ident")
        fw.dma("sp", C.b_ident, writes=[C.b_ident], out=C.ident[:], in_=C.ident_d[:, :])

        def finish():
            fw.barrier()
            return nc

        import os
        skip = os.environ.get("SKIP", "").split(",")
        if "PW" not in skip:
            phase_w(C)
        if stop_after == "PW":
            return finish()
        if "P1" not in skip:
            phase_proj(C)
        if stop_after == "P1":
            return finish()
        if "P2" not in skip:
            phase_attn(C)
        if stop_after == "P2":
            return finish()
        if "P3" not in skip:
            phase_dn(C)
        if stop_after == "P3":
            return finish()
        if "P4" not in skip:
            phase_out(C)
        if stop_after == "P4":
            return finish()
        C.ps_lim = 4
        phase_peer(C)
        return finish()


def _t5_bucket(rel):
    nb, me = 16, 8
    ret = np.where(rel > 0, nb, 0)
    n = np.abs(rel)
    large = me + (np.log(np.maximum(n, 1).astype(np.float32) / np.float32(me))
                  / np.float32(np.log(128.0 / me)) * np.float32(nb - me)).astype(np.int32)
    large = np.minimum(large, nb - 1)
    return ret + np.where(n < me, n, large)


def _bias_table(rel_bias, flip):
    out = np.empty((2, 3, 128, 4, 128), np.float32)
    kk = np.arange(128)[:, None]
    qq = np.arange(128)[None, :]
    for r in range(3):
        d = (r - 1) * 128 + kk - qq
        true_rel = -d if flip else d
        bk = _t5_bucket(true_rel)
        inside = np.abs(d) <= 128
        for g in range(2):
            for hh in range(4):
                out[g, r, :, hh, :] = np.where(inside, rel_bias[bk, 4 * g + hh], np.float32(NEG))
    return out.reshape(6, 128, 512)


def prep_shared(inp):
    sh = {}
    sh["ident"] = np.eye(128, dtype=np.float32)
    sh["w_out"] = np.ascontiguousarray(inp["w_out"][0])
    sh["peer_wq"] = np.ascontiguousarray(inp["peer_wq"][0])
    sink = inp["attn_sink"][0]
    sh["sink_bc"] = np.ascontiguousarray(np.broadcast_to(
        sink.reshape(1, 2, 4, 1), (128, 2, 4, 128)).reshape(128, 2, 512)).astype(np.float32)
    ii = np.arange(128)
    TRI = np.zeros((128, 2, 128), np.float32)
    AFT = np.zeros((128, 2, 128), np.float32)
    NEGI4 = np.zeros((128, 2, 512), np.float32)
    STR4 = np.zeros((128, 2, 512), np.float32)
    for d in range(2):
        prec_eq = (ii[:, None] <= ii[None, :]) if d == 0 else (ii[:, None] >= ii[None, :])
        prec = (ii[:, None] < ii[None, :]) if d == 0 else (ii[:, None] > ii[None, :])
        TRI[:, d, :] = prec_eq
        AFT[:, d, :] = prec.T
        NEGI4[:, d, :] = np.tile(np.where(prec_eq, 0.0, NEG), (1, 4))
        STR4[:, d, :] = -np.tile(prec, (1, 4)).astype(np.float32)
    sh["TRI"], sh["AFT"], sh["NEGI4"], sh["STR4"] = TRI, AFT, NEGI4, STR4
    sh["I4"] = np.tile(np.eye(128, dtype=np.float32), (1, 4))
    for nm in ("ln1_g", "ln1_b", "ln2_g", "ln2_b"):
        sh[nm.replace("_", "") + "_bc"] = np.ascontiguousarray(np.broadcast_to(inp[nm][0][None, :], (128, D))).astype(np.float32)
    sh["keysT"] = np.ascontiguousarray(inp["peer_keys"][0].transpose(0, 2, 1)).astype(np.float32)
    sh["iota16"] = np.ascontiguousarray(np.broadcast_to(np.arange(16, dtype=np.float32), (128, 8, 16, 16)))
    sh["iota16m"] = np.ascontiguousarray(np.broadcast_to(16.0 * np.arange(16, dtype=np.float32), (128, 8, 16, 16)))
    sh["peer_u"] = np.ascontiguousarray(inp["peer_u"][0])
    sh["peer_v"] = np.ascontiguousarray(inp["peer_v"][0])
    sh["normw_bc"] = np.ascontiguousarray(np.broadcast_to(inp["dn_norm_w"][0][None, :], (128, 128))).astype(np.float32)
    return sh


def prep_core(inp, c, sh):
    b, half = c // 2, c % 2
    flip = half == 1
    m = dict(sh)
    xs = inp["x"][b]
    w_in = inp["w_in"][0]
    if flip:
        xs = xs[::-1]
        perm = np.arange(INW)
        perm[5632:5640], perm[5640:5648] = np.arange(5640, 5648), np.arange(5632, 5640)
        perm[5648:5656], perm[5656:5664] = np.arange(5656, 5664), np.arange(5648, 5656)
        w_in = w_in[:, perm]
    m["x"] = np.ascontiguousarray(xs)
    m["w_in"] = np.ascontiguousarray(w_in)
    m["biasT"] = _bias_table(inp["rel_bias"], flip)
    cw = inp["conv_w"][0]
    a_log = inp["a_log"][0]
    dtb = inp["dt_bias"][0]
    if flip:
        cw = cw[::-1]
        a_log = a_log[::-1]
        dtb = dtb[::-1]
    m["cwT"] = np.ascontiguousarray(cw.T.reshape(24, 128, 5)).astype(np.float32)
    m["dtb_bc"] = np.ascontiguousarray(np.broadcast_to(dtb.reshape(1, 1, 16), (128, 32, 16))).astype(np.float32)
    m["alog_bc"] = np.ascontiguousarray(np.broadcast_to(a_log.reshape(1, 1, 16), (128, 32, 16))).astype(np.float32)
    return m


_NC_CACHE = {}


def kernel(**inputs):
    inp = {k: np.asarray(v) for k, v in inputs.items()}
    sh = prep_shared(inp)
    in_maps = [prep_core(inp, c, sh) for c in range(8)]
    if "nc" not in _NC_CACHE:
        _NC_CACHE["nc"] = build()
    nc = _NC_CACHE["nc"]
    res = run_bass_kernel_spmd(nc, in_maps, core_ids=list(range(8)))
    out = np.empty((4, SEQ, D), np.float32)
    for c in range(8):
        yc = np.asarray(res.results[c]["y"]).astype(np.float32)
        b, half = c // 2, c % 2
        if half == 0:
            out[b, :OWN] = yc
        else:
            out[b, OWN:] = yc[::-1]
    return out
```

```python
import numpy as np
from contextlib import ExitStack
import concourse.bass as bass
import concourse.mybir as mybir
from concourse.bass_utils import run_bass_kernel_spmd

F32 = mybir.dt.float32
BF16 = mybir.dt.bfloat16
U32 = mybir.dt.uint32
I32 = mybir.dt.int32
AF = mybir.ActivationFunctionType
ALU = mybir.AluOpType
AX = mybir.AxisListType

D = 2048
SEQ = 4096
OWN = 2048
INW = 5664
NEG = -30000.0
ALPHA = 2.0 ** 0.25
LN_EPS = 1e-5
RMS_EPS = 1e-6


class Buf:
    __slots__ = ("name", "w", "rs", "dsem", "dcnt")

    def __init__(self, name):
        self.name = name
        self.w = None
        self.rs = []
        self.dsem = None
        self.dcnt = 0


class FW:
    def __init__(self, nc, es):
        self.nc = nc
        self.es = es
        self.eng = {"pe": nc.tensor, "act": nc.scalar, "dve": nc.vector, "pool": nc.gpsimd, "sp": nc.sync}
        self.sem = {k: es.enter_context(nc.semaphore("s_" + k)) for k in self.eng}
        self.cnt = {k: 0 for k in self.eng}
        self.waited = {k: {} for k in self.eng}
        self.dbufs = []
        self.nbuf = 0

    def buf(self, name=None):
        self.nbuf += 1
        return Buf(name or ("b%d" % self.nbuf))

    def _resolve(self, ev):
        if ev[0] == "e":
            return ("e_" + ev[1], self.sem[ev[1]], ev[2])
        b = ev[1]
        return ("d_" + b.name, b.dsem, b.dcnt)

    def _waits(self, e, reads, writes, skip_dma_owner=None):
        evs = []
        for b in reads:
            if b.w is not None:
                evs.append(b.w)
        for b in writes:
            if b.w is not None:
                evs.append(b.w)
            evs.extend(b.rs)
        need = {}
        for ev in evs:
            if ev[0] == "e" and ev[1] == "pe" and e == "pe":
                continue
            if ev[0] == "d" and skip_dma_owner is not None and ev[1] is skip_dma_owner:
                continue
            key, sem, val = self._resolve(ev)
            if need.get(key, (None, 0))[1] < val:
                need[key] = (sem, val)
        for key, (sem, val) in need.items():
            if self.waited[e].get(key, 0) >= val:
                continue
            self.eng[e].wait_ge(sem, val)
            self.waited[e][key] = val

    def op(self, e, fn, reads=(), writes=()):
        self._waits(e, reads, writes)
        ins = fn(self.eng[e])
        self.cnt[e] += 1
        ins.then_inc(self.sem[e], 1)
        ev = ("e", e, self.cnt[e])
        for b in reads:
            b.rs.append(ev)
        for b in writes:
            b.w = ev
            b.rs = []
        return ins

    def _dsem(self, b):
        if b.dsem is None:
            b.dsem = self.es.enter_context(self.nc.semaphore("d_" + b.name))
            self.dbufs.append(b)
        return b.dsem

    def dma(self, q, owner, reads=(), writes=(), fn=None, out=None, in_=None):
        self._dsem(owner)
        self._waits(q, reads, writes, skip_dma_owner=owner)
        if fn is None:
            ins = self.eng[q].dma_start(out=out, in_=in_)
        else:
            ins = fn(self.eng[q])
        owner.dcnt += 16
        ins.then_inc(owner.dsem, 16)
        ev = ("d", owner)
        for b in reads:
            b.rs.append(ev)
        for b in writes:
            b.w = ev
            b.rs = []
        return ins

    def barrier(self):
        for e in self.eng:
            for k in self.eng:
                if k == e or self.cnt[k] == 0:
                    continue
                key = "e_" + k
                if self.waited[e].get(key, 0) < self.cnt[k]:
                    self.eng[e].wait_ge(self.sem[k], self.cnt[k])
                    self.waited[e][key] = self.cnt[k]
            for b in self.dbufs:
                key = "d_" + b.name
                if b.dcnt and self.waited[e].get(key, 0) < b.dcnt:
                    self.eng[e].wait_ge(b.dsem, b.dcnt)
                    self.waited[e][key] = b.dcnt


def rsqrt_inplace(fw, ap, b):
    fw.op("act", lambda g: g.activation(out=ap, in_=ap, func=AF.Ln), [b], [b])
    fw.op("act", lambda g: g.activation(out=ap, in_=ap, func=AF.Exp, scale=-0.5), [b], [b])


def _alt(i, engines=("act", "dve")):
    return engines[i % len(engines)]


def evac(fw, e, out, in_, reads, writes):
    if e == "act":
        return fw.op("act", lambda g: g.activation(out=out, in_=in_, func=AF.Copy), reads, writes)
    return fw.op(e, lambda g: g.tensor_copy(out=out, in_=in_), reads, writes)


class Ctx:
    pass


def phase_w(C):
    nc, fw = C.nc, C.fw
    with ExitStack() as ph:
        wst = [ph.enter_context(nc.sbuf_tensor("wst%d" % i, [128, 16, 128], F32)) for i in range(3)]
        wbf = [ph.enter_context(nc.sbuf_tensor("wbf%d" % i, [128, 16, 128], BF16)) for i in range(3)]
        b_wst = [fw.buf("wst%d" % i) for i in range(3)]
        b_wbf = [fw.buf("wbf%d" % i) for i in range(3)]
        it = 0
        for (src, dst, ntile, ncols) in ((C.w_in, C.wi_bf, 45, INW), (C.w_out, C.wo_bf, 16, D), (C.peer_wq, C.wq_bf, 16, D)):
            w_v = src.rearrange("(dt p) c -> p dt c", p=128)
            for j in range(ntile):
                k = it % 3
                nco = min(128, ncols - j * 128)
                fw.dma("sp", b_wst[k], writes=[b_wst[k]], out=wst[k][:, :, 0:nco],
                       in_=w_v[:, :, j * 128:j * 128 + nco])
                evac(fw, _alt(it, ("act", "dve", "pool")), wbf[k][:, :, 0:nco], wst[k][:, :, 0:nco],
                     [b_wst[k]], [b_wbf[k]])
                fw.dma("sp", b_wbf[k], reads=[b_wbf[k]], out=dst[j, :, :, 0:nco], in_=wbf[k][:, :, 0:nco])
                it += 1
        fw.barrier()


def phase_proj(C):
    nc, fw, ps_tiles, ps_bufs = C.nc, C.fw, C.ps_tiles, C.ps_bufs
    with ExitStack() as ph:
        psb = lambda n, s, dt=F32: ph.enter_context(nc.sbuf_tensor(n, list(s), dt))
        xs = [psb("xs%d" % i, [128, 4, D]) for i in range(2)]
        b_xs = [fw.buf("xs%d" % i) for i in range(2)]
        xT = [psb("xT%d" % i, [128, 16, 512], BF16) for i in range(2)]
        b_xT = [[fw.buf("xT%d_%d" % (i, dt)) for dt in range(16)] for i in range(2)]
        NW = 3
        wt = [psb("wt%d" % i, [128, 16, 128], BF16) for i in range(NW)]
        b_wt = [fw.buf("wt%d" % i) for i in range(NW)]
        NS = 4
        stA = [psb("stA%d" % i, [128, 512], BF16) for i in range(NS)]
        stD = [psb("stD%d" % i, [128, 512], F32) for i in range(NS)]
        b_st = [fw.buf("st%d" % i) for i in range(NS)]
        x_v = C.x.rearrange("(t p) d -> p t d", p=128)
        nblk = 8
        ev_i = wq_i = st_i = 0

        def cast_steps():
            if C.peer_u is None:
                return
            ust = [psb("ust%d" % i, [128, D]) for i in range(4)]
            ubf = [psb("ubf%d" % i, [128, D], BF16) for i in range(4)]
            b_ust = [fw.buf("ust%d" % i) for i in range(4)]
            b_ubf = [fw.buf("ubf%d" % i) for i in range(4)]
            it = 0
            for (src, dst) in ((C.peer_u, C.u_bf), (C.peer_v, C.v_bf)):
                for r in range(128):
                    k = it % 4
                    rs = slice(r * 128, (r + 1) * 128)
                    fw.dma("sp", b_ust[k], writes=[b_ust[k]], out=ust[k][:], in_=src[rs, :])
                    evac(fw, _alt(it, ("act", "dve")), ubf[k][:], ust[k][:], [b_ust[k]], [b_ubf[k]])
                    fw.dma("pool", b_ubf[k], reads=[b_ubf[k]], out=dst[rs, :], in_=ubf[k][:])
                    it += 1
                    yield

        caster = cast_steps()
        n_it = 0

        def tiles_for(blk):
            if blk < 4:
                return list(range(45))
            if blk == 4:
                return list(range(8, 36)) + [44]
            return list(range(20, 36)) + [44]

        fw.dma("sp", b_xs[0], writes=[b_xs[0]], out=xs[0][:], in_=x_v[:, 0:4, :])
        for blk in range(nblk):
            k = blk % 2
            if blk + 1 < nblk:
                fw.dma("sp", b_xs[1 - k], writes=[b_xs[1 - k]], out=xs[1 - k][:],
                       in_=x_v[:, (blk + 1) * 4:(blk + 2) * 4, :])
            for dt in range(16):
                pi = C.next_ps()
                for t in range(4):
                    fw.op("pe", lambda g, t=t, dt=dt, pi=pi: g.transpose(
                        out=ps_tiles[pi][:, t * 128:(t + 1) * 128],
                        in_=xs[k][:, t, dt * 128:(dt + 1) * 128], identity=C.ident[:]),
                        reads=[b_xs[k], C.b_ident], writes=[ps_bufs[pi]])
                evac(fw, _alt(ev_i), xT[k][:, dt, :], ps_tiles[pi][:, :], [ps_bufs[pi]], [b_xT[k][dt]])
                ev_i += 1
            for j in tiles_for(blk):
                wk = wq_i % NW
                wq_i += 1
                nco = 128 if j < 44 else 32
                fw.dma("sp", b_wt[wk], writes=[b_wt[wk]], out=wt[wk][:, :, 0:nco], in_=C.wi_bf[j, :, :, 0:nco])
                pi = C.next_ps()
                for dt in range(16):
                    fw.op("pe", lambda g, dt=dt, pi=pi, wk=wk, nco=nco: g.matmul(
                        out=ps_tiles[pi][0:nco, :], lhsT=wt[wk][:, dt, 0:nco], rhs=xT[k][:, dt, :],
                        start=(dt == 0), stop=(dt == 15)),
                        reads=[b_wt[wk], b_xT[k][dt]], writes=[ps_bufs[pi]])
                si = st_i % NS
                st_i += 1
                tok = slice(blk * 512, (blk + 1) * 512)
                if j < 10:
                    evac(fw, _alt(ev_i), stA[si][:, :], ps_tiles[pi][:, :], [ps_bufs[pi]], [b_st[si]])
                    fw.dma("pool", b_st[si], reads=[b_st[si]], out=C.projA[j, :, tok], in_=stA[si][:, :])
                elif j < 44:
                    dst = C.projV[j - 10, :, tok] if j < 12 else C.projD[j - 12, :, tok]
                    evac(fw, _alt(ev_i), stD[si][:, :], ps_tiles[pi][:, :], [ps_bufs[pi]], [b_st[si]])
                    fw.dma("pool", b_st[si], reads=[b_st[si]], out=dst, in_=stD[si][:, :])
                else:
                    evac(fw, _alt(ev_i), stD[si][0:32, :], ps_tiles[pi][0:32, :], [ps_bufs[pi]], [b_st[si]])
                    fw.dma("pool", b_st[si], reads=[b_st[si]], out=C.gates[:, tok], in_=stD[si][0:32, :])
                ev_i += 1
                n_it += 1
                next(caster, None)
        for _ in caster:
            pass
        fw.barrier()


def phase_attn(C):
    nc, fw, ps_tiles, ps_bufs = C.nc, C.fw, C.ps_tiles, C.ps_bufs
    NKB = 17
    with ExitStack() as ph:
        psb = lambda n, s, dt=F32: ph.enter_context(nc.sbuf_tensor(n, list(s), dt))
        qT = psb("qT", [128, 8, OWN], BF16)
        kT = psb("kT", [128, 2, NKB * 128], BF16)
        vT = psb("vTf", [128, 2, NKB * 128], F32)
        V = psb("V", [128, NKB, 2, 128], BF16)
        biasT = psb("biasT_sb", [128, 6, 512], F32)
        esink = psb("esink", [128, 2, 512], F32)
        ones = psb("ones_bf", [128, 128], BF16)
        b_qT, b_kT, b_vT, b_bias, b_esink, b_ones = (fw.buf(n) for n in ("qT", "kT", "vTf", "biasT", "esink", "ones"))
        b_V = [fw.buf("V%d" % i) for i in range(NKB)]
        for h in range(8):
            fw.dma("sp", b_qT, writes=[b_qT], out=qT[:, h, :], in_=C.projA[h, :, 0:OWN])
        for g in range(2):
            fw.dma("sp", b_kT, writes=[b_kT], out=kT[:, g, :], in_=C.projA[8 + g, :, 0:NKB * 128])
            fw.dma("sp", b_vT, writes=[b_vT], out=vT[:, g, :], in_=C.projV[g, :, 0:NKB * 128])
        fw.dma("sp", b_bias, writes=[b_bias], out=biasT[:], in_=C.biasT_d.rearrange("a k c -> k a c"))
        fw.dma("sp", b_esink, writes=[b_esink], out=esink[:], in_=C.sink_d[:, :, :])
        fw.op("act", lambda g_: g_.activation(out=esink[:], in_=esink[:], func=AF.Exp), [b_esink], [b_esink])
        fw.op("dve", lambda g_: g_.memset(ones[:], 1.0), [], [b_ones])
        ev_i = 0
        for kb in range(NKB):
            pi = C.next_ps()
            for g in range(2):
                fw.op("pe", lambda g_, g=g, kb=kb, pi=pi: g_.transpose(
                    out=ps_tiles[pi][:, g * 128:(g + 1) * 128], in_=vT[:, g, kb * 128:(kb + 1) * 128],
                    identity=C.ident[:]), reads=[b_vT, C.b_ident], writes=[ps_bufs[pi]])
            evac(fw, _alt(ev_i), V[:, kb, :, :], ps_tiles[pi][:, 0:256].rearrange("p (g d) -> p g d", g=2),
                 [ps_bufs[pi]], [b_V[kb]])
            ev_i += 1
        NP = 6
        tS = [psb("tS%d" % i, [128, 512], F32) for i in range(NP)]
        pT = [psb("pT%d" % i, [128, 512], BF16) for i in range(NP)]
        b_tS = [fw.buf("tS%d" % i) for i in range(NP)]
        b_pT = [fw.buf("pT%d" % i) for i in range(NP)]
        den = [psb("den%d" % i, [128, 512], F32) for i in range(2)]
        ost = [psb("ost%d" % i, [128, 512], BF16) for i in range(2)]
        b_ost = [fw.buf("ost%d" % i) for i in range(2)]
        b_den = [fw.buf("den%d" % i) for i in range(2)]
        p_i = 0
        scale = 128.0 ** -0.5
        for i in range(16):
            for g in range(2):
                kbs = [kb for kb in (i - 1, i, i + 1) if kb >= 0]
                slots = []
                for kb in kbs:
                    pi = C.next_ps()
                    for hh in range(4):
                        fw.op("pe", lambda g_, g=g, kb=kb, pi=pi, hh=hh, i=i: g_.matmul(
                            out=ps_tiles[pi][:, hh * 128:(hh + 1) * 128], lhsT=kT[:, g, kb * 128:(kb + 1) * 128],
                            rhs=qT[:, 4 * g + hh, i * 128:(i + 1) * 128], start=True, stop=True),
                            reads=[b_kT, b_qT], writes=[ps_bufs[pi]])
                    sl = p_i % NP
                    p_i += 1
                    rel = kb - i + 1
                    fw.op("dve", lambda g_, pi=pi, sl=sl, g=g, rel=rel: g_.scalar_tensor_tensor(
                        out=tS[sl][:], in0=ps_tiles[pi][:, :], scalar=scale, in1=biasT[:, g * 3 + rel, :],
                        op0=ALU.mult, op1=ALU.add), reads=[ps_bufs[pi], b_bias], writes=[b_tS[sl]])
                    fw.op("act", lambda g_, sl=sl: g_.activation(out=pT[sl][:], in_=tS[sl][:], func=AF.Exp),
                          reads=[b_tS[sl]], writes=[b_pT[sl]])
                    slots.append(sl)
                po = C.next_ps()
                pd = C.next_ps()
                for hh in range(4):
                    for n, (kb, sl) in enumerate(zip(kbs, slots)):
                        fw.op("pe", lambda g_, g=g, kb=kb, sl=sl, hh=hh, n=n, po=po: g_.matmul(
                            out=ps_tiles[po][:, hh * 128:(hh + 1) * 128], lhsT=V[:, kb, g, :],
                            rhs=pT[sl][:, hh * 128:(hh + 1) * 128], start=(n == 0), stop=(n == len(kbs) - 1)),
                            reads=[b_V[kb], b_pT[sl]], writes=[ps_bufs[po]])
                for n, sl in enumerate(slots):
                    fw.op("pe", lambda g_, sl=sl, n=n, pd=pd: g_.matmul(
                        out=ps_tiles[pd][:, :], lhsT=ones[:], rhs=pT[sl][:, :], start=(n == 0),
                        stop=(n == len(slots) - 1)), reads=[b_ones, b_pT[sl]], writes=[ps_bufs[pd]])
                dk = (i * 2 + g) % 2
                fw.op("dve", lambda g_, pd=pd, dk=dk, g=g: g_.tensor_tensor(
                    out=den[dk][:], in0=ps_tiles[pd][:, :], in1=esink[:, g, :], op=ALU.add),
                    reads=[ps_bufs[pd], b_esink], writes=[b_den[dk]])
                fw.op("dve", lambda g_, dk=dk: g_.reciprocal(out=den[dk][:], in_=den[dk][:]),
                      reads=[b_den[dk]], writes=[b_den[dk]])
                fw.op("dve", lambda g_, po=po, dk=dk: g_.tensor_tensor(
                    out=ost[dk][:, :], in0=ps_tiles[po][:, :], in1=den[dk][:, :], op=ALU.mult),
                    reads=[ps_bufs[po], b_den[dk]], writes=[b_ost[dk]])
                fw.dma("pool", b_ost[dk], reads=[b_ost[dk]],
                       out=C.mixT_d[:, 4 * g:4 * g + 4, i * 128:(i + 1) * 128],
                       in_=ost[dk][:, :].rearrange("p (h q) -> p h q", h=4))
        fw.barrier()


def phase_dn(C):
    nc, fw, ps_tiles, ps_bufs = C.nc, C.fw, C.ps_tiles, C.ps_bufs
    ident = C.ident
    with ExitStack() as ph:
        psb = lambda n, s, dt=F32: ph.enter_context(nc.sbuf_tensor(n, list(s), dt))
        TRI = psb("TRI_sb", [128, 2, 128])
        AFT = psb("AFT_sb", [128, 2, 128])
        NEGI4 = psb("NEGI4_sb", [128, 2, 512])
        STR4 = psb("STR4_sb", [128, 2, 512])
        I4 = psb("I4_sb", [128, 512])
        cw = psb("cw_sb", [128, 24, 5])
        normw = psb("normw_sb", [128, 128])
        ones_f = psb("ones_f", [128, 128])
        ones_b = psb("ones_b2", [128, 128], BF16)
        b_const = fw.buf("dnconst")
        for (t, src) in ((TRI, C.TRI_d), (AFT, C.AFT_d), (NEGI4, C.NEGI4_d), (STR4, C.STR4_d)):
            fw.dma("sp", b_const, writes=[b_const], out=t[:], in_=src[:, :, :])
        fw.dma("sp", b_const, writes=[b_const], out=I4[:], in_=C.I4_d[:, :])
        fw.dma("sp", b_const, writes=[b_const], out=cw[:], in_=C.cwT_d.rearrange("t p j -> p t j"))
        fw.dma("sp", b_const, writes=[b_const], out=normw[:], in_=C.normw_d[:, :])
        b_ones = fw.buf("dnones")
        fw.op("dve", lambda g: g.memset(ones_f[:], 1.0), [], [b_ones])
        fw.op("dve", lambda g: g.memset(ones_b[:], 1.0), [b_ones], [b_ones])

        beta = psb("beta", [128, 32, 16])
        graw = psb("graw", [128, 32, 16])
        sc_eg = psb("sc_eg", [128, 48, 8])
        sc_negeg = psb("sc_negeg", [128, 48, 8])
        sc_negg = psb("sc_negg", [128, 48, 8])
        sc_ekd = psb("sc_ekd", [128, 48, 8])
        b_beta, b_graw, b_sc = fw.buf("beta"), fw.buf("graw"), fw.buf("sc")
        with ExitStack() as pg:
            gsb = pg.enter_context(nc.sbuf_tensor("gsb", [32, SEQ], F32))
            Gtok = pg.enter_context(nc.sbuf_tensor("Gtok", [128, 32, 32], F32))
            dtb = pg.enter_context(nc.sbuf_tensor("dtb_sb", [128, 32, 16], F32))
            alg = pg.enter_context(nc.sbuf_tensor("alog_sb", [128, 32, 16], F32))
            b_gsb, b_Gtok, b_dtb, b_alg = fw.buf("gsb"), fw.buf("Gtok"), fw.buf("dtb"), fw.buf("alg")
            fw.dma("sp", b_gsb, writes=[b_gsb], out=gsb[:], in_=C.gates[:, :])
            fw.dma("sp", b_dtb, writes=[b_dtb], out=dtb[:], in_=C.dtb_d[:, :, :])
            fw.dma("sp", b_alg, writes=[b_alg], out=alg[:], in_=C.alog_d[:, :, :])
            for half in range(2):
                pi = C.next_ps()
                for cc in range(16):
                    c = half * 16 + cc
                    fw.op("pe", lambda g, pi=pi, cc=cc, c=c: g.transpose(
                        out=ps_tiles[pi][:, cc * 32:(cc + 1) * 32], in_=gsb[0:32, c * 128:(c + 1) * 128],
                        identity=ident[0:32, 0:32]), reads=[b_gsb, C.b_ident], writes=[ps_bufs[pi]])
                evac(fw, "dve", Gtok[:, half * 16:(half + 1) * 16, :],
                     ps_tiles[pi][:, :].rearrange("p (c k) -> p c k", k=32), [ps_bufs[pi]], [b_Gtok])
            fw.op("act", lambda g: g.activation(out=beta[:], in_=Gtok[:, :, 0:16], func=AF.Sigmoid), [b_Gtok], [b_beta])
            fw.op("dve", lambda g: g.tensor_tensor(out=graw[:], in0=Gtok[:, :, 16:32], in1=dtb[:], op=ALU.add),
                  [b_Gtok, b_dtb], [b_graw])
            fw.op("act", lambda g: g.activation(out=graw[:], in_=graw[:], func=AF.Exp), [b_graw], [b_graw])
            fw.op("dve", lambda g: g.tensor_scalar(out=graw[:], in0=graw[:], scalar1=1.0, scalar2=None, op0=ALU.add),
                  [b_graw], [b_graw])
            fw.op("act", lambda g: g.activation(out=graw[:], in_=graw[:], func=AF.Ln), [b_graw], [b_graw])
            fw.op("act", lambda g: g.activation(out=alg[:], in_=alg[:], func=AF.Exp), [b_alg], [b_alg])
            fw.op("dve", lambda g: g.scalar_tensor_tensor(out=graw[:], in0=graw[:], scalar=-1.0, in1=alg[:],
                                                          op0=ALU.mult, op1=ALU.mult), [b_graw, b_alg], [b_graw])
            pg_, pa_ = C.next_ps(), C.next_ps()
            for d in range(2):
                for c in range(16 if d == 0 else 32):
                    dc = c if d == 0 else 16 + c
                    fw.op("pe", lambda g, d=d, c=c, dc=dc: g.matmul(
                        out=ps_tiles[pg_][:, dc * 8:(dc + 1) * 8], lhsT=TRI[:, d, :], rhs=graw[:, c, d * 8:(d + 1) * 8],
                        start=True, stop=True), reads=[b_const, b_graw], writes=[ps_bufs[pg_]])
                    fw.op("pe", lambda g, d=d, c=c, dc=dc: g.matmul(
                        out=ps_tiles[pa_][:, dc * 8:(dc + 1) * 8], lhsT=AFT[:, d, :], rhs=graw[:, c, d * 8:(d + 1) * 8],
                        start=True, stop=True), reads=[b_const, b_graw], writes=[ps_bufs[pa_]])
            v3 = lambda t: t[:, :, :].rearrange("p a b -> p (a b)")
            fw.op("act", lambda g: g.activation(out=v3(sc_eg), in_=ps_tiles[pg_][:, 0:384], func=AF.Exp),
                  [ps_bufs[pg_]], [b_sc])
            fw.op("dve", lambda g: g.tensor_scalar(out=v3(sc_negg), in0=ps_tiles[pg_][:, 0:384], scalar1=-1.0,
                                                   scalar2=None, op0=ALU.mult), [ps_bufs[pg_]], [b_sc])
            fw.op("dve", lambda g: g.tensor_scalar(out=v3(sc_negeg), in0=v3(sc_eg), scalar1=-1.0, scalar2=None,
                                                   op0=ALU.mult), [b_sc], [b_sc])
            fw.op("act", lambda g: g.activation(out=v3(sc_ekd), in_=ps_tiles[pa_][:, 0:384], func=AF.Exp),
                  [ps_bufs[pa_]], [b_sc])
            fw.barrier()

        pad = psb("pad", [128, SEQ + 4])
        acc = psb("acc", [128, SEQ])
        y32 = psb("y32", [128, SEQ])
        b_pad = fw.buf("pad")
        b_acc = [fw.buf("acc0"), fw.buf("acc1")]
        b_y = [fw.buf("y%d" % i) for i in range(8)]
        sq = [psb("sq%d" % i, [128, 512], BF16) for i in range(2)]
        rn = [psb("rn%d" % i, [128, 512]) for i in range(2)]
        b_sq = [fw.buf("sq%d" % i) for i in range(2)]
        b_rn = [fw.buf("rn%d" % i) for i in range(2)]
        QT = psb("QT", [128, OWN], BF16)
        KT = psb("KT", [128, SEQ], BF16)
        Vtok = psb("Vtok", [128, 32, 128], BF16)
        kd = psb("kd", [128, 48, 128], BF16)
        X = psb("Xinv", [128, 48, 128], BF16)
        AT = psb("AT", [128, 32, 128], BF16)
        qgT = psb("qgT", [128, 32, 128], BF16)
        glb = psb("glb", [128, 48])
        zs = psb("zs", [128, OWN])
        obwd = psb("obwd", [128, 16, 128])
        b_QT = [fw.buf("QT%d" % i) for i in range(4)]
        b_KT = [fw.buf("KT%d" % i) for i in range(8)]
        b_Vtok = [fw.buf("Vtok%d" % i) for i in range(8)]
        b_kd = [fw.buf("kd%d" % i) for i in range(48)]
        b_X = [fw.buf("X%d" % i) for i in range(12)]
        b_AT = [fw.buf("AT%d" % i) for i in range(8)]
        b_qgT = [fw.buf("qgT%d" % i) for i in range(8)]
        b_glb = [fw.buf("glb%d" % i) for i in range(12)]
        b_zs = fw.buf("zs")
        b_obwd = [fw.buf("obwd%d" % i) for i in range(16)]
        GB = psb("GB", [128, 512])
        tmpg = psb("tmpg", [128, 512])
        decT = psb("decT", [128, 512])
        EGR = psb("EGR", [128, 512])
        m1 = psb("m1", [128, 512])
        Pm = [psb("Pm%d" % i, [128, 512]) for i in range(2)]
        PT = [psb("PT%d" % i, [128, 512]) for i in range(2)]
        Rn = [psb("Rn%d" % i, [128, 512]) for i in range(2)]
        b_GB, b_tmpg, b_decT, b_EGR, b_m1 = (fw.buf(n) for n in ("GB", "tmpg", "decT", "EGR", "m1"))
        b_Pm = [fw.buf("Pm%d" % i) for i in range(2)]
        b_PT = [fw.buf("PT%d" % i) for i in range(2)]
        b_Rn = [fw.buf("Rn%d" % i) for i in range(2)]
        S32 = psb("S32", [128, 128])
        S16 = psb("S16", [128, 128], BF16)
        b_S32, b_S16 = fw.buf("S32"), fw.buf("S16")
        Rt = [psb("Rt%d" % i, [128, 128], BF16) for i in range(2)]
        vn = [psb("vn%d" % i, [128, 128], BF16) for i in range(2)]
        b_Rt = [fw.buf("Rt%d" % i) for i in range(2)]
        b_vn = [fw.buf("vn%d" % i) for i in range(2)]
        ot = [psb("ot%d" % i, [128, 128]) for i in range(2)]
        osq = [psb("osq%d" % i, [128, 128]) for i in range(2)]
        orr = [psb("orr%d" % i, [128, 2]) for i in range(2)]
        on_ = [psb("on%d" % i, [128, 128]) for i in range(2)]
        mo = [psb("mo%d" % i, [128, 128], BF16) for i in range(2)]
        b_ot = [fw.buf("ot%d" % i) for i in range(2)]
        b_on = [fw.buf("on%d" % i) for i in range(2)]
        b_mo = [fw.buf("mo%d" % i) for i in range(2)]
        fw.op("pool", lambda g: g.memset(pad[:, 0:2], 0.0), [], [b_pad])
        fw.op("pool", lambda g: g.memset(pad[:, SEQ + 2:SEQ + 4], 0.0), [b_pad], [b_pad])
        ev = [0]

        def conv_silu(tile_idx, ct, nload, n):
            fw.dma("sp", b_pad, writes=[b_pad], out=pad[:, 2:2 + nload], in_=C.projD[tile_idx, :, 0:nload])
            hlf = n // 2
            for (lo, hi), e, ba in (((0, hlf), "dve", b_acc[0]), ((hlf, n), "dve", b_acc[1])):
                fw.op(e, lambda g, lo=lo, hi=hi: g.tensor_scalar(
                    out=acc[:, lo:hi], in0=pad[:, lo:hi], scalar1=cw[:, ct, 0:1], scalar2=None, op0=ALU.mult),
                    reads=[b_pad, b_const], writes=[ba])
                for j in range(1, 5):
                    fw.op(e, lambda g, lo=lo, hi=hi, j=j: g.scalar_tensor_tensor(
                        out=acc[:, lo:hi], in0=pad[:, lo + j:hi + j], scalar=cw[:, ct, j:j + 1], in1=acc[:, lo:hi],
                        op0=ALU.mult, op1=ALU.add), reads=[b_pad, b_const, ba], writes=[ba])
            fw.op("act", lambda g: g.activation(out=y32[:, 0:n], in_=acc[:, 0:n], func=AF.Silu),
                  reads=b_acc, writes=b_y[0:n // 512])

        def l2norm(n, is_q):
            for blk in range(n // 512):
                cs = slice(blk * 512, (blk + 1) * 512)
                k = blk % 2
                fw.op("act", lambda g, cs=cs, k=k: g.activation(out=sq[k][:], in_=y32[:, cs], func=AF.Square),
                      reads=[b_y[blk]], writes=[b_sq[k]])
                pi = C.next_ps()
                fw.op("pe", lambda g, pi=pi, k=k: g.matmul(out=ps_tiles[pi][:, :], lhsT=ones_b[:], rhs=sq[k][:],
                                                          start=True, stop=True), reads=[b_ones, b_sq[k]], writes=[ps_bufs[pi]])
                fw.op("dve", lambda g, pi=pi, k=k: g.tensor_scalar(
                    out=rn[k][:], in0=ps_tiles[pi][:, :], scalar1=RMS_EPS, scalar2=None, op0=ALU.add),
                    reads=[ps_bufs[pi]], writes=[b_rn[k]])
                rsqrt_inplace(fw, rn[k][:], b_rn[k])
                if is_q:
                    fw.op("dve", lambda g, cs=cs, k=k: g.scalar_tensor_tensor(
                        out=QT[:, cs], in0=y32[:, cs], scalar=128.0 ** -0.5, in1=rn[k][:], op0=ALU.mult, op1=ALU.mult),
                        reads=[b_y[blk], b_rn[k]], writes=[b_QT[blk]])
                else:
                    fw.op("dve", lambda g, cs=cs, k=k: g.tensor_tensor(out=y32[:, cs], in0=y32[:, cs], in1=rn[k][:],
                                                                      op=ALU.mult), reads=[b_y[blk], b_rn[k]], writes=[b_y[blk]])
                    fw.op("act", lambda g, cs=cs: g.activation(out=KT[:, cs], in_=y32[:, cs], func=AF.Copy),
                          reads=[b_y[blk]], writes=[b_KT[blk]])

        import os
        _nh = int(os.environ.get('DN_HEADS', '8'))
        _sub = os.environ.get('DN_SUB', '123')
        for h in range(_nh):
            fw.dma("sp", b_zs, writes=[b_zs], out=zs[:], in_=C.projD[24 + h, :, 0:OWN])
            fw.op("act", lambda g: g.activation(out=zs[:], in_=zs[:], func=AF.Silu), [b_zs], [b_zs])
            conv_silu(h, h, OWN + 2, OWN)
            l2norm(OWN, True)
            conv_silu(8 + h, 8 + h, SEQ, SEQ)
            l2norm(SEQ, False)
            for c0 in range(0, 32, 4):
                pi = C.next_ps()
                for q in range(4):
                    c = c0 + q
                    fw.op("pe", lambda g, pi=pi, q=q, c=c: g.transpose(
                        out=ps_tiles[pi][:, q * 128:(q + 1) * 128], in_=y32[:, c * 128:(c + 1) * 128], identity=ident[:]),
                        reads=[b_y[c // 4], C.b_ident], writes=[ps_bufs[pi]])
                for q in range(4):
                    c = c0 + q
                    fw.op("dve", lambda g, pi=pi, q=q, c=c: g.tensor_scalar(
                        out=kd[:, 16 + c, :], in0=ps_tiles[pi][:, q * 128:(q + 1) * 128], scalar1=sc_ekd[:, 16 + c, h:h + 1],
                        scalar2=None, op0=ALU.mult), reads=[ps_bufs[pi], b_sc], writes=[b_kd[16 + c]])
                    if c < 16:
                        fw.op("dve", lambda g, pi=pi, q=q, c=c: g.tensor_scalar(
                            out=kd[:, c, :], in0=ps_tiles[pi][:, q * 128:(q + 1) * 128], scalar1=sc_ekd[:, c, h:h + 1],
                            scalar2=None, op0=ALU.mult), reads=[ps_bufs[pi], b_sc], writes=[b_kd[c]])
            conv_silu(16 + h, 16 + h, SEQ, SEQ)
            for c0 in range(0, 32, 4):
                pi = C.next_ps()
                for q in range(4):
                    c = c0 + q
                    fw.op("pe", lambda g, pi=pi, q=q, c=c: g.transpose(
                        out=ps_tiles[pi][:, q * 128:(q + 1) * 128], in_=y32[:, c * 128:(c + 1) * 128], identity=ident[:]),
                        reads=[b_y[c // 4], C.b_ident], writes=[ps_bufs[pi]])
                evac(fw, _alt(ev[0]), Vtok[:, c0:c0 + 4, :], ps_tiles[pi][:, :].rearrange("p (c d) -> p c d", c=4),
                     [ps_bufs[pi]], [b_Vtok[c0 // 4]])
                ev[0] += 1

            for d in ((1, 0) if '2' in _sub else ()):
                for c0 in range(0, 32 if d == 1 else 16, 4):
                    dc0 = c0 if d == 0 else 16 + c0
                    has_out = c0 < 16
                    oc0 = c0 if d == 0 else 16 + c0
                    col = d * 8 + h
                    last = 127 if d == 0 else 0
                    pG = C.next_ps()
                    for q in range(4):
                        c = c0 + q
                        fw.op("pool", lambda g, q=q, c=c: g.tensor_scalar(
                            out=GB[:, q * 128:(q + 1) * 128], in0=ones_f[:], scalar1=graw[:, c, col:col + 1], scalar2=None,
                            op0=ALU.mult), reads=[b_ones, b_graw], writes=[b_GB])
                    for q in range(4):
                        fw.op("pe", lambda g, q=q, pG=pG: g.matmul(
                            out=ps_tiles[pG][:, q * 128:(q + 1) * 128], lhsT=GB[:, q * 128:(q + 1) * 128], rhs=TRI[:, d, :],
                            start=True, stop=True), reads=[b_GB, b_const], writes=[ps_bufs[pG]])
                    fw.op("dve", lambda g, pG=pG: g.tensor_tensor(out=tmpg[:], in0=ps_tiles[pG][:, :], in1=NEGI4[:, d, :],
                                                                 op=ALU.add), reads=[ps_bufs[pG], b_const], writes=[b_tmpg])
                    for q in range(4):
                        fw.op("act", lambda g, q=q: g.activation(
                            out=decT[:, q * 128:(q + 1) * 128], in_=tmpg[:, q * 128:(q + 1) * 128], func=AF.Exp,
                            bias=sc_negg[:, dc0 + q, h:h + 1]), reads=[b_tmpg, b_sc], writes=[b_decT])
                    fw.op("act", lambda g, pG=pG: g.activation(out=EGR[:], in_=ps_tiles[pG][:, :], func=AF.Exp),
                          reads=[ps_bufs[pG]], writes=[b_EGR])
                    fw.op("pool", lambda g: g.tensor_copy(
                        out=glb[:, dc0:dc0 + 4], in_=EGR[:, :].rearrange("p (c i) -> p c i", c=4)[:, :, last]),
                        reads=[b_EGR], writes=[b_glb[dc0 // 4]])
                    pK = C.next_ps()
                    for q in range(4):
                        c = c0 + q
                        fw.op("pe", lambda g, q=q, c=c, pK=pK: g.matmul(
                            out=ps_tiles[pK][:, q * 128:(q + 1) * 128], lhsT=KT[:, c * 128:(c + 1) * 128],
                            rhs=KT[:, c * 128:(c + 1) * 128], start=True, stop=True),
                            reads=[b_KT[c // 4]], writes=[ps_bufs[pK]])
                    for q in range(4):
                        c = c0 + q
                        fw.op("dve", lambda g, q=q, c=c, pK=pK: g.scalar_tensor_tensor(
                            out=m1[:, q * 128:(q + 1) * 128], in0=ps_tiles[pK][:, q * 128:(q + 1) * 128],
                            scalar=beta[:, c, col:col + 1], in1=decT[:, q * 128:(q + 1) * 128], op0=ALU.mult, op1=ALU.mult),
                            reads=[ps_bufs[pK], b_beta, b_decT], writes=[b_m1])
                    fw.op("pool", lambda g: g.tensor_tensor(out=Pm[0][:], in0=m1[:], in1=STR4[:, d, :], op=ALU.mult),
                          reads=[b_m1, b_const], writes=[b_Pm[0]])
                    if has_out:
                        pQ = C.next_ps()
                        for q in range(4):
                            c = c0 + q
                            fw.op("pe", lambda g, q=q, c=c, pQ=pQ: g.matmul(
                                out=ps_tiles[pQ][:, q * 128:(q + 1) * 128], lhsT=KT[:, c * 128:(c + 1) * 128],
                                rhs=QT[:, c * 128:(c + 1) * 128], start=True, stop=True),
                                reads=[b_KT[c // 4], b_QT[c // 4]], writes=[ps_bufs[pQ]])
                        fw.op("dve", lambda g, pQ=pQ: g.tensor_tensor(
                            out=AT[:, oc0:oc0 + 4, :], in0=ps_tiles[pQ][:, :].rearrange("p (c i) -> p c i", c=4),
                            in1=decT[:, :].rearrange("p (c i) -> p c i", c=4), op=ALU.mult),
                            reads=[ps_bufs[pQ], b_decT], writes=[b_AT[oc0 // 4]])
                        fw.op("pool", lambda g: g.tensor_tensor(
                            out=qgT[:, oc0:oc0 + 4, :], in0=QT[:, c0 * 128:(c0 + 4) * 128].rearrange("p (c i) -> p c i", c=4),
                            in1=EGR[:, :].rearrange("p (c i) -> p c i", c=4), op=ALU.mult),
                            reads=[b_QT[c0 // 4], b_EGR], writes=[b_qgT[oc0 // 4]])
                    pT = C.next_ps()
                    for q in range(4):
                        fw.op("pe", lambda g, q=q, pT=pT: g.transpose(
                            out=ps_tiles[pT][:, q * 128:(q + 1) * 128], in_=Pm[0][:, q * 128:(q + 1) * 128], identity=ident[:]),
                            reads=[b_Pm[0], C.b_ident], writes=[ps_bufs[pT]])
                    evac(fw, "act", PT[0][:], ps_tiles[pT][:, :], [ps_bufs[pT]], [b_PT[0]])
                    fw.op("pool", lambda g: g.tensor_tensor(out=Rn[0][:], in0=Pm[0][:], in1=I4[:], op=ALU.add),
                          reads=[b_Pm[0], b_const], writes=[b_Rn[0]])
                    cur = 0
                    for lvl in range(1, 7):
                        nxt = 1 - cur
                        if lvl < 6:
                            pA = C.next_ps()
                            for q in range(4):
                                qs = slice(q * 128, (q + 1) * 128)
                                fw.op("pe", lambda g, qs=qs, pA=pA, cur=cur: g.matmul(
                                    out=ps_tiles[pA][:, qs], lhsT=PT[cur][:, qs], rhs=Pm[cur][:, qs], start=True, stop=True),
                                    reads=[b_PT[cur], b_Pm[cur]], writes=[ps_bufs[pA]])
                        pB = C.next_ps()
                        for q in range(4):
                            qs = slice(q * 128, (q + 1) * 128)
                            fw.op("pe", lambda g, qs=qs, pB=pB, cur=cur: g.matmul(
                                out=ps_tiles[pB][:, qs], lhsT=Pm[cur][:, qs], rhs=PT[cur][:, qs], start=True, stop=True),
                                reads=[b_PT[cur], b_Pm[cur]], writes=[ps_bufs[pB]])
                        evac(fw, "act", PT[nxt][:], ps_tiles[pB][:, :], [ps_bufs[pB]], [b_PT[nxt]])
                        if lvl < 6:
                            evac(fw, "dve", Pm[nxt][:], ps_tiles[pA][:, :], [ps_bufs[pA]], [b_Pm[nxt]])
                        pC = C.next_ps()
                        for q in range(4):
                            qs = slice(q * 128, (q + 1) * 128)
                            fw.op("pe", lambda g, qs=qs, pC=pC, cur=cur, nxt=nxt: g.matmul(
                                out=ps_tiles[pC][:, qs], lhsT=PT[nxt][:, qs], rhs=Rn[cur][:, qs], start=True, stop=True),
                                reads=[b_PT[nxt], b_Rn[cur]], writes=[ps_bufs[pC]])
                        if lvl < 6:
                            fw.op("dve", lambda g, pC=pC, cur=cur, nxt=nxt: g.tensor_tensor(
                                out=Rn[nxt][:], in0=ps_tiles[pC][:, :], in1=Rn[cur][:], op=ALU.add),
                                reads=[ps_bufs[pC], b_Rn[cur]], writes=[b_Rn[nxt]])
                        else:
                            fw.op("dve", lambda g, pC=pC, cur=cur: g.tensor_tensor(
                                out=X[:, dc0:dc0 + 4, :], in0=ps_tiles[pC][:, :].rearrange("p (c i) -> p c i", c=4),
                                in1=Rn[cur][:, :].rearrange("p (c i) -> p c i", c=4), op=ALU.add),
                                reads=[ps_bufs[pC], b_Rn[cur]], writes=[b_X[dc0 // 4]])
                        cur = nxt

            it = 0
            for d in ((1, 0) if '3' in _sub else ()):
                col = d * 8 + h
                fw.op("pool", lambda g: g.memset(S32[:], 0.0), [], [b_S32])
                fw.op("pool", lambda g: g.memset(S16[:], 0.0), [], [b_S16])
                order = list(range(31, -1, -1)) if d == 1 else list(range(16))
                for c in order:
                    dc = c if d == 0 else 16 + c
                    oc = c if d == 0 else 16 + c
                    k = it % 2
                    it += 1
                    cs = slice(c * 128, (c + 1) * 128)
                    p1 = C.next_ps()
                    fw.op("pe", lambda g, p1=p1, cs=cs: g.matmul(out=ps_tiles[p1][:, 0:128], lhsT=KT[:, cs], rhs=S16[:],
                                                                 start=True, stop=True),
                          reads=[b_KT[c // 4], b_S16], writes=[ps_bufs[p1]])
                    fw.op("dve", lambda g, p1=p1, k=k, c=c, dc=dc: g.scalar_tensor_tensor(
                        out=Rt[k][:], in0=ps_tiles[p1][:, 0:128], scalar=sc_negeg[:, dc, h:h + 1], in1=Vtok[:, c, :],
                        op0=ALU.mult, op1=ALU.add), reads=[ps_bufs[p1], b_sc, b_Vtok[c // 4]], writes=[b_Rt[k]])
                    p2 = C.next_ps()
                    fw.op("pe", lambda g, p2=p2, k=k, dc=dc: g.matmul(out=ps_tiles[p2][:, 0:128], lhsT=X[:, dc, :], rhs=Rt[k][:],
                                                                      start=True, stop=True),
                          reads=[b_X[dc // 4], b_Rt[k]], writes=[ps_bufs[p2]])
                    fw.op("dve", lambda g, p2=p2, k=k, c=c: g.tensor_scalar(
                        out=vn[k][:], in0=ps_tiles[p2][:, 0:128], scalar1=beta[:, c, col:col + 1], scalar2=None,
                        op0=ALU.mult), reads=[ps_bufs[p2], b_beta], writes=[b_vn[k]])
                    if c < 16:
                        p3 = C.next_ps()
                        fw.op("pe", lambda g, p3=p3, oc=oc: g.matmul(out=ps_tiles[p3][:, 0:128], lhsT=qgT[:, oc, :], rhs=S16[:],
                                                                     start=True, stop=False),
                              reads=[b_qgT[oc // 4], b_S16], writes=[ps_bufs[p3]])
                        fw.op("pe", lambda g, p3=p3, oc=oc, k=k: g.matmul(out=ps_tiles[p3][:, 0:128], lhsT=AT[:, oc, :], rhs=vn[k][:],
                                                                          start=False, stop=True),
                              reads=[b_AT[oc // 4], b_vn[k]], writes=[ps_bufs[p3]])
                        if d == 1:
                            evac(fw, "act", obwd[:, c, :], ps_tiles[p3][:, 0:128], [ps_bufs[p3]], [b_obwd[c]])
                        else:
                            fw.op("dve", lambda g, p3=p3, k=k, c=c: g.tensor_tensor(
                                out=ot[k][:], in0=ps_tiles[p3][:, 0:128], in1=obwd[:, c, :], op=ALU.add),
                                reads=[ps_bufs[p3], b_obwd[c]], writes=[b_ot[k]])
                            fw.op("pool", lambda g, k=k: g.tensor_tensor(out=osq[k][:], in0=ot[k][:], in1=ot[k][:], op=ALU.mult),
                                  reads=[b_ot[k]], writes=[b_on[k]])
                            fw.op("dve", lambda g, k=k: g.reduce_sum(out=orr[k][:, 0:1], in_=osq[k][:], axis=AX.X),
                                  reads=[b_on[k]], writes=[b_on[k]])
                            fw.op("dve", lambda g, k=k: g.tensor_scalar(out=orr[k][:, 1:2], in0=orr[k][:, 0:1], scalar1=1.0 / 128,
                                                                        scalar2=RMS_EPS, op0=ALU.mult, op1=ALU.add),
                                  reads=[b_on[k]], writes=[b_on[k]])
                            rsqrt_inplace(fw, orr[k][:, 1:2], b_on[k])
                            evac(fw, "dve", orr[k][:, 0:1], orr[k][:, 1:2], [b_on[k]], [b_on[k]])
                            fw.op("dve", lambda g, k=k: g.scalar_tensor_tensor(
                                out=on_[k][:], in0=ot[k][:], scalar=orr[k][:, 0:1], in1=normw[:], op0=ALU.mult, op1=ALU.mult),
                                reads=[b_ot[k], b_on[k], b_const], writes=[b_on[k]])
                            p5 = C.next_ps()
                            fw.op("pe", lambda g, p5=p5, k=k: g.transpose(out=ps_tiles[p5][:, 0:128], in_=on_[k][:], identity=ident[:]),
                                  reads=[b_on[k], C.b_ident], writes=[ps_bufs[p5]])
                            fw.op("dve", lambda g, p5=p5, k=k, cs=cs: g.tensor_tensor(
                                out=mo[k][:], in0=ps_tiles[p5][:, 0:128], in1=zs[:, cs], op=ALU.mult),
                                reads=[ps_bufs[p5], b_zs], writes=[b_mo[k]])
                            fw.dma("sp", b_mo[k], reads=[b_mo[k]], out=C.mixT_d[:, 8 + h, cs], in_=mo[k][:])
                    p4 = C.next_ps()
                    fw.op("pe", lambda g, p4=p4, dc=dc, k=k: g.matmul(out=ps_tiles[p4][:, 0:128], lhsT=kd[:, dc, :], rhs=vn[k][:],
                                                                      start=True, stop=True),
                          reads=[b_kd[dc], b_vn[k]], writes=[ps_bufs[p4]])
                    fw.op("dve", lambda g, p4=p4, dc=dc: g.scalar_tensor_tensor(
                        out=S32[:], in0=S32[:], scalar=glb[:, dc:dc + 1], in1=ps_tiles[p4][:, 0:128], op0=ALU.mult, op1=ALU.add),
                        reads=[b_S32, b_glb[dc // 4], ps_bufs[p4]], writes=[b_S32])
                    evac(fw, "act", S16[:], S32[:], [b_S32], [b_S16])
        fw.barrier()


def layer_norm(fw, t, st, sqs, gbc, bbc, b_t, b_st, b_sqs, b_gb, e_sq="pool", e_add="pool"):
    fw.op("dve", lambda g: g.reduce_sum(out=st[:, 0:1], in_=t, axis=AX.X), [b_t], [b_st])
    fw.op("dve", lambda g: g.tensor_scalar(out=st[:, 1:2], in0=st[:, 0:1], scalar1=-1.0 / D, scalar2=None, op0=ALU.mult),
          [b_st], [b_st])
    fw.op("dve", lambda g: g.tensor_scalar(out=t, in0=t, scalar1=st[:, 1:2], scalar2=None, op0=ALU.add), [b_t, b_st], [b_t])
    if e_sq == "act":
        fw.op("act", lambda g: g.activation(out=sqs, in_=t, func=AF.Square), [b_t], [b_sqs])
    else:
        fw.op(e_sq, lambda g: g.tensor_tensor(out=sqs, in0=t, in1=t, op=ALU.mult), [b_t], [b_sqs])
    fw.op("dve", lambda g: g.reduce_sum(out=st[:, 2:3], in_=sqs, axis=AX.X), [b_sqs], [b_st])
    fw.op("dve", lambda g: g.tensor_scalar(out=st[:, 3:4], in0=st[:, 2:3], scalar1=1.0 / D, scalar2=LN_EPS, op0=ALU.mult,
                                           op1=ALU.add), [b_st], [b_st])
    rsqrt_inplace(fw, st[:, 3:4], b_st)
    evac(fw, "dve", st[:, 2:3], st[:, 3:4], [b_st], [b_st])
    fw.op("dve", lambda g: g.scalar_tensor_tensor(out=t, in0=t, scalar=st[:, 2:3], in1=gbc, op0=ALU.mult, op1=ALU.mult),
          [b_t, b_st, b_gb], [b_t])
    fw.op(e_add, lambda g: g.tensor_tensor(out=t, in0=t, in1=bbc, op=ALU.add), [b_t, b_gb], [b_t])


def phase_out(C):
    nc, fw, ps_tiles, ps_bufs = C.nc, C.fw, C.ps_tiles, C.ps_bufs
    with ExitStack() as ph:
        psb = lambda n, s, dt=F32: ph.enter_context(nc.sbuf_tensor(n, list(s), dt))
        wo = psb("wo", [128, 16, D], BF16)
        b_wo = fw.buf("wo")
        for j in range(16):
            fw.dma("sp", b_wo, writes=[b_wo], out=wo[:, :, j * 128:(j + 1) * 128], in_=C.wo_bf[j, :, :, :])
        g1 = psb("g1", [128, D])
        b1 = psb("b1", [128, D])
        b_gb = fw.buf("gb1")
        fw.dma("sp", b_gb, writes=[b_gb], out=g1[:], in_=C.ln1g_d[:, :])
        fw.dma("sp", b_gb, writes=[b_gb], out=b1[:], in_=C.ln1b_d[:, :])
        mt = [psb("mt%d" % i, [128, 16, 128], BF16) for i in range(2)]
        xt = [psb("xt%d" % i, [128, D]) for i in range(2)]
        tl = [psb("tl%d" % i, [128, D]) for i in range(2)]
        hTs = [psb("hTs%d" % i, [128, 16, 128], BF16) for i in range(2)]
        st = [psb("st%d" % i, [128, 4]) for i in range(2)]
        sqs = psb("sqs", [128, D])
        b_mt = [fw.buf("mt%d" % i) for i in range(2)]
        b_xt = [fw.buf("xt%d" % i) for i in range(2)]
        b_tl = [fw.buf("tl%d" % i) for i in range(2)]
        b_hTs = [fw.buf("hTs%d" % i) for i in range(2)]
        b_st = [fw.buf("lst%d" % i) for i in range(2)]
        b_sqs = fw.buf("sqs")
        for tt in range(16):
            k = tt % 2
            ts_ = slice(tt * 128, (tt + 1) * 128)
            fw.dma("sp", b_mt[k], writes=[b_mt[k]], out=mt[k][:], in_=C.mixT_d[:, :, ts_])
            fw.dma("sp", b_xt[k], writes=[b_xt[k]], out=xt[k][:], in_=C.x[ts_, :])
            banks = [C.next_ps() for _ in range(4)]
            for nb in range(4):
                for kt in range(16):
                    fw.op("pe", lambda g, nb=nb, kt=kt, k=k: g.matmul(
                        out=ps_tiles[banks[nb]][:, :], lhsT=mt[k][:, kt, :], rhs=wo[:, kt, nb * 512:(nb + 1) * 512],
                        start=(kt == 0), stop=(kt == 15)), reads=[b_mt[k], b_wo], writes=[ps_bufs[banks[nb]]])
            for nb in range(4):
                cs = slice(nb * 512, (nb + 1) * 512)
                fw.op("dve", lambda g, nb=nb, cs=cs, k=k: g.scalar_tensor_tensor(
                    out=tl[k][:, cs], in0=xt[k][:, cs], scalar=ALPHA, in1=ps_tiles[banks[nb]][:, :], op0=ALU.mult, op1=ALU.add),
                    reads=[b_xt[k], ps_bufs[banks[nb]]], writes=[b_tl[k]])
            layer_norm(fw, tl[k][:], st[k], sqs[:], g1[:], b1[:], b_tl[k], b_st[k], b_sqs, b_gb)
            fw.dma("pool", b_tl[k], reads=[b_tl[k]], out=C.h_d[ts_, :], in_=tl[k][:])
            for q4 in range(4):
                pi = C.next_ps()
                for q in range(4):
                    dt = q4 * 4 + q
                    fw.op("pe", lambda g, pi=pi, q=q, dt=dt, k=k: g.transpose(
                        out=ps_tiles[pi][:, q * 128:(q + 1) * 128], in_=tl[k][:, dt * 128:(dt + 1) * 128], identity=C.ident[:]),
                        reads=[b_tl[k], C.b_ident], writes=[ps_bufs[pi]])
                evac(fw, "act", hTs[k][:, q4 * 4:(q4 + 1) * 4, :], ps_tiles[pi][:, :].rearrange("p (a b) -> p a b", a=4),
                     [ps_bufs[pi]], [b_hTs[k]])
            fw.dma("pool", b_hTs[k], reads=[b_hTs[k]], out=C.hT_d[:, :, ts_], in_=hTs[k][:])
        fw.barrier()


def phase_peer(C):
    nc, fw, ps_tiles, ps_bufs = C.nc, C.fw, C.ps_tiles, C.ps_bufs
    NEGBIG = -1.0e30
    with ExitStack() as ph:
        psb = lambda n, s, dt=F32: ph.enter_context(nc.sbuf_tensor(n, list(s), dt))
        idx = psb("idx", [128, 16, 128], U32)
        gate = psb("gate", [128, 16, 128])
        b_idx = [fw.buf("idx%d" % i) for i in range(16)]
        b_gate = [fw.buf("gate%d" % i) for i in range(16)]
        with ExitStack() as pa:
            pab = lambda n, s, dt=F32: pa.enter_context(nc.sbuf_tensor(n, list(s), dt))
            wq = pab("wq", [128, 16, D], BF16)
            b_wq = fw.buf("wq")
            for j in range(16):
                fw.dma("sp", b_wq, writes=[b_wq], out=wq[:, :, j * 128:(j + 1) * 128], in_=C.wq_bf[j, :, :, :])
            keysf = pab("keysf", [128, 2, 128])
            keysb = pab("keysb", [128, 2, 128], BF16)
            iota = pab("iota_sb", [128, 8, 16, 16])
            iotam = pab("iotam_sb", [128, 8, 16, 16])
            eq2 = pab("eq2", [128, 8, 16, 16])
            b_keys, b_iota = fw.buf("keys"), fw.buf("iota")
            fw.dma("sp", b_keys, writes=[b_keys], out=keysf[:], in_=C.keysT_d.rearrange("p d k -> d p k"))
            fw.op("dve", lambda g: g.tensor_copy(out=keysb[:], in_=keysf[:]), [b_keys], [b_keys])
            fw.dma("sp", b_iota, writes=[b_iota], out=iota[:], in_=C.iota_d[:, :, :, :])
            fw.dma("sp", b_iota, writes=[b_iota], out=iotam[:], in_=C.iotam_d[:, :, :, :])
            hTt = [pab("hTt%d" % i, [128, 16, 128], BF16) for i in range(2)]
            b_hTt = [fw.buf("hTt%d" % i) for i in range(2)]
            qT = pab("qTp", [128, 16, 128], BF16)
            sc = pab("sc", [128, 16, 128])
            sc2 = pab("sc2", [128, 16, 128])
            sv = pab("sv", [128, 16, 16])
            si = pab("si", [128, 16, 16], U32)
            sif = pab("sif", [128, 16, 16])
            cand = pab("cand", [128, 8, 256])
            cand2 = pab("cand2", [128, 8, 256])
            tsv = pab("tsv", [128, 8, 16])
            tpos = pab("tpos", [128, 8, 16], U32)
            posf = pab("posf", [128, 8, 16])
            cf = pab("cf", [128, 8, 16])
            rf = pab("rf", [128, 8, 16])
            eq = pab("eq", [128, 8, 16, 16])
            i1 = pab("i1", [128, 8, 16])
            i2 = pab("i2", [128, 8, 16])
            ef = pab("ef", [128, 8, 16])
            esm = pab("esm", [128, 8, 16])
            ssum = pab("ssum", [128, 8])
            b_qT = [fw.buf("qTp%d" % i) for i in range(4)]
            b_sc = [fw.buf("sc%d" % i) for i in range(4)]
            b_tk = fw.buf("tk")
            for tt in range(16):
                k = tt % 2
                ts_ = slice(tt * 128, (tt + 1) * 128)
                fw.dma("sp", b_hTt[k], writes=[b_hTt[k]], out=hTt[k][:], in_=C.hT_d[:, :, ts_])
                for q4 in range(4):
                    pi = C.next_ps()
                    for q in range(4):
                        hp = q4 * 4 + q
                        for dt in range(16):
                            fw.op("pe", lambda g, pi=pi, q=q, hp=hp, dt=dt, k=k: g.matmul(
                                out=ps_tiles[pi][:, q * 128:(q + 1) * 128], lhsT=wq[:, dt, hp * 128:(hp + 1) * 128],
                                rhs=hTt[k][:, dt, :], start=(dt == 0), stop=(dt == 15)),
                                reads=[b_wq, b_hTt[k]], writes=[ps_bufs[pi]])
                    evac(fw, "act", qT[:, q4 * 4:(q4 + 1) * 4, :], ps_tiles[pi][:, :].rearrange("p (a b) -> p a b", a=4),
                         [ps_bufs[pi]], [b_qT[q4]])
                for q4 in range(4):
                    pi = C.next_ps()
                    for q in range(4):
                        hp = q4 * 4 + q
                        fw.op("pe", lambda g, pi=pi, q=q, hp=hp: g.matmul(
                            out=ps_tiles[pi][:, q * 128:(q + 1) * 128], lhsT=qT[:, hp, :], rhs=keysb[:, hp % 2, :],
                            start=True, stop=True), reads=[b_qT[q4], b_keys], writes=[ps_bufs[pi]])
                    evac(fw, "act", sc[:, q4 * 4:(q4 + 1) * 4, :], ps_tiles[pi][:, :].rearrange("p (a b) -> p a b", a=4),
                         [ps_bufs[pi]], [b_sc[q4]])
                T = [b_tk]
                for hp in range(16):
                    R_ = [b_sc[hp // 4], b_tk]
                    fw.op("dve", lambda g, hp=hp: g.max(out=sv[:, hp, 0:8], in_=sc[:, hp, :]), R_, T)
                    fw.op("dve", lambda g, hp=hp: g.max_index(out=si[:, hp, 0:8], in_max=sv[:, hp, 0:8], in_values=sc[:, hp, :]), R_, T)
                    fw.op("dve", lambda g, hp=hp: g.match_replace(out=sc2[:, hp, :], in_to_replace=sv[:, hp, 0:8],
                                                                  in_values=sc[:, hp, :], imm_value=NEGBIG), R_, T)
                    fw.op("dve", lambda g, hp=hp: g.max(out=sv[:, hp, 8:16], in_=sc2[:, hp, :]), R_, T)
                    fw.op("dve", lambda g, hp=hp: g.max_index(out=si[:, hp, 8:16], in_max=sv[:, hp, 8:16], in_values=sc2[:, hp, :]), R_, T)
                fw.op("dve", lambda g: g.tensor_copy(out=sif[:], in_=si[:]), T, T)
                sv4 = sv[:, :, :].rearrange("p (h two) r -> p h two r", two=2)
                sif4 = sif[:, :, :].rearrange("p (h two) r -> p h two r", two=2)
                cand4 = cand[:, :, :].rearrange("p h (r c) -> p h r c", c=16)
                fw.op("dve", lambda g: g.tensor_tensor(
                    out=cand4, in0=sv4[:, :, 0, :].unsqueeze(3).to_broadcast([128, 8, 16, 16]),
                    in1=sv4[:, :, 1, :].unsqueeze(2).to_broadcast([128, 8, 16, 16]), op=ALU.add), T, T)
                for hd in range(8):
                    fw.op("dve", lambda g, hd=hd: g.max(out=tsv[:, hd, 0:8], in_=cand[:, hd, :]), T, T)
                    fw.op("dve", lambda g, hd=hd: g.max_index(out=tpos[:, hd, 0:8], in_max=tsv[:, hd, 0:8], in_values=cand[:, hd, :]), T, T)
                    fw.op("dve", lambda g, hd=hd: g.match_replace(out=cand2[:, hd, :], in_to_replace=tsv[:, hd, 0:8],
                                                                  in_values=cand[:, hd, :], imm_value=NEGBIG), T, T)
                    fw.op("dve", lambda g, hd=hd: g.max(out=tsv[:, hd, 8:16], in_=cand2[:, hd, :]), T, T)
                    fw.op("dve", lambda g, hd=hd: g.max_index(out=tpos[:, hd, 8:16], in_max=tsv[:, hd, 8:16], in_values=cand2[:, hd, :]), T, T)
                fw.op("dve", lambda g: g.tensor_tensor(out=esm[:], in0=tsv[:], in1=tsv[:, :, 0:1].to_broadcast([128, 8, 16]),
                                                       op=ALU.subtract), T, T)
                fw.op("act", lambda g: g.activation(out=esm[:], in_=esm[:], func=AF.Exp), T, T)
                fw.op("dve", lambda g: g.reduce_sum(out=ssum[:], in_=esm[:], axis=AX.X), T, T)
                fw.op("dve", lambda g: g.reciprocal(out=ssum[:], in_=ssum[:]), T, T)
                fw.op("dve", lambda g, tt=tt: g.tensor_tensor(
                    out=gate[:, tt, :].rearrange("p (h r) -> p h r", r=16), in0=esm[:],
                    in1=ssum[:, :].unsqueeze(2).to_broadcast([128, 8, 16]), op=ALU.mult), T, [b_tk, b_gate[tt]])
                fw.op("dve", lambda g: g.tensor_copy(out=posf[:], in_=tpos[:]), T, T)
                fl = lambda t: t[:, :, :, :].rearrange("p h a b -> p (h a b)")
                f3 = lambda t: t[:, :, :].rearrange("p h r -> p (h r)")
                fw.op("dve", lambda g: g.tensor_tensor(
                    out=eq[:], in0=posf[:, :, :].unsqueeze(3).to_broadcast([128, 8, 16, 16]), in1=iotam[:], op=ALU.subtract),
                    T + [b_iota], T)
                fw.op("dve", lambda g: g.tensor_scalar(out=fl(eq2), in0=fl(eq), scalar1=0.0, scalar2=None, op0=ALU.is_ge), T, T)
                fw.op("dve", lambda g: g.scalar_tensor_tensor(out=fl(eq), in0=fl(eq), scalar=15.0, in1=fl(eq2), op0=ALU.is_le,
                                                              op1=ALU.mult), T, T)
                fw.op("dve", lambda g: g.tensor_tensor(out=eq2[:], in0=eq[:], in1=iota[:], op=ALU.mult), T + [b_iota], T)
                fw.op("dve", lambda g: g.reduce_sum(out=rf[:], in_=eq2[:], axis=AX.X), T, T)
                fw.op("dve", lambda g: g.tensor_tensor(
                    out=eq2[:], in0=eq[:], in1=sif4[:, :, 0, :].unsqueeze(2).to_broadcast([128, 8, 16, 16]), op=ALU.mult), T, T)
                fw.op("dve", lambda g: g.reduce_sum(out=i1[:], in_=eq2[:], axis=AX.X), T, T)
                fw.op("dve", lambda g: g.scalar_tensor_tensor(out=f3(cf), in0=f3(rf), scalar=-16.0, in1=f3(posf), op0=ALU.mult,
                                                              op1=ALU.add), T, T)
                fw.op("dve", lambda g: g.tensor_tensor(
                    out=eq[:], in0=cf[:, :, :].unsqueeze(3).to_broadcast([128, 8, 16, 16]), in1=iota[:], op=ALU.is_equal),
                    T + [b_iota], T)
                fw.op("dve", lambda g: g.tensor_tensor(
                    out=eq2[:], in0=eq[:], in1=sif4[:, :, 1, :].unsqueeze(2).to_broadcast([128, 8, 16, 16]), op=ALU.mult), T, T)
                fw.op("dve", lambda g: g.reduce_sum(out=i2[:], in_=eq2[:], axis=AX.X), T, T)
                fw.op("dve", lambda g: g.scalar_tensor_tensor(
                    out=ef[:, :, :].rearrange("p h r -> p (h r)"), in0=i1[:, :, :].rearrange("p h r -> p (h r)"), scalar=128.0,
                    in1=i2[:, :, :].rearrange("p h r -> p (h r)"), op0=ALU.mult, op1=ALU.add), T, T)
                fw.op("dve", lambda g, tt=tt: g.tensor_copy(out=idx[:, tt, :], in_=ef[:, :, :].rearrange("p h r -> p (h r)")),
                      T, [b_tk, b_idx[tt]])
            fw.barrier()

        with ExitStack() as pb:
            pbb = lambda n, s, dt=F32: pb.enter_context(nc.sbuf_tensor(n, list(s), dt))
            g2 = pbb("g2", [128, D])
            b2 = pbb("b2", [128, D])
            b_gb = fw.buf("gb2")
            fw.dma("sp", b_gb, writes=[b_gb], out=g2[:], in_=C.ln2g_d[:, :])
            fw.dma("sp", b_gb, writes=[b_gb], out=b2[:], in_=C.ln2b_d[:, :])
            NU, NV = 8, 8
            ht = [pbb("ht%d" % i, [128, D]) for i in range(2)]
            ug = [pbb("ug%d" % i, [128, D], BF16) for i in range(NU)]
            vg = [pbb("vg%d" % i, [128, D], BF16) for i in range(NV)]
            dg = [pbb("dg%d" % i, [128, 128], BF16) for i in range(NV)]
            junk = pbb("junk", [128, D], BF16)
            av = pbb("av", [128, 128])
            wgt = pbb("wgt", [128, 128])
            tl2 = pbb("tl2", [128, D])
            sqs = pbb("sqs2", [128, D])
            st = pbb("st2", [128, 4])
            b_ht = [fw.buf("ht%d" % i) for i in range(2)]
            b_ug = [fw.buf("ug%d" % i) for i in range(NU)]
            b_vg = [fw.buf("vg%d" % i) for i in range(NV)]
            b_dg = [fw.buf("dg%d" % i) for i in range(NV)]
            b_junk, b_av, b_wgt, b_tl2, b_sqs, b_st = (fw.buf(n) for n in ("junk", "av", "wgt", "tl2", "sqs2", "st2"))
            acc_banks = [4, 5, 6, 7]
            ui = vi = 0
            for tt in range(16):
                k = tt % 2
                ts_ = slice(tt * 128, (tt + 1) * 128)
                fw.dma("sp", b_ht[k], writes=[b_ht[k]], out=ht[k][:], in_=C.h_d[ts_, :])
                for sl in range(128):
                    u = ui % NU
                    ui += 1
                    fw.dma("pool", b_ug[u], reads=[b_idx[tt]], writes=[b_ug[u]],
                           fn=lambda g, u=u, sl=sl, tt=tt: g.indirect_dma_start(
                               out=ug[u][:], out_offset=None, in_=C.u_bf[:, :],
                               in_offset=bass.IndirectOffsetOnAxis(ap=idx[:, tt, sl:sl + 1], axis=0)))
                    fw.op("dve", lambda g, u=u, sl=sl, k=k: g.scalar_tensor_tensor(
                        out=junk[:], in0=ug[u][:], scalar=1.0, in1=ht[k][:], op0=ALU.mult, op1=ALU.mult,
                        accum_out=av[:, sl:sl + 1]), reads=[b_ug[u], b_ht[k]], writes=[b_junk, b_av])
                fw.op("act", lambda g: g.activation(out=wgt[:], in_=av[:], func=AF.Gelu), [b_av], [b_wgt])
                fw.op("dve", lambda g, tt=tt: g.tensor_tensor(out=wgt[:], in0=wgt[:], in1=gate[:, tt, :], op=ALU.mult),
                      [b_wgt, b_gate[tt]], [b_wgt])
                for sl in range(128):
                    v = vi % NV
                    vi += 1
                    fw.dma("pool", b_vg[v], reads=[b_idx[tt]], writes=[b_vg[v]],
                           fn=lambda g, v=v, sl=sl, tt=tt: g.indirect_dma_start(
                               out=vg[v][:], out_offset=None, in_=C.v_bf[:, :],
                               in_offset=bass.IndirectOffsetOnAxis(ap=idx[:, tt, sl:sl + 1], axis=0)))
                    fw.op("dve", lambda g, v=v, sl=sl: g.tensor_scalar(
                        out=dg[v][:], in0=C.ident[:], scalar1=wgt[:, sl:sl + 1], scalar2=None, op0=ALU.mult),
                        reads=[C.b_ident, b_wgt], writes=[b_dg[v]])
                    for nb in range(4):
                        fw.op("pe", lambda g, nb=nb, v=v, sl=sl: g.matmul(
                            out=ps_tiles[acc_banks[nb]][:, :], lhsT=dg[v][:], rhs=vg[v][:, nb * 512:(nb + 1) * 512],
                            start=(sl == 0), stop=(sl == 127)), reads=[b_dg[v], b_vg[v]], writes=[ps_bufs[acc_banks[nb]]])
                for nb in range(4):
                    cs = slice(nb * 512, (nb + 1) * 512)
                    fw.op("dve", lambda g, nb=nb, cs=cs, k=k: g.scalar_tensor_tensor(
                        out=tl2[:, cs], in0=ht[k][:, cs], scalar=ALPHA, in1=ps_tiles[acc_banks[nb]][:, :], op0=ALU.mult,
                        op1=ALU.add), reads=[b_ht[k], ps_bufs[acc_banks[nb]]], writes=[b_tl2])
                layer_norm(fw, tl2[:], st, sqs[:], g2[:], b2[:], b_tl2, b_st, b_sqs, b_gb, e_sq="act", e_add="dve")
                fw.dma("sp", b_tl2, reads=[b_tl2], out=C.y[ts_, :], in_=tl2[:])
        fw.barrier()


def build(stop_after=None, dbg=()):
    nc = bass.Bass("TRN2", target_bir_lowering=False)
    C = Ctx()
    C.nc = nc
    dram_in = lambda n, s, dt=F32: nc.dram_tensor(n, list(s), dt, kind="ExternalInput").ap()
    C.x = dram_in("x", [SEQ, D])
    C.w_in = dram_in("w_in", [D, INW])
    C.w_out = dram_in("w_out", [D, D])
    C.peer_wq = dram_in("peer_wq", [D, D])
    C.ident_d = dram_in("ident", [128, 128])
    C.biasT_d = dram_in("biasT", [6, 128, 512])
    C.sink_d = dram_in("sink_bc", [128, 2, 512])
    C.TRI_d = dram_in("TRI", [128, 2, 128])
    C.AFT_d = dram_in("AFT", [128, 2, 128])
    C.NEGI4_d = dram_in("NEGI4", [128, 2, 512])
    C.STR4_d = dram_in("STR4", [128, 2, 512])
    C.I4_d = dram_in("I4", [128, 512])
    C.cwT_d = dram_in("cwT", [24, 128, 5])
    C.normw_d = dram_in("normw_bc", [128, 128])
    C.dtb_d = dram_in("dtb_bc", [128, 32, 16])
    C.alog_d = dram_in("alog_bc", [128, 32, 16])
    C.ln1g_d = dram_in("ln1g_bc", [128, D])
    C.ln1b_d = dram_in("ln1b_bc", [128, D])
    C.ln2g_d = dram_in("ln2g_bc", [128, D])
    C.ln2b_d = dram_in("ln2b_bc", [128, D])
    C.keysT_d = dram_in("keysT", [2, 128, 128])
    C.iota_d = dram_in("iota16", [128, 8, 16, 16])
    C.iotam_d = dram_in("iota16m", [128, 8, 16, 16])
    C.peer_u = C.peer_v = None
    if stop_after is None:
        C.peer_u = dram_in("peer_u", [16384, D])
        C.peer_v = dram_in("peer_v", [16384, D])
    C.y = nc.dram_tensor("y", [OWN, D], F32, kind="ExternalOutput").ap()

    def scratch(n, s, dt=F32):
        if n in dbg:
            return nc.dram_tensor(n, list(s), dt, kind="ExternalOutput").ap()
        return nc.dram_tensor(n, list(s), dt).ap()

    C.wi_bf = scratch("wi_bf", [45, 128, 16, 128], BF16)
    C.wo_bf = scratch("wo_bf", [16, 128, 16, 128], BF16)
    C.wq_bf = scratch("wq_bf", [16, 128, 16, 128], BF16)
    C.projA = scratch("projA", [10, 128, 2560], BF16)
    C.projV = scratch("projV", [2, 128, 2560], F32)
    C.projD = scratch("projD", [32, 128, SEQ], F32)
    C.gates = scratch("gates_d", [32, SEQ], F32)
    C.mixT_d = scratch("mixT_d", [128, 16, OWN], BF16)
    C.h_d = scratch("h_d", [OWN, D], F32)
    C.u_bf = scratch("u_bf", [16384, D], BF16)
    C.v_bf = scratch("v_bf", [16384, D], BF16)
    C.hT_d = scratch("hT_d", [128, 16, OWN], BF16)

    with ExitStack() as es:
        fw = FW(nc, es)
        C.fw = fw
        C.es = es
        sb = lambda n, s, dt=F32: es.enter_context(nc.sbuf_tensor(n, list(s), dt))
        C.ps_tiles = [es.enter_context(nc.psum_tensor("ps%d" % i, [128, 512], F32)) for i in range(8)]
        C.ps_bufs = [fw.buf("ps%d" % i) for i in range(8)]
        C.ps_i = 0

        C.ps_lim = 8

        def next_ps():
            C.ps_i += 1
            return (C.ps_i - 1) % C.ps_lim
        C.next_ps = next_ps
        C.ident = sb("ident_sb", [128, 128])
        C.b_ident = fw.buf("ident")
        fw.dma("sp", C.b_ident, writes=[C.b_ident], out=C.ident[:], in_=C.ident_d[:, :])

        def finish():
            fw.barrier()
            return nc

        import os
        skip = os.environ.get("SKIP", "").split(",")
        if "PW" not in skip:
            phase_w(C)
        if stop_after == "PW":
            return finish()
        if "P1" not in skip:
            phase_proj(C)
        if stop_after == "P1":
            return finish()
        if "P2" not in skip:
            phase_attn(C)
        if stop_after == "P2":
            return finish()
        if "P3" not in skip:
            phase_dn(C)
        if stop_after == "P3":
            return finish()
        if "P4" not in skip:
            phase_out(C)
        if stop_after == "P4":
            return finish()
        C.ps_lim = 4
        phase_peer(C)
        return finish()


def _t5_bucket(rel):
    nb, me = 16, 8
    ret = np.where(rel > 0, nb, 0)
    n = np.abs(rel)
    large = me + (np.log(np.maximum(n, 1).astype(np.float32) / np.float32(me))
                  / np.float32(np.log(128.0 / me)) * np.float32(nb - me)).astype(np.int32)
    large = np.minimum(large, nb - 1)
    return ret + np.where(n < me, n, large)


def _bias_table(rel_bias, flip):
    out = np.empty((2, 3, 128, 4, 128), np.float32)
    kk = np.arange(128)[:, None]
    qq = np.arange(128)[None, :]
    for r in range(3):
        d = (r - 1) * 128 + kk - qq
        true_rel = -d if flip else d
        bk = _t5_bucket(true_rel)
        inside = np.abs(d) <= 128
        for g in range(2):
            for hh in range(4):
                out[g, r, :, hh, :] = np.where(inside, rel_bias[bk, 4 * g + hh], np.float32(NEG))
    return out.reshape(6, 128, 512)


def prep_shared(inp):
    sh = {}
    sh["ident"] = np.eye(128, dtype=np.float32)
    sh["w_out"] = np.ascontiguousarray(inp["w_out"][0])
    sh["peer_wq"] = np.ascontiguousarray(inp["peer_wq"][0])
    sink = inp["attn_sink"][0]
    sh["sink_bc"] = np.ascontiguousarray(np.broadcast_to(
        sink.reshape(1, 2, 4, 1), (128, 2, 4, 128)).reshape(128, 2, 512)).astype(np.float32)
    ii = np.arange(128)
    TRI = np.zeros((128, 2, 128), np.float32)
    AFT = np.zeros((128, 2, 128), np.float32)
    NEGI4 = np.zeros((128, 2, 512), np.float32)
    STR4 = np.zeros((128, 2, 512), np.float32)
    for d in range(2):
        prec_eq = (ii[:, None] <= ii[None, :]) if d == 0 else (ii[:, None] >= ii[None, :])
        prec = (ii[:, None] < ii[None, :]) if d == 0 else (ii[:, None] > ii[None, :])
        TRI[:, d, :] = prec_eq
        AFT[:, d, :] = prec.T
        NEGI4[:, d, :] = np.tile(np.where(prec_eq, 0.0, NEG), (1, 4))
        STR4[:, d, :] = -np.tile(prec, (1, 4)).astype(np.float32)
    sh["TRI"], sh["AFT"], sh["NEGI4"], sh["STR4"] = TRI, AFT, NEGI4, STR4
    sh["I4"] = np.tile(np.eye(128, dtype=np.float32), (1, 4))
    for nm in ("ln1_g", "ln1_b", "ln2_g", "ln2_b"):
        sh[nm.replace("_", "") + "_bc"] = np.ascontiguousarray(np.broadcast_to(inp[nm][0][None, :], (128, D))).astype(np.float32)
    sh["keysT"] = np.ascontiguousarray(inp["peer_keys"][0].transpose(0, 2, 1)).astype(np.float32)
    sh["iota16"] = np.ascontiguousarray(np.broadcast_to(np.arange(16, dtype=np.float32), (128, 8, 16, 16)))
    sh["iota16m"] = np.ascontiguousarray(np.broadcast_to(16.0 * np.arange(16, dtype=np.float32), (128, 8, 16, 16)))
    sh["peer_u"] = np.ascontiguousarray(inp["peer_u"][0])
    sh["peer_v"] = np.ascontiguousarray(inp["peer_v"][0])
    sh["normw_bc"] = np.ascontiguousarray(np.broadcast_to(inp["dn_norm_w"][0][None, :], (128, 128))).astype(np.float32)
    return sh


def prep_core(inp, c, sh):
    b, half = c // 2, c % 2
    flip = half == 1
    m = dict(sh)
    xs = inp["x"][b]
    w_in = inp["w_in"][0]
    if flip:
        xs = xs[::-1]
        perm = np.arange(INW)
        perm[5632:5640], perm[5640:5648] = np.arange(5640, 5648), np.arange(5632, 5640)
        perm[5648:5656], perm[5656:5664] = np.arange(5656, 5664), np.arange(5648, 5656)
        w_in = w_in[:, perm]
    m["x"] = np.ascontiguousarray(xs)
    m["w_in"] = np.ascontiguousarray(w_in)
    m["biasT"] = _bias_table(inp["rel_bias"], flip)
    cw = inp["conv_w"][0]
    a_log = inp["a_log"][0]
    dtb = inp["dt_bias"][0]
    if flip:
        cw = cw[::-1]
        a_log = a_log[::-1]
        dtb = dtb[::-1]
    m["cwT"] = np.ascontiguousarray(cw.T.reshape(24, 128, 5)).astype(np.float32)
    m["dtb_bc"] = np.ascontiguousarray(np.broadcast_to(dtb.reshape(1, 1, 16), (128, 32, 16))).astype(np.float32)
    m["alog_bc"] = np.ascontiguousarray(np.broadcast_to(a_log.reshape(1, 1, 16), (128, 32, 16))).astype(np.float32)
    return m


_NC_CACHE = {}


def kernel(**inputs):
    inp = {k: np.asarray(v) for k, v in inputs.items()}
    sh = prep_shared(inp)
    in_maps = [prep_core(inp, c, sh) for c in range(8)]
    if "nc" not in _NC_CACHE:
        _NC_CACHE["nc"] = build()
    nc = _NC_CACHE["nc"]
    res = run_bass_kernel_spmd(nc, in_maps, core_ids=list(range(8)))
    out = np.empty((4, SEQ, D), np.float32)
    for c in range(8):
        yc = np.asarray(res.results[c]["y"]).astype(np.float32)
        b, half = c // 2, c % 2
        if half == 0:
            out[b, :OWN] = yc
        else:
            out[b, OWN:] = yc[::-1]
    return out
```
